# Optimizing a Trainium2 kernel written in Bass

```python
import jax, jax.numpy as jnp
from jax import lax
import numpy as np

D_MODEL = 1024
BATCH = 2
SEQ = 16384
DEPTH = 1
DEC_BATCH = 8
DEC_SEQ = 4096
PAST_LEN = 128

MIX_WIDTH = D_MODEL
MLA_WIDTH = D_MODEL // 2
FNET_WIDTH = MIX_WIDTH - MLA_WIDTH
N_HEADS = 4
V_HEAD_DIM = MLA_WIDTH // N_HEADS
QK_NOPE_DIM = 128
QK_ROPE_DIM = 64
QK_HEAD_DIM = QK_NOPE_DIM + QK_ROPE_DIM
Q_LORA = D_MODEL // 4
KV_LORA = D_MODEL // 8
FNET_GROUPS = 4
FNET_GROUP_DIM = FNET_WIDTH // FNET_GROUPS
D_FF = 2816
IN_COLS = Q_LORA + KV_LORA + QK_ROPE_DIM + FNET_WIDTH
Q_BLOCK = 128
EPS = 1e-6
ROPE_THETA = 10000.0

kernel_name = "hybrid_mla_fnet_macaron_encoder"


def rms_norm(x, g):
    xf = x.astype(jnp.float32)
    y = xf * lax.rsqrt(jnp.mean(xf * xf, axis=-1, keepdims=True) + EPS)
    return (y * g.astype(jnp.float32)).astype(x.dtype)


def swiglu(x, w_gate, w_up, w_down):
    return (jax.nn.silu(x @ w_gate) * (x @ w_up)) @ w_down


def rope_tables(seq):
    inv = 1.0 / (ROPE_THETA ** (jnp.arange(0, QK_ROPE_DIM, 2, dtype=jnp.float32) / QK_ROPE_DIM))
    ang = jnp.arange(seq, dtype=jnp.float32)[:, None] * inv[None, :]
    return jnp.cos(ang), jnp.sin(ang)


def apply_rope(x, cos, sin):
    xf = x.astype(jnp.float32)
    x1, x2 = jnp.split(xf, 2, axis=-1)
    out = jnp.concatenate([x1 * cos - x2 * sin, x2 * cos + x1 * sin], axis=-1)
    return out.astype(x.dtype)


def bidirectional_attention(q, k, v):
    B, S, H, Dq = q.shape
    nb = S // Q_BLOCK
    qb = q.reshape(B, nb, Q_BLOCK, H, Dq).transpose(1, 0, 2, 3, 4)
    scale = Dq ** -0.5

    def one_block(qblk):
        s = jnp.einsum('bqhd,bkhd->bhqk', qblk, k, preferred_element_type=jnp.float32) * scale
        p = jax.nn.softmax(s, axis=-1)
        return jnp.einsum('bhqk,bkhd->bqhd', p.astype(v.dtype), v)

    out = lax.map(one_block, qb)
    return out.transpose(1, 0, 2, 3, 4).reshape(B, S, H, v.shape[-1])


def fourier_mix(u):
    B, S, _ = u.shape
    ug = u.reshape(B, S, FNET_GROUPS, FNET_GROUP_DIM).astype(jnp.float32)
    f = jnp.fft.fft2(ug, axes=(1, 3), norm="ortho")
    return jnp.real(f).reshape(B, S, FNET_WIDTH).astype(u.dtype)


def token_mixing(h, w_in, g_q, w_uq, g_kv, w_ukv, w_o):
    B, S, _ = h.shape
    proj = h @ w_in
    c_q, c_kv, k_r, u = jnp.split(
        proj, [Q_LORA, Q_LORA + KV_LORA, Q_LORA + KV_LORA + QK_ROPE_DIM], axis=-1)
    q = (rms_norm(c_q, g_q) @ w_uq).reshape(B, S, N_HEADS, QK_HEAD_DIM)
    q_nope, q_rope = jnp.split(q, [QK_NOPE_DIM], axis=-1)
    kv = (rms_norm(c_kv, g_kv) @ w_ukv).reshape(B, S, N_HEADS, QK_NOPE_DIM + V_HEAD_DIM)
    k_nope, v = jnp.split(kv, [QK_NOPE_DIM], axis=-1)
    cos, sin = rope_tables(S)
    q_rope = apply_rope(q_rope, cos[:, None, :], sin[:, None, :])
    k_rope = apply_rope(k_r, cos, sin)
    q_full = jnp.concatenate([q_nope, q_rope], axis=-1)
    k_full = jnp.concatenate(
        [k_nope, jnp.broadcast_to(k_rope[:, :, None, :], (B, S, N_HEADS, QK_ROPE_DIM))], axis=-1)
    attn = bidirectional_attention(q_full, k_full, v).reshape(B, S, MLA_WIDTH)
    four = fourier_mix(u)
    return jnp.concatenate([attn, four], axis=-1) @ w_o


def trunk(x, g_ffn1, w1_gate, w1_up, w1_down, g_mix, w_in, g_q, w_uq, g_kv, w_ukv, w_o,
          g_ffn2, w2_gate, w2_up, w2_down, g_final):
    for l in range(DEPTH):
        x = x + 0.5 * swiglu(rms_norm(x, g_ffn1[l]), w1_gate[l], w1_up[l], w1_down[l])
        x = x + token_mixing(rms_norm(x, g_mix[l]), w_in[l], g_q[l], w_uq[l], g_kv[l], w_ukv[l], w_o[l])
        x = x + 0.5 * swiglu(rms_norm(x, g_ffn2[l]), w2_gate[l], w2_up[l], w2_down[l])
    return rms_norm(x, g_final)


def setup_inputs(seed: int = 0) -> dict:
    key = jax.random.key(seed)
    ks = jax.random.split(key, 20)

    def w(k, shape, fan_in):
        return jax.random.normal(k, shape, jnp.float32) * (fan_in ** -0.5)

    def gain(k, shape):
        return 1.0 + 0.01 * jax.random.normal(k, shape, jnp.float32)

    L = DEPTH
    return {
        "x_prompt": jax.random.normal(ks[0], (BATCH, SEQ, D_MODEL), jnp.float32),
        "x_sample": jax.random.normal(ks[1], (DEC_BATCH, DEC_SEQ, D_MODEL), jnp.float32),
        "g_ffn1": gain(ks[2], (L, D_MODEL)),
        "w1_gate": w(ks[3], (L, D_MODEL, D_FF), D_MODEL),
        "w1_up": w(ks[4], (L, D_MODEL, D_FF), D_MODEL),
        "w1_down": w(ks[5], (L, D_FF, D_MODEL), D_FF),
        "g_mix": gain(ks[6], (L, D_MODEL)),
        "w_in": w(ks[7], (L, D_MODEL, IN_COLS), D_MODEL),
        "g_q": gain(ks[8], (L, Q_LORA)),
        "w_uq": w(ks[9], (L, Q_LORA, N_HEADS * QK_HEAD_DIM), Q_LORA),
        "g_kv": gain(ks[10], (L, KV_LORA)),
        "w_ukv": w(ks[11], (L, KV_LORA, N_HEADS * (QK_NOPE_DIM + V_HEAD_DIM)), KV_LORA),
        "w_o": w(ks[12], (L, MIX_WIDTH, D_MODEL), MIX_WIDTH),
        "g_ffn2": gain(ks[13], (L, D_MODEL)),
        "w2_gate": w(ks[14], (L, D_MODEL, D_FF), D_MODEL),
        "w2_up": w(ks[15], (L, D_MODEL, D_FF), D_MODEL),
        "w2_down": w(ks[16], (L, D_FF, D_MODEL), D_FF),
        "g_final": gain(ks[17], (D_MODEL,)),
    }


def reference(x_prompt, x_sample, g_ffn1, w1_gate, w1_up, w1_down, g_mix, w_in, g_q, w_uq,
              g_kv, w_ukv, w_o, g_ffn2, w2_gate, w2_up, w2_down, g_final):
    y_prompt = trunk(x_prompt, g_ffn1, w1_gate, w1_up, w1_down, g_mix, w_in, g_q, w_uq, g_kv,
                     w_ukv, w_o, g_ffn2, w2_gate, w2_up, w2_down, g_final)
    y_sample = trunk(x_sample, g_ffn1, w1_gate, w1_up, w1_down, g_mix, w_in, g_q, w_uq, g_kv,
                     w_ukv, w_o, g_ffn2, w2_gate, w2_up, w2_down, g_final)
    return (y_prompt, y_sample)
```

```python
import numpy as np
from contextlib import ExitStack

import concourse.bass as bass
import concourse.mybir as mybir
from concourse.bass_utils import run_bass_kernel_spmd

F32 = mybir.dt.float32
BF16 = mybir.dt.bfloat16
ALU = mybir.AluOpType
AF = mybir.ActivationFunctionType

D = 1024
DFF = 2816
NFC = 22
NTOK = 8192
NT = 16
USE_CC = False
NLOC = 8192 if USE_CC else 20480
NTL = NLOC // 512
NCORE = 8
EPS = 1e-6
SCALE = 192 ** -0.5


class Op:
    __slots__ = ("eng", "fn", "deps", "sig", "dma", "chan", "n", "has_dep", "cc")


class Prog:
    ENGS = ("pe", "act", "dve", "pool", "sp")

    def __init__(self, nc):
        self.nc = nc
        self.ops = []
        self.lw = {}
        self.rd = {}
        self.pending_dma = []
        self.last_compute = {}
        self.init_hooks = {}

    def add(self, eng, fn, R=(), W=(), dma=False, chan=None, n=1, cc=False):
        op = Op()
        op.eng, op.fn, op.dma, op.chan, op.n = eng, fn, dma, chan, n
        op.cc = cc
        op.deps, op.sig, op.has_dep = set(), None, False
        for r in R:
            w = self.lw.get(r)
            if w is not None:
                self._dep(op, w, True)
        for x in W:
            w = self.lw.get(x)
            if w is not None:
                self._dep(op, w, False)
            for r in self.rd.get(x, {}).values():
                self._dep(op, r, False)
        for r in R:
            k = ("d", len(self.ops)) if dma else eng
            self.rd.setdefault(r, {})[k] = op
        for x in W:
            self.lw[x] = op
            self.rd[x] = {}
        self.ops.append(op)
        if dma:
            self.pending_dma.append(op)
        else:
            self.last_compute[eng] = op
        return op

    def _dep(self, op, d, raw):
        if d is op:
            return
        if (not op.dma) and (not d.dma) and op.eng == d.eng:
            if (not raw) or op.eng == "pe":
                return
        op.deps.add(d)
        d.has_dep = True

    def barrier(self):
        targets = list(self.last_compute.values()) + list(self.pending_dma)
        for e in self.ENGS:
            b = Op()
            b.eng, b.fn, b.dma, b.chan, b.n = e, None, False, None, 1
            b.cc = False
            b.deps, b.sig, b.has_dep = set(), None, False
            for t in targets:
                if (not t.dma) and t.eng == e:
                    continue
                b.deps.add(t)
                t.has_dep = True
            self.ops.append(b)
        self.lw.clear()
        self.rd.clear()
        self.pending_dma = []

    def emit(self):
        nc = self.nc
        cnt = {}
        for o in self.ops:
            if not o.has_dep:
                continue
            key = ("c", o.chan) if o.dma else ("e", o.eng)
            cnt[key] = cnt.get(key, 0) + ((1 if o.cc else 16 * o.n) if o.dma else 1)
            o.sig = (key, cnt[key])
        with ExitStack() as es:
            sems = {}
            for i, key in enumerate(cnt):
                sems[key] = es.enter_context(nc.semaphore("s%d" % i))
            block = es.enter_context(nc.Block())
            decos = {"pe": block.tensor, "act": block.scalar, "dve": block.vector,
                     "pool": block.gpsimd, "sp": block.sync}
            for e in self.ENGS:
                ops_e = [o for o in self.ops if o.eng == e]

                def body(eng, ops_e=ops_e, e=e):
                    waited = {}
                    if e in self.init_hooks:
                        self.init_hooks[e](eng)
                    for o in ops_e:
                        need = {}
                        for d in o.deps:
                            k, v = d.sig
                            if need.get(k, 0) < v:
                                need[k] = v
                        for k, v in need.items():
                            if waited.get(k, 0) < v:
                                eng.wait_ge(sems[k], v)
                                waited[k] = v
                        if o.fn is None:
                            continue
                        r = o.fn(eng)
                        if o.sig is not None:
                            if o.cc:
                                r[0].then_inc(sems[o.sig[0]])
                            elif o.dma:
                                assert len(r) == o.n
                                for ins in r:
                                    ins.then_inc(sems[o.sig[0]], 16)
                            else:
                                r.then_inc(sems[o.sig[0]], 1)

                decos[e](body)


class Arena:
    def __init__(self, ap, nbytes):
        self.ap = ap
        self.cap = nbytes // 2
        self.off = 0

    def reset(self):
        self.off = 0

    def bf(self, n):
        n2 = (n + 31) // 32 * 32
        assert self.off + n2 <= self.cap, ("arena overflow", self.off, n2, self.cap)
        v = self.ap[:, self.off:self.off + n]
        self.off += n2
        return v

    def f32(self, n):
        return self.bf(2 * n).bitcast(F32)


def build_program():
    nc = bass.Bass("TRN2", target_bir_lowering=False)

    def din(name, shape):
        return nc.dram_tensor(name, shape, F32, kind="ExternalInput").ap()

    xin = din("xin", [NLOC, D])
    g_ffn1 = din("g_ffn1", [D]); g_mix = din("g_mix", [D]); g_ffn2 = din("g_ffn2", [D])
    g_final = din("g_final", [D]); g_q = din("g_q", [256]); g_kv = din("g_kv", [128])
    w_uq = din("w_uq", [256, 768]); w_ukv = din("w_ukv", [128, 1024])
    BIGW = {"w1_gate": (D, DFF), "w1_up": (D, DFF), "w1_down": (DFF, D),
            "w2_gate": (D, DFF), "w2_up": (D, DFF), "w2_down": (DFF, D),
            "w_in": (D, 960), "w_o": (D, D), "dft_cos": (2048, 2048), "dft_sin": (2048, 2048)}
    wfull = {k: din(k, [r, c]) for k, (r, c) in BIGW.items()}
    rope_tok = din("rope_tok", [NLOC, 128])
    rope_cos = din("rope_cos", [64, NLOC]); rope_sin = din("rope_sin", [64, NLOC])
    ch_tabs = din("ch_tabs", [3, 128, 128])
    tw = din("tw", [20, 2, 2048])
    twt = din("twt", [4, 2048, 16])
    ident_in = din("ident", [128, 128])
    yout = nc.dram_tensor("yout", [NTOK, D], F32, kind="ExternalOutput").ap()

    def dscr(name, shape, dt):
        return nc.dram_tensor(name, shape, dt).ap()

    x1_d = dscr("x1_d", [NLOC, D], F32)
    x2_d = dscr("x2_d", [NTOK, D], F32)
    ql_d = dscr("ql_d", [4 * 128, NTOK], BF16)
    qr_d = dscr("qr_d", [4 * 64, NTOK], BF16)
    v_p = [dscr("v_p%d" % b, [16384, 128], BF16) for b in range(2)]
    kt_p = [dscr("kt_p%d" % b, [192, 16384], BF16) for b in range(2)]
    u_p = [dscr("u_p%d" % b, [16384, 512], BF16) for b in range(2)]
    SR = 6656
    send = dscr("send", [SR, 512], BF16)
    gath = dscr("gath", [8 * SR, 512], BF16)
    v_samp = dscr("v_samp", [4096, 128], BF16)
    kt_samp = dscr("kt_samp", [192, 4096], BF16)
    u_samp = dscr("u_samp", [4096, 512], BF16)
    ef_g = dscr("ef_g", [8 * 2048, 2048], BF16)
    ef_samp = dscr("ef_samp", [2048, 2048], BF16)
    ef_t = dscr("ef_t", [8 * 2 * 2048, 512], BF16)
    efs_t = dscr("efs_t", [2 * 2 * 2048, 512], BF16)
    four_d = dscr("four_d", [512, NTOK], BF16)

    P = Prog(nc)
    if USE_CC:
        P.init_hooks['pool'] = lambda eng: get_pid()
    es = ExitStack()
    ARENA_BYTES = 206 * 1024
    arena_t = es.enter_context(nc.sbuf_tensor("arena", [128, ARENA_BYTES // 2], BF16))
    AR = Arena(arena_t, ARENA_BYTES)
    ident = es.enter_context(nc.sbuf_tensor("identb", [128, 128], BF16))
    ones = es.enter_context(nc.sbuf_tensor("onesb", [128, 128], BF16))
    dummy = es.enter_context(nc.sbuf_tensor("dummyt", [128, 8], F32))
    epst = es.enter_context(nc.sbuf_tensor("epst", [128, 1], F32))
    PS = [es.enter_context(nc.psum_tensor("ps%d" % i, [128, 512], F32)) for i in range(8)]

    def psb(i):
        return PS[i][:, :].bitcast(BF16).rearrange("p (a b) -> p a b", a=8)

    pidc = {}

    def get_pid():
        if 'v' not in pidc:
            pidc['v'] = nc.partition_id([mybir.EngineType.Pool])
        return pidc['v']

    P.add("pool", lambda e: [e.dma_start(out=ident[:, :], in_=ident_in)], W=["ident"], dma=True, chan="c_ident")
    P.add("dve", lambda e: e.memset(ones[:, :], 1.0), W=["ones"])
    P.add("dve", lambda e: e.memset(epst[:, :], EPS), W=["eps"])
    P.add("dve", lambda e: e.memset(dummy[:, :], 0.0), W=["dummy"])

    w1_gate, w1_up, w1_down = wfull["w1_gate"], wfull["w1_up"], wfull["w1_down"]
    w2_gate, w2_up, w2_down = wfull["w2_gate"], wfull["w2_up"], wfull["w2_down"]
    w_in, w_o, dft_cos, dft_sin = wfull["w_in"], wfull["w_o"], wfull["dft_cos"], wfull["dft_sin"]

    def load_w_kc(dst3, src2, kchunks, chan, key, eng="pool", rkeys=()):
        v = src2.rearrange("(kc p) f -> p kc f", p=128)
        def fn(e):
            return [e.dma_start(out=dst3[:, kc, :], in_=v[:, kc, :]) for kc in range(kchunks)]
        P.add(eng, fn, R=list(rkeys), W=[key], dma=True, chan=chan, n=kchunks)

    def norm_a(tag, X, gain, hn4, stat, src_keys):
        ss, rt, rs = stat
        def f_stats(e):
            for s in range(4):
                e.scalar_tensor_tensor(out=hn4[:, s, :], in0=X[:, s, :], scalar=1.0 / D, in1=X[:, s, :],
                                       op0=ALU.mult, op1=ALU.mult, accum_out=ss[:, s:s + 1])
            return e.tensor_copy(out=dummy[:, 0:1], in_=dummy[:, 1:2])
        P.add("dve", f_stats, R=src_keys + ["dummy"], W=[tag + "ss"] + [tag + "hn%d" % s for s in range(4)])
        P.add("act", lambda e: e.activation(out=rt, in_=ss, func=AF.Sqrt, bias=epst[:, 0:1], scale=1.0),
              R=[tag + "ss", "eps"], W=[tag + "rt"])
        P.add("dve", lambda e: e.reciprocal(out=rs, in_=rt), R=[tag + "rt"], W=[tag + "rs"])
        def one(s):
            P.add("dve", lambda e: e.scalar_tensor_tensor(
                out=hn4[:, s, :], in0=X[:, s, :], scalar=rs[:, s:s + 1], in1=gain,
                op0=ALU.mult, op1=ALU.mult), R=src_keys + [tag + "rs", tag + "gain"], W=[tag + "hn%d" % s])
        for s in range(4):
            one(s)

    def norm_a_split(tag, X, gain, hn4, stat, src_keys, junkb):
        ss, rt, rs = stat
        def stats():
            def f_stats(e):
                for s in range(4):
                    e.activation(out=junkb, in_=X[:, s, :], func=AF.Square, scale=1.0 / 32.0, accum_out=ss[:, s:s + 1])
                return e.activation(out=dummy[:, 2:3], in_=dummy[:, 3:4], func=AF.Copy)
            P.add("act", f_stats, R=src_keys + ["dummy"], W=[tag + "ss", tag + "junkb"])
            P.add("act", lambda e: e.activation(out=rt, in_=ss, func=AF.Sqrt, bias=epst[:, 0:1], scale=1.0),
                  R=[tag + "ss", "eps"], W=[tag + "rt"])
            P.add("dve", lambda e: e.reciprocal(out=rs, in_=rt), R=[tag + "rt"], W=[tag + "rs"])
        def hn(s):
            P.add("dve", lambda e: e.scalar_tensor_tensor(
                out=hn4[:, s, :], in0=X[:, s, :], scalar=rs[:, s:s + 1], in1=gain,
                op0=ALU.mult, op1=ALU.mult), R=src_keys + [tag + "rs", tag + "gain"], W=[tag + "hn%d" % s])
        return stats, hn

    def norm_b(tag, hn4, hT):
        def one(s):
            tb = s % 2
            def f_tr(e):
                r = None
                for kc in range(8):
                    r = e.transpose(out=psb(tb)[:, kc, :], in_=hn4[:, s, kc * 128:(kc + 1) * 128],
                                    identity=ident[:, :])
                return r
            P.add("pe", f_tr, R=[tag + "hn%d" % s, "ident"], W=["psT%d" % tb])
            P.add("act", lambda e: e.copy(out=hT[:, :, s * 128:(s + 1) * 128], in_=psb(tb)),
                  R=["psT%d" % tb], W=[tag + "hT%d" % s])
        for s in range(4):
            one(s)

    def ffn_phase(tag, src, dst, wg, wu, wd, g_in, g_fin, ntiles):
        AR.reset()
        t = tag
        WG = AR.bf(8 * DFF).rearrange("p (a b) -> p a b", a=8)
        WU = AR.bf(8 * DFF).rearrange("p (a b) -> p a b", a=8)
        WD = AR.bf(NFC * D).rearrange("p (a b) -> p a b", a=NFC)
        XP = [AR.f32(D) for _ in range(2)]
        XR = [AR.f32(D) for _ in range(2)]
        hn4 = AR.bf(4 * D).rearrange("p (a b) -> p a b", a=4)
        hT = AR.bf(8 * 512).rearrange("p (a b) -> p a b", a=8)
        actT = AR.bf(NFC * 512).rearrange("p (a b) -> p a b", a=NFC)
        sg = [AR.f32(512) for _ in range(2)]
        junk = AR.bf(D)
        gain = AR.f32(D)
        gfin = AR.f32(D) if g_fin is not None else None
        stp = [(AR.f32(1), AR.f32(1), AR.f32(1)) for _ in range(2)]
        stf = [(AR.f32(1), AR.f32(1), AR.f32(1)) for _ in range(2)]
        P.add("sp", lambda e: [e.dma_start(out=gain, in_=g_in.partition_broadcast(128))],
              W=[t + "gain"], dma=True, chan=t + "gain")
        if g_fin is not None:
            P.add("sp", lambda e: [e.dma_start(out=gfin, in_=g_fin.partition_broadcast(128))],
                  W=[t + "gfin"], dma=True, chan=t + "gfin")
        cnt = {"p": 0, "r": 0}

        def prep_sub(i, s):
            sl = cnt["p"] % 2
            cnt["p"] += 1
            xp = XP[sl]
            ss, rt, rs = stp[sl]
            r0 = i * 512 + s * 128
            P.add("sp", lambda e: [e.dma_start(out=xp, in_=src[r0:r0 + 128, :])], W=[t + "XP%d" % sl], dma=True,
                  chan=t + "xp%d" % sl)
            def f_stats(e):
                e.scalar_tensor_tensor(out=hn4[:, s, :], in0=xp, scalar=1.0 / D, in1=xp,
                                       op0=ALU.mult, op1=ALU.mult, accum_out=ss)
                return e.tensor_copy(out=dummy[:, 0:1], in_=dummy[:, 1:2])
            P.add("dve", f_stats, R=[t + "XP%d" % sl, "dummy"], W=[t + "ss%d" % sl, t + "hn%d" % s])
            P.add("act", lambda e: e.activation(out=rt, in_=ss, func=AF.Sqrt, bias=epst[:, 0:1], scale=1.0),
                  R=[t + "ss%d" % sl, "eps"], W=[t + "rt%d" % sl])
            P.add("dve", lambda e: e.reciprocal(out=rs, in_=rt), R=[t + "rt%d" % sl], W=[t + "rs%d" % sl])
            P.add("dve", lambda e: e.scalar_tensor_tensor(
                out=hn4[:, s, :], in0=xp, scalar=rs[:, 0:1], in1=gain, op0=ALU.mult, op1=ALU.mult),
                R=[t + "XP%d" % sl, t + "rs%d" % sl, t + "gain"], W=[t + "hn%d" % s])

        def prep_a(i):
            for s in range(4):
                prep_sub(i, s)

        load_w_kc(WG, wg, 8, t + "wg", t + "WG")
        load_w_kc(WU, wu, 8, t + "wu", t + "WU")
        load_w_kc(WD, wd, NFC, t + "wd", t + "WD")
        hTkeys = [t + "hT%d" % s for s in range(4)]

        def gu(fc):
            gb, ub = 2 + (fc % 2), 4 + (fc % 2)
            def mm(Wm, bank, wkey):
                def f_mm(e):
                    r = None
                    for kc in range(8):
                        r = e.matmul(out=PS[bank][:, :], lhsT=Wm[:, kc, fc * 128:(fc + 1) * 128],
                                     rhs=hT[:, kc, :], start=(kc == 0), stop=(kc == 7))
                    return r
                P.add("pe", f_mm, R=hTkeys + [wkey], W=["ps%d" % bank])
            mm(WG, gb, t + "WG")
            mm(WU, ub, t + "WU")
            P.add("act", lambda e: e.activation(out=sg[fc % 2], in_=PS[gb][:, :], func=AF.Silu),
                  R=["ps%d" % gb], W=[t + "sg%d" % (fc % 2)])
            P.add("dve", lambda e: e.tensor_tensor(out=actT[:, fc, :], in0=sg[fc % 2], in1=PS[ub][:, :], op=ALU.mult),
                  R=[t + "sg%d" % (fc % 2), "ps%d" % ub], W=[t + "actT%d" % fc])

        def down_sub(i, s):
            sl = cnt["r"] % 2
            cnt["r"] += 1
            xr = XR[sl]
            xkey = t + "XR%d" % sl
            r0 = i * 512 + s * 128
            P.add("sp", lambda e: [e.dma_start(out=xr, in_=src[r0:r0 + 128, :])], W=[xkey], dma=True, chan=t + "xr%d" % sl)
            def half_(half):
                bank = 6 + half
                def f_dn(e):
                    r = None
                    for fc in range(NFC):
                        r = e.matmul(out=PS[bank][:, :], lhsT=actT[:, fc, s * 128:(s + 1) * 128],
                                     rhs=WD[:, fc, half * 512:(half + 1) * 512],
                                     start=(fc == 0), stop=(fc == NFC - 1))
                    return r
                P.add("pe", f_dn, R=[t + "actT%d" % fc for fc in range(NFC)] + [t + "WD"], W=["ps%d" % bank])
                P.add("dve", lambda e: e.scalar_tensor_tensor(
                    out=xr[:, half * 512:(half + 1) * 512], in0=PS[bank][:, :], scalar=0.5,
                    in1=xr[:, half * 512:(half + 1) * 512], op0=ALU.mult, op1=ALU.add),
                    R=["ps%d" % bank, xkey], W=[xkey])
            half_(0)
            half_(1)
            if g_fin is not None:
                ss, rt, rs = stf[sl]
                def f_st(e):
                    e.scalar_tensor_tensor(out=junk, in0=xr, scalar=1.0 / D, in1=xr,
                                           op0=ALU.mult, op1=ALU.mult, accum_out=ss)
                    return e.tensor_copy(out=dummy[:, 0:1], in_=dummy[:, 1:2])
                P.add("dve", f_st, R=[xkey, "dummy"], W=[t + "fss%d" % sl, t + "junk"])
                P.add("act", lambda e: e.activation(out=rt, in_=ss, func=AF.Sqrt, bias=epst[:, 0:1], scale=1.0),
                      R=[t + "fss%d" % sl, "eps"], W=[t + "frt%d" % sl])
                P.add("dve", lambda e: e.reciprocal(out=rs, in_=rt), R=[t + "frt%d" % sl], W=[t + "frs%d" % sl])
                P.add("dve", lambda e: e.scalar_tensor_tensor(
                    out=xr, in0=xr, scalar=rs[:, 0:1], in1=gfin, op0=ALU.mult, op1=ALU.mult),
                    R=[xkey, t + "frs%d" % sl, t + "gfin"], W=[xkey])
            P.add("sp", lambda e: [e.dma_start(out=dst[r0:r0 + 128, :], in_=xr)], R=[xkey], dma=True, chan=t + "xs%d" % sl)

        def tile(i):
            for fc in range(NFC):
                gu(fc)
                if fc == 9 and i + 1 < ntiles:
                    prep_a(i + 1)
            if i + 1 < ntiles:
                norm_b(t, hn4, hT)
            for s in range(4):
                down_sub(i, s)

        prep_a(0)
        norm_b(t, hn4, hT)
        for i in range(ntiles):
            tile(i)
        P.barrier()

    def proj_phase():
        AR.reset()
        t = "A"
        WIN = AR.bf(8 * 960).rearrange("p (a b) -> p a b", a=8)
        WUQ = AR.bf(2 * 768).rearrange("p (a b) -> p a b", a=2)
        WUQS = AR.bf(2 * 256).rearrange("p (a b) -> p a b", a=2)
        WUKn = AR.bf(4 * 128).rearrange("p (a b) -> p a b", a=4)
        WUKT = AR.bf(4 * 128).rearrange("p (a b) -> p a b", a=4)
        Xb = [AR.f32(4 * D).rearrange("p (a b) -> p a b", a=4) for _ in range(2)]
        hn4 = AR.bf(4 * D).rearrange("p (a b) -> p a b", a=4)
        hT = AR.bf(8 * 512).rearrange("p (a b) -> p a b", a=8)
        gain = AR.f32(D)
        gq = AR.f32(256)
        gkv = AR.f32(128)
        stat = [(AR.f32(4), AR.f32(4), AR.f32(4)) for _ in range(2)]
        RT = [AR.f32(4 * 128).rearrange("p (a b) -> p a b", a=4) for _ in range(2)]
        RC = [AR.f32(512) for _ in range(2)]
        RS = [AR.f32(512) for _ in range(2)]
        st2 = [(AR.f32(2), AR.f32(2), AR.f32(2)) for _ in range(2)]
        junk = [AR.bf(384) for _ in range(2)]
        PAS = [AR.f32(448) for _ in range(4)]
        junkb = AR.bf(D)
        cqn = [AR.bf(256) for _ in range(2)]
        krt = [AR.f32(64) for _ in range(2)]
        kra = [AR.f32(64) for _ in range(2)]
        krr = [AR.bf(64) for _ in range(2)]
        CQT = AR.bf(2 * 512).rearrange("p (a b) -> p a b", a=2)
        KTLs = [AR.bf(512) for _ in range(2)]
        KTRs = [AR.bf(512) for _ in range(2)]
        UBs = [AR.bf(4 * 512).rearrange("p (a b) -> p a b", a=4) for _ in range(2)]
        VBs = [AR.bf(4 * 128).rearrange("p (a b) -> p a b", a=4) for _ in range(2)]
        qn_sb = [AR.bf(512) for _ in range(2)]
        QLs = [AR.bf(4 * 512).rearrange("p (a b) -> p a b", a=4) for _ in range(2)]
        QRs = [AR.bf(4 * 512).rearrange("p (a b) -> p a b", a=4) for _ in range(2)]
        qt1 = [AR.f32(512) for _ in range(2)]
        qt2 = [AR.f32(512) for _ in range(2)]

        P.add("sp", lambda e: [e.dma_start(out=gain, in_=g_mix.partition_broadcast(128))], W=[t + "gain"], dma=True, chan="Again")
        P.add("sp", lambda e: [e.dma_start(out=gq, in_=g_q.partition_broadcast(128))], W=["gq"], dma=True, chan="Agq")
        P.add("sp", lambda e: [e.dma_start(out=gkv, in_=g_kv.partition_broadcast(128))], W=["gkv"], dma=True, chan="Agkv")

        def load_x(i):
            sl = i % 2
            v = x1_d[i * 512:(i + 1) * 512, :].rearrange("(s p) d -> p s d", p=128)
            P.add("sp", lambda e: [e.dma_start(out=Xb[sl], in_=v)], W=[t + "X%d" % sl], dma=True, chan="Axl%d" % sl)
            vr = rope_tok[i * 512:(i + 1) * 512, :].rearrange("(s p) d -> p s d", p=128)
            P.add("sp", lambda e: [e.dma_start(out=RT[sl], in_=vr)], W=["RT%d" % sl], dma=True, chan="Art%d" % sl)
            P.add("sp", lambda e: [e.dma_start(out=RC[sl][0:64, :], in_=rope_cos[:, i * 512:(i + 1) * 512]),
                                   e.dma_start(out=RS[sl][0:64, :], in_=rope_sin[:, i * 512:(i + 1) * 512])],
                  W=["RCS%d" % sl], dma=True, chan="Arcs%d" % sl, n=2)

        load_x(0)
        load_x(1)
        load_w_kc(WIN, w_in, 8, "Awin", "WIN")
        load_w_kc(WUQ, w_uq, 2, "Awuq", "WUQ", eng="pool")
        uqv = w_uq.rearrange("(kc p) (h f) -> p kc h f", p=128, h=4)
        def f_wuqs(e):
            r = []
            for h in range(4):
                for kc in range(2):
                    r.append(e.dma_start(out=WUQS[:, kc, h * 64:h * 64 + 32], in_=uqv[:, kc, h, 160:192]))
                    r.append(e.dma_start(out=WUQS[:, kc, h * 64 + 32:h * 64 + 64], in_=uqv[:, kc, h, 128:160]))
            return r
        P.add("pool", f_wuqs, W=["WUQS"], dma=True, chan="Awuqs", n=16)
        ukv = w_ukv.rearrange("l (h two n) -> l h two n", h=4, two=2)
        P.add("pool", lambda e: [e.dma_start(out=WUKn, in_=ukv[:, :, 0, :])], W=["WUKn"], dma=True, chan="Awukn")
        def f_trw(e):
            r = None
            for h in range(4):
                r = e.transpose(out=psb(0)[:, h, :], in_=WUKn[:, h, :], identity=ident[:, :])
            return r
        P.add("pe", f_trw, R=["WUKn", "ident"], W=["psT0"])
        P.add("act", lambda e: e.copy(out=WUKT, in_=psb(0)[:, 0:4, :]), R=["psT0"], W=["WUKT"])

        def subtile_parts(i, s):
            sl = i % 2
            b2 = s % 2
            pa, pb = (2, 3) if s % 2 == 0 else (4, 5)
            ka, kb_ = "ps%d" % pa, "ps%d" % pb
            UB, VB, KTL, KTR, RTt = UBs[sl], VBs[sl], KTLs[sl], KTRs[sl], RT[sl]
            ss, rt, rs = st2[b2]
            def f_pp(e):
                r = None
                for kc in range(8):
                    e.matmul(out=PS[pa][:, 0:448], lhsT=hT[:, kc, s * 128:(s + 1) * 128], rhs=WIN[:, kc, 0:448],
                             start=(kc == 0), stop=(kc == 7))
                for kc in range(8):
                    r = e.matmul(out=PS[pb][:, :], lhsT=hT[:, kc, s * 128:(s + 1) * 128], rhs=WIN[:, kc, 448:960],
                                 start=(kc == 0), stop=(kc == 7))
                return r
            pas = PAS[s]
            pk = "PAS%d" % s
            def pp():
                P.add("pe", f_pp, R=[t + "hT%d" % s, "WIN"], W=[ka, kb_])
                P.add("act", lambda e: e.copy(out=pas, in_=PS[pa][:, 0:448]), R=[ka], W=[pk])
                P.add("act", lambda e: e.copy(out=UB[:, s, :], in_=PS[pb][:, :]), R=[kb_], W=["UB%d_%d" % (sl, s)])
            def post():
                def f_sq(e):
                    e.activation(out=junk[b2][:, 0:256], in_=pas[:, 0:256], func=AF.Square, scale=1.0 / 16.0,
                                 accum_out=ss[:, 0:1])
                    e.activation(out=junk[b2][:, 256:384], in_=pas[:, 256:384], func=AF.Square, scale=128 ** -0.5,
                                 accum_out=ss[:, 1:2])
                    return e.activation(out=dummy[:, 2:3], in_=dummy[:, 3:4], func=AF.Copy)
                P.add("act", f_sq, R=[pk, "dummy"], W=["ss2_%d" % b2, "junk%d" % b2])
                P.add("act", lambda e: e.activation(out=rt, in_=ss, func=AF.Sqrt, bias=epst[:, 0:1], scale=1.0),
                      R=["ss2_%d" % b2, "eps"], W=["rt2_%d" % b2])
                P.add("dve", lambda e: e.reciprocal(out=rs, in_=rt), R=["rt2_%d" % b2], W=["rs2_%d" % b2])
                P.add("dve", lambda e: e.scalar_tensor_tensor(
                    out=cqn[b2], in0=pas[:, 0:256], scalar=rs[:, 0:1], in1=gq, op0=ALU.mult, op1=ALU.mult),
                    R=[pk, "rs2_%d" % b2, "gq"], W=["cqn%d" % b2])
                P.add("dve", lambda e: e.scalar_tensor_tensor(
                    out=VB[:, s, :], in0=pas[:, 256:384], scalar=rs[:, 1:2], in1=gkv, op0=ALU.mult, op1=ALU.mult),
                    R=[pk, "rs2_%d" % b2, "gkv"], W=["VB%d_%d" % (sl, s)])
                def f_kr(e):
                    e.tensor_tensor(out=kra[b2], in0=pas[:, 384:448], in1=RTt[:, s, 0:64], op=ALU.mult)
                    e.tensor_tensor(out=krt[b2][:, 0:32], in0=pas[:, 416:448], in1=RTt[:, s, 64:96], op=ALU.mult)
                    e.tensor_tensor(out=krt[b2][:, 32:64], in0=pas[:, 384:416], in1=RTt[:, s, 96:128], op=ALU.mult)
                    return e.tensor_copy(out=dummy[:, 0:1], in_=dummy[:, 1:2])
                P.add("dve", f_kr, R=[pk, "RT%d" % sl, "dummy"], W=["kra%d" % b2])
                P.add("dve", lambda e: e.tensor_tensor(out=krr[b2], in0=kra[b2], in1=krt[b2], op=ALU.add),
                      R=["kra%d" % b2], W=["krr%d" % b2])
            def tr():
                def f_tr(e):
                    e.transpose(out=psb(b2)[:, 0, :], in_=cqn[b2][:, 0:128], identity=ident[:, :])
                    e.transpose(out=psb(b2)[:, 1, :], in_=cqn[b2][:, 128:256], identity=ident[:, :])
                    e.transpose(out=psb(b2)[:, 2, :], in_=VB[:, s, :], identity=ident[:, :])
                    return e.transpose(out=psb(b2)[0:64, 3, :], in_=krr[b2], identity=ident[:, :])
                P.add("pe", f_tr, R=["cqn%d" % b2, "VB%d_%d" % (sl, s), "krr%d" % b2, "ident"], W=["psT%d" % b2])
                P.add("act", lambda e: e.copy(out=CQT[:, :, s * 128:(s + 1) * 128], in_=psb(b2)[:, 0:2, :]),
                      R=["psT%d" % b2], W=["CQT%d" % s])
                P.add("act", lambda e: e.copy(out=KTL[:, s * 128:(s + 1) * 128], in_=psb(b2)[:, 2, :]),
                      R=["psT%d" % b2], W=["KTL%d_%d" % (sl, s)])
                P.add("act", lambda e: e.copy(out=KTR[0:64, s * 128:(s + 1) * 128], in_=psb(b2)[0:64, 3, :]),
                      R=["psT%d" % b2], W=["KTR%d_%d" % (sl, s)])
            return pp, post, tr

        cqt_keys = ["CQT%d" % s for s in range(4)]

        def qhead(i, h):
            sl = i % 2
            hb = h % 2
            QL, QR, RCt, RSt = QLs[sl], QRs[sl], RC[sl], RS[sl]
            def f_qn(e):
                e.matmul(out=PS[4][:, :], lhsT=WUQ[:, 0, h * 192:h * 192 + 128], rhs=CQT[:, 0, :], start=True, stop=False)
                return e.matmul(out=PS[4][:, :], lhsT=WUQ[:, 1, h * 192:h * 192 + 128], rhs=CQT[:, 1, :], start=False, stop=True)
            P.add("pe", f_qn, R=cqt_keys + ["WUQ"], W=["ps4"])
            P.add("act", lambda e: e.copy(out=qn_sb[hb], in_=PS[4][:, :]), R=["ps4"], W=["qn%d" % hb])
            P.add("pe", lambda e: e.matmul(out=PS[5][:, :], lhsT=WUKT[:, h, :], rhs=qn_sb[hb], start=True, stop=True),
                  R=["qn%d" % hb, "WUKT"], W=["ps5"])
            P.add("act", lambda e: e.activation(out=QL[:, h, :], in_=PS[5][:, :], func=AF.Copy, scale=SCALE),
                  R=["ps5"], W=["QL%d_%d" % (sl, h)])
            def f_qr(e):
                e.matmul(out=PS[6][0:64, :], lhsT=WUQ[:, 0, h * 192 + 128:h * 192 + 192], rhs=CQT[:, 0, :], start=True, stop=False)
                e.matmul(out=PS[6][0:64, :], lhsT=WUQ[:, 1, h * 192 + 128:h * 192 + 192], rhs=CQT[:, 1, :], start=False, stop=True)
                e.matmul(out=PS[7][0:64, :], lhsT=WUQS[:, 0, h * 64:h * 64 + 64], rhs=CQT[:, 0, :], start=True, stop=False)
                return e.matmul(out=PS[7][0:64, :], lhsT=WUQS[:, 1, h * 64:h * 64 + 64], rhs=CQT[:, 1, :], start=False, stop=True)
            P.add("pe", f_qr, R=cqt_keys + ["WUQ", "WUQS"], W=["ps6", "ps7"])
            def f_rq(e):
                e.tensor_tensor(out=qt1[hb][0:64, :], in0=PS[6][0:64, :], in1=RCt[0:64, :], op=ALU.mult)
                e.tensor_tensor(out=qt2[hb][0:64, :], in0=PS[7][0:64, :], in1=RSt[0:64, :], op=ALU.mult)
                return e.tensor_copy(out=dummy[:, 0:1], in_=dummy[:, 1:2])
            P.add("dve", f_rq, R=["ps6", "ps7", "RCS%d" % sl, "dummy"], W=["qt%d" % hb])
            P.add("dve", lambda e: e.tensor_tensor(out=QR[0:64, h, :], in0=qt1[hb][0:64, :], in1=qt2[hb][0:64, :], op=ALU.add),
                  R=["qt%d" % hb], W=["QR%d_%d" % (sl, h)])

        def own_tok0(i):
            if USE_CC:
                return i * 512
            if i < 8:
                return i * 512
            if i >= 32:
                return 4096 + (i - 32) * 512
            return None

        def tile(i):
            sl = i % 2
            UB, VB, KTL, KTR, QL, QR = UBs[sl], VBs[sl], KTLs[sl], KTRs[sl], QLs[sl], QRs[sl]
            parts = [subtile_parts(i, s) for s in range(4)]
            if i + 1 < NTL:
                nsl = (i + 1) % 2
                nstats, nhn = norm_a_split(t, Xb[nsl], gain, hn4, stat[nsl], [t + "X%d" % nsl], junkb)
                nstats()
            else:
                nhn = lambda s: None
            parts[0][0]()
            parts[1][0]()
            parts[0][1]()
            nhn(0)
            parts[2][0]()
            parts[1][1]()
            nhn(1)
            parts[3][0]()
            parts[0][2]()
            parts[2][1]()
            nhn(2)
            parts[1][2]()
            parts[3][1]()
            nhn(3)
            parts[2][2]()
            parts[3][2]()
            tok0 = own_tok0(i)
            if tok0 is not None:
                for h in range(4):
                    qhead(i, h)
                qlv = ql_d.rearrange("(h p) n -> p h n", p=128)[:, :, tok0:tok0 + 512]
                qrv = qr_d.rearrange("(h p) n -> p h n", p=64)[:, :, tok0:tok0 + 512]
                P.add("sp", lambda e: [e.dma_start(out=qlv, in_=QL), e.dma_start(out=qrv, in_=QR[0:64, :, :])],
                      R=["QL%d_%d" % (sl, h) for h in range(4)] + ["QR%d_%d" % (sl, h) for h in range(4)],
                      dma=True, chan="Aq%d" % sl, n=2)
            if USE_CC and i < 8:
                r0 = i * 512
                vv = send[0:1024, :].rearrange("r (q l) -> (r q) l", q=4)
                ktv = send[1024:2560, :].rearrange("(f a) c -> f (a c)", a=8)
                uv = send[2560:6656, :]
            elif USE_CC:
                r0 = (i - 8) * 512
                uv, vv, ktv = u_samp, v_samp, kt_samp
            elif i < 32:
                r0 = i * 512
                uv, vv, ktv = u_p[0], v_p[0], kt_p[0]
            else:
                r0 = (i - 32) * 512
                uv, vv, ktv = u_samp, v_samp, kt_samp
            uo = uv[r0:r0 + 512, :].rearrange("(s p) c -> p s c", p=128)
            vo = vv[r0:r0 + 512, :].rearrange("(s p) c -> p s c", p=128)
            P.add("sp", lambda e: [
                e.dma_start(out=uo, in_=UB), e.dma_start(out=vo, in_=VB),
                e.dma_start(out=ktv[0:128, r0:r0 + 512], in_=KTL),
                e.dma_start(out=ktv[128:192, r0:r0 + 512], in_=KTR[0:64, :])],
                R=["UB%d_%d" % (sl, s) for s in range(4)] + ["VB%d_%d" % (sl, s) for s in range(4)]
                  + ["KTL%d_%d" % (sl, s) for s in range(4)] + ["KTR%d_%d" % (sl, s) for s in range(4)],
                W=["send_p"] if (USE_CC and i < 8) else [], dma=True, chan="Ast%d" % sl, n=4)
            if USE_CC and i == 7:
                P.add("pool", lambda e: [e.collective_compute(
                    "AllGather", ALU.bypass, replica_groups=[list(range(NCORE))],
                    ins=[send.opt()], outs=[gath.opt()])], R=["send_p"], W=["gath"], dma=True, chan="cc_all", cc=True)
            if i + 1 < NTL:
                norm_b(t, hn4, hT)
            if i + 2 < NTL:
                load_x(i + 2)

        norm_a(t, Xb[0], gain, hn4, stat[0], [t + "X0"])
        norm_b(t, hn4, hT)
        for i in range(NTL):
            tile(i)
        P.barrier()

    def fourier1():
        assert not USE_CC
        AR.reset()
        COS = AR.bf(16 * 2048).rearrange("p (a b) -> p a b", a=16)
        SIN = AR.bf(16 * 2048).rearrange("p (a b) -> p a b", a=16)
        U = [AR.bf(16 * 512).rearrange("p (a b) -> p a b", a=16) for _ in range(2)]
        CH = AR.bf(3 * 128).rearrange("p (a b) -> p a b", a=3)
        ATs = [AR.bf(512) for _ in range(2)]
        BTs = [AR.bf(512) for _ in range(2)]
        STG = [[AR.bf(4 * 512).rearrange("p (a b) -> p a b", a=4) for _ in range(2)] for _ in range(2)]
        load_w_kc(COS, dft_cos, 16, "Fcos", "COS")
        load_w_kc(SIN, dft_sin, 16, "Fsin", "SIN")
        P.add("pool", lambda e: [e.dma_start(out=CH, in_=ch_tabs.rearrange("a p c -> p a c"))], W=["CH"], dma=True, chan="Fch")
        NPJ = 8
        NJ = NPJ + 2
        cnt = {"q": 0, "s": 0}

        def load_u(job):
            ub = U[job % 2]
            if job < NPJ:
                v = u_p[0].rearrange("(n2 e) c -> e n2 c", e=8)[job].rearrange("(j p) c -> p j c", p=128)
            else:
                v = u_samp.rearrange("(j p two) c -> two p j c", p=128, two=2)[job - NPJ]
            P.add("sp", lambda e: [e.dma_start(out=ub, in_=v)], W=["U%d" % (job % 2)], dma=True, chan="Fu%d" % (job % 2))

        def blk(job, kb, g, stg, sk):
            ub = U[job % 2]
            q = cnt["q"] % 2
            cnt["q"] += 1
            pe_b, pf_b = 2 + 2 * q, 3 + 2 * q
            def f_ab(e):
                for j in range(16):
                    e.matmul(out=PS[0][:, :], lhsT=ub[:, j, g * 128:(g + 1) * 128],
                             rhs=COS[:, j, kb * 512:(kb + 1) * 512], start=(j == 0), stop=(j == 15))
                r = None
                for j in range(16):
                    r = e.matmul(out=PS[1][:, :], lhsT=ub[:, j, g * 128:(g + 1) * 128],
                                 rhs=SIN[:, j, kb * 512:(kb + 1) * 512], start=(j == 0), stop=(j == 15))
                return r
            P.add("pe", f_ab, R=["U%d" % (job % 2), "COS", "SIN"], W=["ps0", "ps1"])
            P.add("act", lambda e: e.copy(out=ATs[q], in_=PS[0][:, :]), R=["ps0"], W=["AT%d" % q])
            P.add("dve", lambda e: e.tensor_copy(out=BTs[q], in_=PS[1][:, :]), R=["ps1"], W=["BT%d" % q])
            def f_ef(e):
                r = None
                for sub in range(4):
                    cs = slice(sub * 128, (sub + 1) * 128)
                    e.matmul(out=PS[pe_b][:, cs], lhsT=ATs[q][:, cs], rhs=CH[:, 0, :], start=True, stop=False)
                    e.matmul(out=PS[pe_b][:, cs], lhsT=BTs[q][:, cs], rhs=CH[:, 2, :], start=False, stop=True)
                    e.matmul(out=PS[pf_b][:, cs], lhsT=BTs[q][:, cs], rhs=CH[:, 0, :], start=True, stop=False)
                    r = e.matmul(out=PS[pf_b][:, cs], lhsT=ATs[q][:, cs], rhs=CH[:, 1, :], start=False, stop=True)
                return r
            P.add("pe", f_ef, R=["AT%d" % q, "BT%d" % q, "CH"], W=["ps%d" % pe_b, "ps%d" % pf_b])
            P.add("act", lambda e: e.copy(out=stg[0][:, :, g * 128:(g + 1) * 128],
                                          in_=PS[pe_b][:, :].rearrange("p (s c) -> p s c", s=4)),
                  R=["ps%d" % pe_b], W=[sk + "e%d" % g])
            P.add("dve", lambda e: e.tensor_copy(out=stg[1][:, :, g * 128:(g + 1) * 128],
                                                 in_=PS[pf_b][:, :].rearrange("p (s c) -> p s c", s=4)),
                  R=["ps%d" % pf_b], W=[sk + "f%d" % g])

        def kblock(job, kb):
            sl = cnt["s"] % 2
            cnt["s"] += 1
            stg = STG[sl]
            sk = "STG%d" % sl
            for g in range(4):
                blk(job, kb, g, stg, sk)
            dst = ef_t if job < NPJ else efs_t
            n1 = job if job < NPJ else job - NPJ
            r0 = n1 * 4096 + kb * 512
            de = dst[r0:r0 + 512, :].rearrange("(s p) c -> p s c", p=128)
            df = dst[r0 + 2048:r0 + 2048 + 512, :].rearrange("(s p) c -> p s c", p=128)
            P.add("sp", lambda e: [e.dma_start(out=de, in_=stg[0]), e.dma_start(out=df, in_=stg[1])],
                  R=[sk + "e%d" % g for g in range(4)] + [sk + "f%d" % g for g in range(4)],
                  dma=True, chan="Fst%d" % sl, n=2)

        load_u(0)
        load_u(1)
        for job in range(NJ):
            for kb in range(4):
                kblock(job, kb)
            if job + 2 < NJ:
                load_u(job + 2)
        P.barrier()

    def fourier2():
        AR.reset()
        ET = [[AR.bf(4 * 512).rearrange("p (a b) -> p a b", a=4) for _ in range(16)] for _ in range(2)]
        TWs = [AR.f32(4 * 16).rearrange("p (a b) -> p a b", a=4) for _ in range(2)]
        DG = [AR.bf(4 * 16 * 128).rearrange("p (a b c) -> p a b c", a=4, b=16) for _ in range(2)]
        ob = [AR.bf(512) for _ in range(2)]
        cnt = {"d": 0, "o": 0, "l": 0}
        engs = ("act", "dve", "pool")

        def seg_cg(seg, cg, nj, src_t, esl):
            tok0 = seg * 2048 + cg * 512
            dsl = cnt["d"] % 2
            cnt["d"] += 1
            tws, dg = TWs[dsl], DG[dsl]
            tv = twt[seg, cg * 512:(cg + 1) * 512, :].rearrange("(ch p) j -> p ch j", p=128)
            P.add("sp", lambda e: [e.dma_start(out=tws, in_=tv)], W=["TW%d" % dsl], dma=True, chan="Gt%d" % dsl)
            def mk_diag(ch):
                idb = ident[:, :].unsqueeze(1).to_broadcast([128, nj, 128])
                twb = tws[:, ch, 0:nj].unsqueeze(2).to_broadcast([128, nj, 128])
                P.add("dve", lambda e: e.tensor_tensor(out=dg[:, ch, 0:nj, :], in0=idb, in1=twb, op=ALU.mult),
                      R=["TW%d" % dsl, "ident"], W=["DG%d_0" % dsl])
            for ch in range(4):
                mk_diag(ch)
            dkeys = ["DG%d_0" % dsl]
            def one_g(g):
                osl = cnt["o"] % 2
                cnt["o"] += 1
                bank = 2 + osl
                def f_mm(e):
                    r = None
                    for ch in range(4):
                        for j in range(nj):
                            r = e.matmul(out=PS[bank][:, ch * 128:(ch + 1) * 128],
                                         lhsT=ET[esl][j][:, ch, g * 128:(g + 1) * 128], rhs=dg[:, ch, j, :],
                                         start=(j == 0), stop=(j == nj - 1))
                    return r
                P.add("pe", f_mm, R=["ET%d" % esl] + dkeys, W=["ps%d" % bank])
                o = ob[osl]
                P.add("act", lambda e: e.copy(out=o, in_=PS[bank][:, :]), R=["ps%d" % bank], W=["ob%d" % osl])
                P.add("sp", lambda e: [e.dma_start(out=four_d[g * 128:(g + 1) * 128, tok0:tok0 + 512], in_=o)],
                      R=["ob%d" % osl], dma=True, chan="Gs%d" % osl)
            for g in range(4):
                one_g(g)

        def load_et(cg, nj, src_t):
            esl = cnt["l"] % 2
            cnt["l"] += 1
            def f(e):
                r = []
                for j in range(nj):
                    n1, ef = j // 2, j % 2
                    r0 = n1 * 4096 + ef * 2048 + cg * 512
                    r.append(e.dma_start(out=ET[esl][j], in_=src_t[r0:r0 + 512, :].rearrange("(ch p) c -> p ch c", p=128)))
                return r
            P.add("sp", f, W=["ET%d" % esl], dma=True, chan="Gl%d" % esl, n=nj)
            return esl

        work = [(cg, 16, ef_t, (0, 1)) for cg in range(4)] + [(cg, 4, efs_t, (2, 3)) for cg in range(4)]
        esl_next = load_et(*work[0][:3])
        for wi, (cg, nj, src_t, segs) in enumerate(work):
            esl = esl_next
            if wi + 1 < len(work):
                esl_next = load_et(*work[wi + 1][:3])
            for seg in segs:
                seg_cg(seg, cg, nj, src_t, esl)
        P.barrier()

    def attn_phase():
        AR.reset()
        KTL = AR.bf(16384)
        KTR = AR.bf(16384)
        V = AR.bf(128 * 128).rearrange("p (a b) -> p a b", a=128)
        QLs = [AR.bf(4 * 512).rearrange("p (a b) -> p a b", a=4) for _ in range(2)]
        QRs = [AR.bf(4 * 512).rearrange("p (a b) -> p a b", a=4) for _ in range(2)]
        pTp = [AR.bf(2 * 512).rearrange("p (a b) -> p a b", a=2) for _ in range(3)]
        pT = [pTp[i // 2][:, i % 2, :] for i in range(6)]
        sacc2w = [AR.f32(2 * 512).rearrange("p (a b) -> p a b", a=2) for _ in range(2)]
        sacc = [AR.f32(512) for _ in range(2)]
        saccb = [AR.bf(512) for _ in range(2)]
        slo = [AR.bf(512) for _ in range(2)]
        rinv = [AR.f32(512) for _ in range(2)]
        olat = [AR.bf(512) for _ in range(2)]
        mixTs = [AR.bf(8 * 512).rearrange("p (a b) -> p a b", a=8) for _ in range(2)]
        WO = AR.bf(8 * D).rearrange("p (a b) -> p a b", a=8)
        WUV = AR.bf(4 * 128).rearrange("p (a b) -> p a b", a=4)
        X1s = [AR.f32(4 * D).rearrange("p (a b) -> p a b", a=4) for _ in range(2)]
        load_w_kc(WO, w_o, 8, "Bwo", "WO")
        ukv = w_ukv.rearrange("l (h two n) -> l h two n", h=4, two=2)
        P.add("pool", lambda e: [e.dma_start(out=WUV, in_=ukv[:, :, 1, :])], W=["WUV"], dma=True, chan="Bwuv")
        st = {"hc": 0, "tc": 0}
        P.add("dve", lambda e: e.memset(KTR[64:128, :], 0.0), W=["KVz"])
        P.add("dve", lambda e: e.memset(QRs[0][64:128, :, :], 0.0), W=["Qz0"])
        P.add("dve", lambda e: e.memset(QRs[1][64:128, :, :], 0.0), W=["Qz1"])

        def load_kv(seq):
            if seq == "s":
                P.add("sp", lambda e: [e.dma_start(out=KTL[:, 0:4096], in_=kt_samp[0:128, :]),
                                       e.dma_start(out=KTR[0:64, 0:4096], in_=kt_samp[128:192, :]),
                                       e.dma_start(out=V[:, 0:32, :], in_=v_samp.rearrange("(j p) l -> p j l", p=128))],
                      W=["KV"], dma=True, chan="Bkv", n=3)
            else:
                b = seq
                def f_kv(e):
                    r = []
                    for rr in range(8):
                        if USE_CC:
                            gk = gath[rr * SR + 1024:rr * SR + 2560, :].rearrange("(f a) c -> f (a c)", a=8)
                            gvv = gath[rr * SR:rr * SR + 1024, :].rearrange("r (q l) -> (r q) l", q=4)
                            r.append(e.dma_start(out=KTL[:, rr * 2048:(rr + 1) * 2048], in_=gk[0:128, b * 2048:(b + 1) * 2048]))
                            r.append(e.dma_start(out=KTR[0:64, rr * 2048:(rr + 1) * 2048], in_=gk[128:192, b * 2048:(b + 1) * 2048]))
                            r.append(e.dma_start(out=V[:, rr * 16:(rr + 1) * 16, :],
                                                 in_=gvv[b * 2048:(b + 1) * 2048, :].rearrange("(j p) l -> p j l", p=128)))
                            continue
                        r.append(e.dma_start(out=KTL[:, rr * 2048:(rr + 1) * 2048], in_=kt_p[b][0:128, rr * 2048:(rr + 1) * 2048]))
                        r.append(e.dma_start(out=KTR[0:64, rr * 2048:(rr + 1) * 2048], in_=kt_p[b][128:192, rr * 2048:(rr + 1) * 2048]))
                        r.append(e.dma_start(out=V[:, rr * 16:(rr + 1) * 16, :],
                                             in_=v_p[b][rr * 2048:(rr + 1) * 2048, :].rearrange("(j p) l -> p j l", p=128)))
                    return r
                P.add("sp", f_kv, W=["KV"], dma=True, chan="Bkv", n=24)

        def load_blk(k, tok0):
            bs = k % 2
            QL, QR, X1, mixT = QLs[bs], QRs[bs], X1s[bs], mixTs[bs]
            qlv = ql_d.rearrange("(h p) n -> p h n", p=128)[:, :, tok0:tok0 + 512]
            qrv = qr_d.rearrange("(h p) n -> p h n", p=64)[:, :, tok0:tok0 + 512]
            P.add("sp", lambda e: [e.dma_start(out=QL, in_=qlv), e.dma_start(out=QR[0:64, :, :], in_=qrv)],
                  W=["Q%d" % bs], dma=True, chan="Bq%d" % bs, n=2)
            lt = tok0 if (USE_CC or tok0 < 4096) else 16384 + tok0 - 4096
            x1v = x1_d[lt:lt + 512, :].rearrange("(s p) d -> p s d", p=128)
            P.add("sp", lambda e: [e.dma_start(out=X1, in_=x1v)], W=["X1_%d" % bs], dma=True, chan="Bx%d" % bs)
            fv = four_d.rearrange("(g p) n -> p g n", p=128)[:, :, tok0:tok0 + 512]
            P.add("sp", lambda e: [e.dma_start(out=mixT[:, 4:8, :], in_=fv)], W=["mixF%d" % bs], dma=True, chan="Bf%d" % bs)

        pend = []

        def flush(upto):
            while pend and pend[0][0] <= upto:
                pend.pop(0)[1]()

        def head(k, h, nk):
            bs = k % 2
            QL, QR, mixT = QLs[bs], QRs[bs], mixTs[bs]
            hs = st["hc"] % 2
            st["hc"] += 1
            po_b = 3 + hs
            SB = (0, 1, 2)
            def qk(kt):
                sb = SB[kt % 3]
                def f(e):
                    e.matmul(out=PS[sb][:, :], lhsT=KTL[:, kt * 128:(kt + 1) * 128], rhs=QL[:, h, :], start=True, stop=False)
                    return e.matmul(out=PS[sb][:, :], lhsT=KTR[:, kt * 128:(kt + 1) * 128], rhs=QR[:, h, :],
                                    start=False, stop=True)
                P.add("pe", f, R=["KV", "Q%d" % bs, "KVz", "Qz%d" % bs], W=["ps%d" % sb])
            def rest(kt):
                sb = SB[kt % 3]
                ps_ = st["tc"] % 6
                st["tc"] += 1
                P.add("act", lambda e: e.activation(out=pT[ps_], in_=PS[sb][:, :], func=AF.Exp),
                      R=["ps%d" % sb], W=["pT%d" % ps_])
                if kt % 2 == 1:
                    pr = ps_ // 2
                    w = sacc2w[hs]
                    if kt == 1:
                        P.add("dve", lambda e: e.tensor_copy(out=w, in_=pTp[pr]),
                              R=["pT%d" % (ps_ - 1), "pT%d" % ps_], W=["sacc%d" % hs])
                    else:
                        P.add("dve", lambda e: e.tensor_tensor(out=w, in0=w, in1=pTp[pr], op=ALU.add),
                              R=["pT%d" % (ps_ - 1), "pT%d" % ps_, "sacc%d" % hs], W=["sacc%d" % hs])
                P.add("pe", lambda e: e.matmul(out=PS[po_b][:, :], lhsT=V[:, kt, :], rhs=pT[ps_],
                                              start=(kt == 0), stop=(kt == nk - 1)),
                      R=["pT%d" % ps_, "KV"], W=["ps%d" % po_b])
            qk(0)
            qk(1)
            for kt in range(nk):
                if kt + 2 < nk:
                    qk(kt + 2)
                rest(kt)
                flush(kt)
            flush(10 ** 9)
            P.add("dve", lambda e: e.tensor_tensor(out=sacc[hs], in0=sacc2w[hs][:, 0, :], in1=sacc2w[hs][:, 1, :], op=ALU.add),
                  R=["sacc%d" % hs], W=["sacc%d" % hs])
            P.add("dve", lambda e: e.tensor_copy(out=saccb[hs], in_=sacc[hs]), R=["sacc%d" % hs], W=["saccb%d" % hs])
            P.add("dve", lambda e: e.tensor_tensor(out=slo[hs], in0=sacc[hs], in1=saccb[hs], op=ALU.subtract),
                  R=["saccb%d" % hs, "sacc%d" % hs], W=["slo%d" % hs])
            def step_a():
                def f_sum(e):
                    e.matmul(out=PS[5][:, :], lhsT=ones[:, :], rhs=saccb[hs], start=True, stop=False)
                    return e.matmul(out=PS[5][:, :], lhsT=ones[:, :], rhs=slo[hs], start=False, stop=True)
                P.add("pe", f_sum, R=["saccb%d" % hs, "slo%d" % hs, "ones"], W=["ps5"])
                P.add("dve", lambda e: e.reciprocal(out=rinv[hs], in_=PS[5][:, :]), R=["ps5"], W=["rinv%d" % hs])
                P.add("dve", lambda e: e.tensor_tensor(out=olat[hs], in0=PS[po_b][:, :], in1=rinv[hs], op=ALU.mult),
                      R=["ps%d" % po_b, "rinv%d" % hs], W=["olat%d" % hs])
            def step_b():
                P.add("pe", lambda e: e.matmul(out=PS[5][:, :], lhsT=WUV[:, h, :], rhs=olat[hs], start=True, stop=True),
                      R=["olat%d" % hs, "WUV"], W=["ps5"])
                P.add("act", lambda e: e.copy(out=mixT[:, h, :], in_=PS[5][:, :]), R=["ps5"], W=["mix%d_%d" % (bs, h)])
            pend.append((3, step_a))
            pend.append((9, step_b))

        def wo(k, tok0):
            bs = k % 2
            X1, mixT = X1s[bs], mixTs[bs]
            def one(s, half):
                bank = 6 + half
                def f_wo(e):
                    r = None
                    for ch in range(8):
                        r = e.matmul(out=PS[bank][:, :], lhsT=mixT[:, ch, s * 128:(s + 1) * 128],
                                     rhs=WO[:, ch, half * 512:(half + 1) * 512], start=(ch == 0), stop=(ch == 7))
                    return r
                P.add("pe", f_wo, R=["mix%d_%d" % (bs, h) for h in range(4)] + ["mixF%d" % bs, "WO"], W=["ps%d" % bank])
                P.add("dve", lambda e: e.tensor_tensor(
                    out=X1[:, s, half * 512:(half + 1) * 512], in0=PS[bank][:, :],
                    in1=X1[:, s, half * 512:(half + 1) * 512], op=ALU.add),
                    R=["ps%d" % bank, "X1_%d" % bs], W=["X1_%d" % bs])
            kt0 = 12
            for s in range(4):
                for half in range(2):
                    pend.append((kt0, (lambda s=s, half=half: one(s, half))))
                    kt0 += 2
            x2v = x2_d[tok0:tok0 + 512, :].rearrange("(s p) d -> p s d", p=128)
            pend.append((kt0, lambda: P.add("sp", lambda e: [e.dma_start(out=x2v, in_=X1)], R=["X1_%d" % bs], dma=True,
                                            chan="Bxs%d" % bs)))

        blocks = []
        for seq in (("s", 0, 1) if USE_CC else ("s", 0)):
            if seq == "s":
                for j in range(8):
                    blocks.append((seq, 4096 + 512 * j, 32))
            elif USE_CC:
                for j in range(4):
                    blocks.append((seq, seq * 2048 + 512 * j, 128))
            else:
                for j in range(8):
                    blocks.append((seq, 512 * j, 128))
        cur = None
        load_blk(0, blocks[0][1])
        for k, (seq, tok0, nk) in enumerate(blocks):
            if seq != cur:
                load_kv(seq)
                cur = seq
            head(k, 0, nk)
            if k + 1 < len(blocks):
                load_blk(k + 1, blocks[k + 1][1])
            for h in range(1, 4):
                head(k, h, nk)
            wo(k, tok0)
        flush(10 ** 9)
        P.barrier()

    import os
    stop = int(os.environ.get("MK_STOP", "99"))
    if stop >= 1:
        ffn_phase("F1", xin, x1_d, w1_gate, w1_up, w1_down, g_ffn1, None, NTL)
    if stop >= 2:
        proj_phase()
    if stop >= 3:
        fourier1()
    if stop >= 4:
        fourier2()
    if stop >= 5:
        attn_phase()
    if stop >= 6:
        ffn_phase("F2", x2_d, yout, w2_gate, w2_up, w2_down, g_ffn2, g_final, NT)
    P.barrier()
    P.emit()
    es.close()
    return nc


_CACHE = {}


def _tables(c):
    f32 = np.float32
    inv = (1.0 / (np.float32(10000.0) ** (np.arange(0, 64, 2, dtype=f32) / f32(64)))).astype(f32)
    if USE_CC:
        ppos = 2048 * c + np.arange(2048)
    else:
        t = np.arange(16384)
        ppos = 4096 * (((t // 4096) + (c % 4)) % 4) + (t % 4096)
    pos = (np.concatenate([ppos, ppos, np.arange(4096)]) if USE_CC else np.concatenate([ppos, np.arange(4096)])).astype(f32)
    ang = (pos[:, None] * inv[None, :]).astype(f32)
    cs, sn = np.cos(ang).astype(f32), np.sin(ang).astype(f32)
    rope_tok = np.concatenate([cs, cs, -sn, sn], axis=1).astype(f32)
    rope_cos = (np.concatenate([cs, cs], axis=1).T * f32(SCALE)).astype(f32)
    rope_sin = (np.concatenate([-sn, sn], axis=1).T * f32(SCALE)).astype(f32)
    k2 = np.arange(2048)
    tw = np.zeros((20, 2, 2048), np.float64)
    normP = 1.0 / np.sqrt(16384.0 * 128.0)
    normS = 1.0 / np.sqrt(4096.0 * 128.0)
    for b in range(2):
        for n1 in range(8):
            k = (2048 * c + k2) if USE_CC else (4096 * (c % 4) + 2048 * b + k2)
            a = 2 * np.pi * (((n1 * k) % 16384) / 16384.0 + (0.0 if USE_CC else (((c % 4) * k) % 4) / 4.0))
            tw[b * 8 + n1, 0] = np.cos(a) * normP
            tw[b * 8 + n1, 1] = -np.sin(a) * normP
    for hh in range(2):
        for n1 in range(2):
            k = 2048 * hh + k2
            a = 2 * np.pi * ((n1 * k) % 4096) / 4096.0
            tw[16 + hh * 2 + n1, 0] = np.cos(a) * normS
            tw[16 + hh * 2 + n1, 1] = -np.sin(a) * normS
    twt = np.zeros((4, 2048, 16), np.float64)
    for seg in range(2):
        twt[seg] = tw[seg * 8:(seg + 1) * 8].transpose(2, 0, 1).reshape(2048, 16)
    for hh in range(2):
        twt[2 + hh, :, 0:4] = tw[16 + hh * 2:16 + hh * 2 + 2].transpose(2, 0, 1).reshape(2048, 4)
    return rope_tok, np.ascontiguousarray(rope_cos), np.ascontiguousarray(rope_sin), tw.astype(f32), twt.astype(f32)


def _const_tables():
    n = np.arange(2048)
    m = (n[:, None] * n[None, :]) % 2048
    a = 2 * np.pi * m / 2048.0
    dc, ds = np.cos(a).astype(np.float32), np.sin(a).astype(np.float32)
    c = np.arange(128)
    ac = 2 * np.pi * ((c[:, None] * c[None, :]) % 128) / 128.0
    ch = np.stack([np.cos(ac), np.sin(ac), -np.sin(ac)]).astype(np.float32)
    return dc, ds, ch, np.eye(128, dtype=np.float32)


def kernel(**inputs):
    if "nc" not in _CACHE:
        _CACHE["nc"] = build_program()
        _CACHE["const"] = _const_tables()
    nc = _CACHE["nc"]
    dc, ds, ch, ident = _CACHE["const"]
    f = lambda k: np.ascontiguousarray(np.asarray(inputs[k], dtype=np.float32))
    xp, xs = f("x_prompt"), f("x_sample")
    shared = {
        "g_ffn1": f("g_ffn1")[0], "g_mix": f("g_mix")[0], "g_ffn2": f("g_ffn2")[0], "g_final": f("g_final"),
        "g_q": f("g_q")[0], "g_kv": f("g_kv")[0],
        "w_uq": f("w_uq")[0], "w_ukv": f("w_ukv")[0], "ch_tabs": ch, "ident": ident,
    }
    for k in ("w1_gate", "w1_up", "w1_down", "w2_gate", "w2_up", "w2_down", "w_in", "w_o"):
        shared[k] = f(k)[0]
    shared["dft_cos"], shared["dft_sin"] = dc, ds
    in_maps = []
    for c in range(NCORE):
        rt, rc, rs, tw, twt = _tables(c)
        m = dict(shared)
        if USE_CC:
            p0, p1 = xp[0, 2048 * c:2048 * (c + 1)], xp[1, 2048 * c:2048 * (c + 1)]
        else:
            p0 = np.roll(xp[c // 4].reshape(4, 4096, D), -(c % 4), axis=0).reshape(16384, D)
            p1 = None
        m["xin"] = np.ascontiguousarray(np.concatenate([p0, xs[c]] if p1 is None else [p0, p1, xs[c]], axis=0))
        m["rope_tok"], m["rope_cos"], m["rope_sin"], m["tw"], m["twt"] = rt, rc, rs, tw, twt
        in_maps.append(m)
    res = run_bass_kernel_spmd(nc, in_maps, core_ids=list(range(NCORE)))
    yp = np.empty((2, 16384, D), np.float32)
    ys = np.empty((8, 4096, D), np.float32)
    for c in range(NCORE):
        y = np.asarray(res.results[c]["yout"])
        if USE_CC:
            yp[0, 2048 * c:2048 * (c + 1)] = y[0:2048]
            yp[1, 2048 * c:2048 * (c + 1)] = y[2048:4096]
        else:
            yp[c // 4, 4096 * (c % 4):4096 * (c % 4 + 1)] = y[0:4096]
        ys[c] = y[4096:8192]
    return (yp, ys)
```

```python
import numpy as np
from contextlib import ExitStack

import concourse.bass as bass
import concourse.mybir as mybir
from concourse.bass_utils import run_bass_kernel_spmd

F32 = mybir.dt.float32
BF16 = mybir.dt.bfloat16
ALU = mybir.AluOpType
AF = mybir.ActivationFunctionType

D = 1024
DFF = 2816
NFC = 22
NTOK = 8192
NT = 16
USE_CC = False
NLOC = 8192 if USE_CC else 20480
NTL = NLOC // 512
NCORE = 8
EPS = 1e-6
SCALE = 192 ** -0.5


class Op:
    __slots__ = ("eng", "fn", "deps", "sig", "dma", "chan", "n", "has_dep", "cc")


class Prog:
    ENGS = ("pe", "act", "dve", "pool", "sp")

    def __init__(self, nc):
        self.nc = nc
        self.ops = []
        self.lw = {}
        self.rd = {}
        self.pending_dma = []
        self.last_compute = {}
        self.init_hooks = {}

    def add(self, eng, fn, R=(), W=(), dma=False, chan=None, n=1, cc=False):
        op = Op()
        op.eng, op.fn, op.dma, op.chan, op.n = eng, fn, dma, chan, n
        op.cc = cc
        op.deps, op.sig, op.has_dep = set(), None, False
        for r in R:
            w = self.lw.get(r)
            if w is not None:
                self._dep(op, w, True)
        for x in W:
            w = self.lw.get(x)
            if w is not None:
                self._dep(op, w, False)
            for r in self.rd.get(x, {}).values():
                self._dep(op, r, False)
        for r in R:
            k = ("d", len(self.ops)) if dma else eng
            self.rd.setdefault(r, {})[k] = op
        for x in W:
            self.lw[x] = op
            self.rd[x] = {}
        self.ops.append(op)
        if dma:
            self.pending_dma.append(op)
        else:
            self.last_compute[eng] = op
        return op

    def _dep(self, op, d, raw):
        if d is op:
            return
        if (not op.dma) and (not d.dma) and op.eng == d.eng:
            if (not raw) or op.eng == "pe":
                return
        op.deps.add(d)
        d.has_dep = True

    def barrier(self):
        targets = list(self.last_compute.values()) + list(self.pending_dma)
        for e in self.ENGS:
            b = Op()
            b.eng, b.fn, b.dma, b.chan, b.n = e, None, False, None, 1
            b.cc = False
            b.deps, b.sig, b.has_dep = set(), None, False
            for t in targets:
                if (not t.dma) and t.eng == e:
                    continue
                b.deps.add(t)
                t.has_dep = True
            self.ops.append(b)
        self.lw.clear()
        self.rd.clear()
        self.pending_dma = []

    def emit(self):
        nc = self.nc
        cnt = {}
        for o in self.ops:
            if not o.has_dep:
                continue
            key = ("c", o.chan) if o.dma else ("e", o.eng)
            cnt[key] = cnt.get(key, 0) + ((1 if o.cc else 16 * o.n) if o.dma else 1)
            o.sig = (key, cnt[key])
        with ExitStack() as es:
            sems = {}
            for i, key in enumerate(cnt):
                sems[key] = es.enter_context(nc.semaphore("s%d" % i))
            block = es.enter_context(nc.Block())
            decos = {"pe": block.tensor, "act": block.scalar, "dve": block.vector,
                     "pool": block.gpsimd, "sp": block.sync}
            for e in self.ENGS:
                ops_e = [o for o in self.ops if o.eng == e]

                def body(eng, ops_e=ops_e, e=e):
                    waited = {}
                    if e in self.init_hooks:
                        self.init_hooks[e](eng)
                    for o in ops_e:
                        need = {}
                        for d in o.deps:
                            k, v = d.sig
                            if need.get(k, 0) < v:
                                need[k] = v
                        for k, v in need.items():
                            if waited.get(k, 0) < v:
                                eng.wait_ge(sems[k], v)
                                waited[k] = v
                        if o.fn is None:
                            continue
                        r = o.fn(eng)
                        if o.sig is not None:
                            if o.cc:
                                r[0].then_inc(sems[o.sig[0]])
                            elif o.dma:
                                assert len(r) == o.n
                                for ins in r:
                                    ins.then_inc(sems[o.sig[0]], 16)
                            else:
                                r.then_inc(sems[o.sig[0]], 1)

                decos[e](body)


class Arena:
    def __init__(self, ap, nbytes):
        self.ap = ap
        self.cap = nbytes // 2
        self.off = 0

    def reset(self):
        self.off = 0

    def bf(self, n):
        n2 = (n + 31) // 32 * 32
        assert self.off + n2 <= self.cap, ("arena overflow", self.off, n2, self.cap)
        v = self.ap[:, self.off:self.off + n]
        self.off += n2
        return v

    def f32(self, n):
        return self.bf(2 * n).bitcast(F32)


def build_program():
    nc = bass.Bass("TRN2", target_bir_lowering=False)

    def din(name, shape):
        return nc.dram_tensor(name, shape, F32, kind="ExternalInput").ap()

    xin = din("xin", [NLOC, D])
    g_ffn1 = din("g_ffn1", [D]); g_mix = din("g_mix", [D]); g_ffn2 = din("g_ffn2", [D])
    g_final = din("g_final", [D]); g_q = din("g_q", [256]); g_kv = din("g_kv", [128])
    w_uq = din("w_uq", [256, 768]); w_ukv = din("w_ukv", [128, 1024])
    BIGW = {"w1_gate": (D, DFF), "w1_up": (D, DFF), "w1_down": (DFF, D),
            "w2_gate": (D, DFF), "w2_up": (D, DFF), "w2_down": (DFF, D),
            "w_in": (D, 960), "w_o": (D, D), "dft_cos": (2048, 2048), "dft_sin": (2048, 2048)}
    wfull = {k: din(k, [r, c]) for k, (r, c) in BIGW.items()}
    rope_tok = din("rope_tok", [NLOC, 128])
    rope_cos = din("rope_cos", [64, NLOC]); rope_sin = din("rope_sin", [64, NLOC])
    ch_tabs = din("ch_tabs", [3, 128, 128])
    tw = din("tw", [20, 2, 2048])
    twt = din("twt", [4, 2048, 16])
    ident_in = din("ident", [128, 128])
    yout = nc.dram_tensor("yout", [NTOK, D], F32, kind="ExternalOutput").ap()

    def dscr(name, shape, dt):
        return nc.dram_tensor(name, shape, dt).ap()

    x1_d = dscr("x1_d", [NLOC, D], F32)
    x2_d = dscr("x2_d", [NTOK, D], F32)
    ql_d = dscr("ql_d", [4 * 128, NTOK], BF16)
    qr_d = dscr("qr_d", [4 * 64, NTOK], BF16)
    v_p = [dscr("v_p%d" % b, [16384, 128], BF16) for b in range(2)]
    kt_p = [dscr("kt_p%d" % b, [192, 16384], BF16) for b in range(2)]
    u_p = [dscr("u_p%d" % b, [16384, 512], BF16) for b in range(2)]
    SR = 6656
    send = dscr("send", [SR, 512], BF16)
    gath = dscr("gath", [8 * SR, 512], BF16)
    v_samp = dscr("v_samp", [4096, 128], BF16)
    kt_samp = dscr("kt_samp", [192, 4096], BF16)
    u_samp = dscr("u_samp", [4096, 512], BF16)
    ef_g = dscr("ef_g", [8 * 2048, 2048], BF16)
    ef_samp = dscr("ef_samp", [2048, 2048], BF16)
    ef_t = dscr("ef_t", [8 * 2 * 2048, 512], BF16)
    efs_t = dscr("efs_t", [2 * 2 * 2048, 512], BF16)
    four_d = dscr("four_d", [512, NTOK], BF16)

    P = Prog(nc)
    if USE_CC:
        P.init_hooks['pool'] = lambda eng: get_pid()
    es = ExitStack()
    ARENA_BYTES = 206 * 1024
    arena_t = es.enter_context(nc.sbuf_tensor("arena", [128, ARENA_BYTES // 2], BF16))
    AR = Arena(arena_t, ARENA_BYTES)
    ident = es.enter_context(nc.sbuf_tensor("identb", [128, 128], BF16))
    ones = es.enter_context(nc.sbuf_tensor("onesb", [128, 128], BF16))
    dummy = es.enter_context(nc.sbuf_tensor("dummyt", [128, 8], F32))
    epst = es.enter_context(nc.sbuf_tensor("epst", [128, 1], F32))
    PS = [es.enter_context(nc.psum_tensor("ps%d" % i, [128, 512], F32)) for i in range(8)]

    def psb(i):
        return PS[i][:, :].bitcast(BF16).rearrange("p (a b) -> p a b", a=8)

    pidc = {}

    def get_pid():
        if 'v' not in pidc:
            pidc['v'] = nc.partition_id([mybir.EngineType.Pool])
        return pidc['v']

    P.add("pool", lambda e: [e.dma_start(out=ident[:, :], in_=ident_in)], W=["ident"], dma=True, chan="c_ident")
    P.add("dve", lambda e: e.memset(ones[:, :], 1.0), W=["ones"])
    P.add("dve", lambda e: e.memset(epst[:, :], EPS), W=["eps"])
    P.add("dve", lambda e: e.memset(dummy[:, :], 0.0), W=["dummy"])

    w1_gate, w1_up, w1_down = wfull["w1_gate"], wfull["w1_up"], wfull["w1_down"]
    w2_gate, w2_up, w2_down = wfull["w2_gate"], wfull["w2_up"], wfull["w2_down"]
    w_in, w_o, dft_cos, dft_sin = wfull["w_in"], wfull["w_o"], wfull["dft_cos"], wfull["dft_sin"]

    def load_w_kc(dst3, src2, kchunks, chan, key, eng="pool", rkeys=()):
        v = src2.rearrange("(kc p) f -> p kc f", p=128)
        def fn(e):
            return [e.dma_start(out=dst3[:, kc, :], in_=v[:, kc, :]) for kc in range(kchunks)]
        P.add(eng, fn, R=list(rkeys), W=[key], dma=True, chan=chan, n=kchunks)

    def norm_a(tag, X, gain, hn4, stat, src_keys):
        ss, rt, rs = stat
        def f_stats(e):
            for s in range(4):
                e.scalar_tensor_tensor(out=hn4[:, s, :], in0=X[:, s, :], scalar=1.0 / D, in1=X[:, s, :],
                                       op0=ALU.mult, op1=ALU.mult, accum_out=ss[:, s:s + 1])
            return e.tensor_copy(out=dummy[:, 0:1], in_=dummy[:, 1:2])
        P.add("dve", f_stats, R=src_keys + ["dummy"], W=[tag + "ss"] + [tag + "hn%d" % s for s in range(4)])
        P.add("act", lambda e: e.activation(out=rt, in_=ss, func=AF.Sqrt, bias=epst[:, 0:1], scale=1.0),
              R=[tag + "ss", "eps"], W=[tag + "rt"])
        P.add("dve", lambda e: e.reciprocal(out=rs, in_=rt), R=[tag + "rt"], W=[tag + "rs"])
        def one(s):
            P.add("dve", lambda e: e.scalar_tensor_tensor(
                out=hn4[:, s, :], in0=X[:, s, :], scalar=rs[:, s:s + 1], in1=gain,
                op0=ALU.mult, op1=ALU.mult), R=src_keys + [tag + "rs", tag + "gain"], W=[tag + "hn%d" % s])
        for s in range(4):
            one(s)

    def norm_a_split(tag, X, gain, hn4, stat, src_keys, junkb):
        ss, rt, rs = stat
        def stats():
            def f_stats(e):
                for s in range(4):
                    e.activation(out=junkb, in_=X[:, s, :], func=AF.Square, scale=1.0 / 32.0, accum_out=ss[:, s:s + 1])
                return e.activation(out=dummy[:, 2:3], in_=dummy[:, 3:4], func=AF.Copy)
            P.add("act", f_stats, R=src_keys + ["dummy"], W=[tag + "ss", tag + "junkb"])
            P.add("act", lambda e: e.activation(out=rt, in_=ss, func=AF.Sqrt, bias=epst[:, 0:1], scale=1.0),
                  R=[tag + "ss", "eps"], W=[tag + "rt"])
            P.add("dve", lambda e: e.reciprocal(out=rs, in_=rt), R=[tag + "rt"], W=[tag + "rs"])
        def hn(s):
            P.add("dve", lambda e: e.scalar_tensor_tensor(
                out=hn4[:, s, :], in0=X[:, s, :], scalar=rs[:, s:s + 1], in1=gain,
                op0=ALU.mult, op1=ALU.mult), R=src_keys + [tag + "rs", tag + "gain"], W=[tag + "hn%d" % s])
        return stats, hn

    def norm_b(tag, hn4, hT):
        def one(s):
            tb = s % 2
            def f_tr(e):
                r = None
                for kc in range(8):
                    r = e.transpose(out=psb(tb)[:, kc, :], in_=hn4[:, s, kc * 128:(kc + 1) * 128],
                                    identity=ident[:, :])
                return r
            P.add("pe", f_tr, R=[tag + "hn%d" % s, "ident"], W=["psT%d" % tb])
            P.add("act", lambda e: e.copy(out=hT[:, :, s * 128:(s + 1) * 128], in_=psb(tb)),
                  R=["psT%d" % tb], W=[tag + "hT%d" % s])
        for s in range(4):
            one(s)

    def ffn_phase(tag, src, dst, wg, wu, wd, g_in, g_fin, ntiles):
        AR.reset()
        t = tag
        WG = AR.bf(8 * DFF).rearrange("p (a b) -> p a b", a=8)
        WU = AR.bf(8 * DFF).rearrange("p (a b) -> p a b", a=8)
        WD = AR.bf(NFC * D).rearrange("p (a b) -> p a b", a=NFC)
        XP = [AR.f32(D) for _ in range(2)]
        XR = [AR.f32(D) for _ in range(2)]
        hn4 = AR.bf(4 * D).rearrange("p (a b) -> p a b", a=4)
        hT = AR.bf(8 * 512).rearrange("p (a b) -> p a b", a=8)
        actT = AR.bf(NFC * 512).rearrange("p (a b) -> p a b", a=NFC)
        sg = [AR.f32(512) for _ in range(2)]
        junk = AR.bf(D)
        gain = AR.f32(D)
        gfin = AR.f32(D) if g_fin is not None else None
        stp = [(AR.f32(1), AR.f32(1), AR.f32(1)) for _ in range(2)]
        stf = [(AR.f32(1), AR.f32(1), AR.f32(1)) for _ in range(2)]
        P.add("sp", lambda e: [e.dma_start(out=gain, in_=g_in.partition_broadcast(128))],
              W=[t + "gain"], dma=True, chan=t + "gain")
        if g_fin is not None:
            P.add("sp", lambda e: [e.dma_start(out=gfin, in_=g_fin.partition_broadcast(128))],
                  W=[t + "gfin"], dma=True, chan=t + "gfin")
        cnt = {"p": 0, "r": 0}

        def prep_sub(i, s):
            sl = cnt["p"] % 2
            cnt["p"] += 1
            xp = XP[sl]
            ss, rt, rs = stp[sl]
            r0 = i * 512 + s * 128
            P.add("sp", lambda e: [e.dma_start(out=xp, in_=src[r0:r0 + 128, :])], W=[t + "XP%d" % sl], dma=True,
                  chan=t + "xp%d" % sl)
            def f_stats(e):
                e.scalar_tensor_tensor(out=hn4[:, s, :], in0=xp, scalar=1.0 / D, in1=xp,
                                       op0=ALU.mult, op1=ALU.mult, accum_out=ss)
                return e.tensor_copy(out=dummy[:, 0:1], in_=dummy[:, 1:2])
            P.add("dve", f_stats, R=[t + "XP%d" % sl, "dummy"], W=[t + "ss%d" % sl, t + "hn%d" % s])
            P.add("act", lambda e: e.activation(out=rt, in_=ss, func=AF.Sqrt, bias=epst[:, 0:1], scale=1.0),
                  R=[t + "ss%d" % sl, "eps"], W=[t + "rt%d" % sl])
            P.add("dve", lambda e: e.reciprocal(out=rs, in_=rt), R=[t + "rt%d" % sl], W=[t + "rs%d" % sl])
            P.add("dve", lambda e: e.scalar_tensor_tensor(
                out=hn4[:, s, :], in0=xp, scalar=rs[:, 0:1], in1=gain, op0=ALU.mult, op1=ALU.mult),
                R=[t + "XP%d" % sl, t + "rs%d" % sl, t + "gain"], W=[t + "hn%d" % s])

        def prep_a(i):
            for s in range(4):
                prep_sub(i, s)

        load_w_kc(WG, wg, 8, t + "wg", t + "WG")
        load_w_kc(WU, wu, 8, t + "wu", t + "WU")
        load_w_kc(WD, wd, NFC, t + "wd", t + "WD")
        hTkeys = [t + "hT%d" % s for s in range(4)]

        def gu(fc):
            gb, ub = 2 + (fc % 2), 4 + (fc % 2)
            def mm(Wm, bank, wkey):
                def f_mm(e):
                    r = None
                    for kc in range(8):
                        r = e.matmul(out=PS[bank][:, :], lhsT=Wm[:, kc, fc * 128:(fc + 1) * 128],
                                     rhs=hT[:, kc, :], start=(kc == 0), stop=(kc == 7))
                    return r
                P.add("pe", f_mm, R=hTkeys + [wkey], W=["ps%d" % bank])
            mm(WG, gb, t + "WG")
            mm(WU, ub, t + "WU")
            P.add("act", lambda e: e.activation(out=sg[fc % 2], in_=PS[gb][:, :], func=AF.Silu),
                  R=["ps%d" % gb], W=[t + "sg%d" % (fc % 2)])
            P.add("dve", lambda e: e.tensor_tensor(out=actT[:, fc, :], in0=sg[fc % 2], in1=PS[ub][:, :], op=ALU.mult),
                  R=[t + "sg%d" % (fc % 2), "ps%d" % ub], W=[t + "actT%d" % fc])

        def down_sub(i, s):
            sl = cnt["r"] % 2
            cnt["r"] += 1
            xr = XR[sl]
            xkey = t + "XR%d" % sl
            r0 = i * 512 + s * 128
            P.add("sp", lambda e: [e.dma_start(out=xr, in_=src[r0:r0 + 128, :])], W=[xkey], dma=True, chan=t + "xr%d" % sl)
            def half_(half):
                bank = 6 + half
                def f_dn(e):
                    r = None
                    for fc in range(NFC):
                        r = e.matmul(out=PS[bank][:, :], lhsT=actT[:, fc, s * 128:(s + 1) * 128],
                                     rhs=WD[:, fc, half * 512:(half + 1) * 512],
                                     start=(fc == 0), stop=(fc == NFC - 1))
                    return r
                P.add("pe", f_dn, R=[t + "actT%d" % fc for fc in range(NFC)] + [t + "WD"], W=["ps%d" % bank])
                P.add("dve", lambda e: e.scalar_tensor_tensor(
                    out=xr[:, half * 512:(half + 1) * 512], in0=PS[bank][:, :], scalar=0.5,
                    in1=xr[:, half * 512:(half + 1) * 512], op0=ALU.mult, op1=ALU.add),
                    R=["ps%d" % bank, xkey], W=[xkey])
            half_(0)
            half_(1)
            if g_fin is not None:
                ss, rt, rs = stf[sl]
                def f_st(e):
                    e.scalar_tensor_tensor(out=junk, in0=xr, scalar=1.0 / D, in1=xr,
                                           op0=ALU.mult, op1=ALU.mult, accum_out=ss)
                    return e.tensor_copy(out=dummy[:, 0:1], in_=dummy[:, 1:2])
                P.add("dve", f_st, R=[xkey, "dummy"], W=[t + "fss%d" % sl, t + "junk"])
                P.add("act", lambda e: e.activation(out=rt, in_=ss, func=AF.Sqrt, bias=epst[:, 0:1], scale=1.0),
                      R=[t + "fss%d" % sl, "eps"], W=[t + "frt%d" % sl])
                P.add("dve", lambda e: e.reciprocal(out=rs, in_=rt), R=[t + "frt%d" % sl], W=[t + "frs%d" % sl])
                P.add("dve", lambda e: e.scalar_tensor_tensor(
                    out=xr, in0=xr, scalar=rs[:, 0:1], in1=gfin, op0=ALU.mult, op1=ALU.mult),
                    R=[xkey, t + "frs%d" % sl, t + "gfin"], W=[xkey])
            P.add("sp", lambda e: [e.dma_start(out=dst[r0:r0 + 128, :], in_=xr)], R=[xkey], dma=True, chan=t + "xs%d" % sl)

        def tile(i):
            for fc in range(NFC):
                gu(fc)
                if fc == 9 and i + 1 < ntiles:
                    prep_a(i + 1)
            if i + 1 < ntiles:
                norm_b(t, hn4, hT)
            for s in range(4):
                down_sub(i, s)

        prep_a(0)
        norm_b(t, hn4, hT)
        for i in range(ntiles):
            tile(i)
        P.barrier()

    def proj_phase():
        AR.reset()
        t = "A"
        WIN = AR.bf(8 * 960).rearrange("p (a b) -> p a b", a=8)
        WUQ = AR.bf(2 * 768).rearrange("p (a b) -> p a b", a=2)
        WUQS = AR.bf(2 * 256).rearrange("p (a b) -> p a b", a=2)
        WUKn = AR.bf(4 * 128).rearrange("p (a b) -> p a b", a=4)
        WUKT = AR.bf(4 * 128).rearrange("p (a b) -> p a b", a=4)
        Xb = [AR.f32(4 * D).rearrange("p (a b) -> p a b", a=4) for _ in range(2)]
        hn4 = AR.bf(4 * D).rearrange("p (a b) -> p a b", a=4)
        hT = AR.bf(8 * 512).rearrange("p (a b) -> p a b", a=8)
        gain = AR.f32(D)
        gq = AR.f32(256)
        gkv = AR.f32(128)
        stat = [(AR.f32(4), AR.f32(4), AR.f32(4)) for _ in range(2)]
        RT = [AR.f32(4 * 128).rearrange("p (a b) -> p a b", a=4) for _ in range(2)]
        RC = [AR.f32(512) for _ in range(2)]
        RS = [AR.f32(512) for _ in range(2)]
        st2 = [(AR.f32(2), AR.f32(2), AR.f32(2)) for _ in range(2)]
        junk = [AR.bf(384) for _ in range(2)]
        PAS = [AR.f32(448) for _ in range(4)]
        junkb = AR.bf(D)
        cqn = [AR.bf(256) for _ in range(2)]
        krt = [AR.f32(64) for _ in range(2)]
        kra = [AR.f32(64) for _ in range(2)]
        krr = [AR.bf(64) for _ in range(2)]
        CQT = AR.bf(2 * 512).rearrange("p (a b) -> p a b", a=2)
        KTLs = [AR.bf(512) for _ in range(2)]
        KTRs = [AR.bf(512) for _ in range(2)]
        UBs = [AR.bf(4 * 512).rearrange("p (a b) -> p a b", a=4) for _ in range(2)]
        VBs = [AR.bf(4 * 128).rearrange("p (a b) -> p a b", a=4) for _ in range(2)]
        qn_sb = [AR.bf(512) for _ in range(2)]
        QLs = [AR.bf(4 * 512).rearrange("p (a b) -> p a b", a=4) for _ in range(2)]
        QRs = [AR.bf(4 * 512).rearrange("p (a b) -> p a b", a=4) for _ in range(2)]
        qt1 = [AR.f32(512) for _ in range(2)]
        qt2 = [AR.f32(512) for _ in range(2)]

        P.add("sp", lambda e: [e.dma_start(out=gain, in_=g_mix.partition_broadcast(128))], W=[t + "gain"], dma=True, chan="Again")
        P.add("sp", lambda e: [e.dma_start(out=gq, in_=g_q.partition_broadcast(128))], W=["gq"], dma=True, chan="Agq")
        P.add("sp", lambda e: [e.dma_start(out=gkv, in_=g_kv.partition_broadcast(128))], W=["gkv"], dma=True, chan="Agkv")

        def load_x(i):
            sl = i % 2
            v = x1_d[i * 512:(i + 1) * 512, :].rearrange("(s p) d -> p s d", p=128)
            P.add("sp", lambda e: [e.dma_start(out=Xb[sl], in_=v)], W=[t + "X%d" % sl], dma=True, chan="Axl%d" % sl)
            vr = rope_tok[i * 512:(i + 1) * 512, :].rearrange("(s p) d -> p s d", p=128)
            P.add("sp", lambda e: [e.dma_start(out=RT[sl], in_=vr)], W=["RT%d" % sl], dma=True, chan="Art%d" % sl)
            P.add("sp", lambda e: [e.dma_start(out=RC[sl][0:64, :], in_=rope_cos[:, i * 512:(i + 1) * 512]),
                                   e.dma_start(out=RS[sl][0:64, :], in_=rope_sin[:, i * 512:(i + 1) * 512])],
                  W=["RCS%d" % sl], dma=True, chan="Arcs%d" % sl, n=2)

        load_x(0)
        load_x(1)
        load_w_kc(WIN, w_in, 8, "Awin", "WIN")
        load_w_kc(WUQ, w_uq, 2, "Awuq", "WUQ", eng="pool")
        uqv = w_uq.rearrange("(kc p) (h f) -> p kc h f", p=128, h=4)
        def f_wuqs(e):
            r = []
            for h in range(4):
                for kc in range(2):
                    r.append(e.dma_start(out=WUQS[:, kc, h * 64:h * 64 + 32], in_=uqv[:, kc, h, 160:192]))
                    r.append(e.dma_start(out=WUQS[:, kc, h * 64 + 32:h * 64 + 64], in_=uqv[:, kc, h, 128:160]))
            return r
        P.add("pool", f_wuqs, W=["WUQS"], dma=True, chan="Awuqs", n=16)
        ukv = w_ukv.rearrange("l (h two n) -> l h two n", h=4, two=2)
        P.add("pool", lambda e: [e.dma_start(out=WUKn, in_=ukv[:, :, 0, :])], W=["WUKn"], dma=True, chan="Awukn")
        def f_trw(e):
            r = None
            for h in range(4):
                r = e.transpose(out=psb(0)[:, h, :], in_=WUKn[:, h, :], identity=ident[:, :])
            return r
        P.add("pe", f_trw, R=["WUKn", "ident"], W=["psT0"])
        P.add("act", lambda e: e.copy(out=WUKT, in_=psb(0)[:, 0:4, :]), R=["psT0"], W=["WUKT"])

        def subtile_parts(i, s, own=True):
            c0 = 0 if own else 256
            sl = i % 2
            b2 = s % 2
            pa, pb = (2, 3) if s % 2 == 0 else (4, 5)
            ka, kb_ = "ps%d" % pa, "ps%d" % pb
            UB, VB, KTL, KTR, RTt = UBs[sl], VBs[sl], KTLs[sl], KTRs[sl], RT[sl]
            ss, rt, rs = st2[b2]
            def f_pp(e):
                r = None
                for kc in range(8):
                    e.matmul(out=PS[pa][:, c0:448], lhsT=hT[:, kc, s * 128:(s + 1) * 128], rhs=WIN[:, kc, c0:448],
                             start=(kc == 0), stop=(kc == 7))
                for kc in range(8):
                    r = e.matmul(out=PS[pb][:, :], lhsT=hT[:, kc, s * 128:(s + 1) * 128], rhs=WIN[:, kc, 448:960],
                                 start=(kc == 0), stop=(kc == 7))
                return r
            pas = PAS[s]
            pk = "PAS%d" % s
            def pp():
                P.add("pe", f_pp, R=[t + "hT%d" % s, "WIN"], W=[ka, kb_])
                P.add("act", lambda e: e.copy(out=pas[:, c0:448], in_=PS[pa][:, c0:448]), R=[ka], W=[pk])
                P.add("act", lambda e: e.copy(out=UB[:, s, :], in_=PS[pb][:, :]), R=[kb_], W=["UB%d_%d" % (sl, s)])
            def post():
                def f_sq(e):
                    if own:
                        e.activation(out=junk[b2][:, 0:256], in_=pas[:, 0:256], func=AF.Square, scale=1.0 / 16.0,
                                     accum_out=ss[:, 0:1])
                    e.activation(out=junk[b2][:, 256:384], in_=pas[:, 256:384], func=AF.Square, scale=128 ** -0.5,
                                 accum_out=ss[:, 1:2])
                    return e.activation(out=dummy[:, 2:3], in_=dummy[:, 3:4], func=AF.Copy)
                P.add("act", f_sq, R=[pk, "dummy"], W=["ss2_%d" % b2, "junk%d" % b2])
                P.add("act", lambda e: e.activation(out=rt, in_=ss, func=AF.Sqrt, bias=epst[:, 0:1], scale=1.0),
                      R=["ss2_%d" % b2, "eps"], W=["rt2_%d" % b2])
                P.add("dve", lambda e: e.reciprocal(out=rs, in_=rt), R=["rt2_%d" % b2], W=["rs2_%d" % b2])
                if own:
                    P.add("dve", lambda e: e.scalar_tensor_tensor(
                        out=cqn[b2], in0=pas[:, 0:256], scalar=rs[:, 0:1], in1=gq, op0=ALU.mult, op1=ALU.mult),
                        R=[pk, "rs2_%d" % b2, "gq"], W=["cqn%d" % b2])
                P.add("dve", lambda e: e.scalar_tensor_tensor(
                    out=VB[:, s, :], in0=pas[:, 256:384], scalar=rs[:, 1:2], in1=gkv, op0=ALU.mult, op1=ALU.mult),
                    R=[pk, "rs2_%d" % b2, "gkv"], W=["VB%d_%d" % (sl, s)])
                def f_kr(e):
                    e.tensor_tensor(out=kra[b2], in0=pas[:, 384:448], in1=RTt[:, s, 0:64], op=ALU.mult)
                    e.tensor_tensor(out=krt[b2][:, 0:32], in0=pas[:, 416:448], in1=RTt[:, s, 64:96], op=ALU.mult)
                    e.tensor_tensor(out=krt[b2][:, 32:64], in0=pas[:, 384:416], in1=RTt[:, s, 96:128], op=ALU.mult)
                    return e.tensor_copy(out=dummy[:, 0:1], in_=dummy[:, 1:2])
                P.add("dve", f_kr, R=[pk, "RT%d" % sl, "dummy"], W=["kra%d" % b2])
                P.add("dve", lambda e: e.tensor_tensor(out=krr[b2], in0=kra[b2], in1=krt[b2], op=ALU.add),
                      R=["kra%d" % b2], W=["krr%d" % b2])
            def tr():
                def f_tr(e):
                    if own:
                        e.transpose(out=psb(b2)[:, 0, :], in_=cqn[b2][:, 0:128], identity=ident[:, :])
                        e.transpose(out=psb(b2)[:, 1, :], in_=cqn[b2][:, 128:256], identity=ident[:, :])
                    e.transpose(out=psb(b2)[:, 2, :], in_=VB[:, s, :], identity=ident[:, :])
                    return e.transpose(out=psb(b2)[0:64, 3, :], in_=krr[b2], identity=ident[:, :])
                P.add("pe", f_tr, R=(["cqn%d" % b2] if own else []) + ["VB%d_%d" % (sl, s), "krr%d" % b2, "ident"], W=["psT%d" % b2])
                if own:
                    P.add("act", lambda e: e.copy(out=CQT[:, :, s * 128:(s + 1) * 128], in_=psb(b2)[:, 0:2, :]),
                          R=["psT%d" % b2], W=["CQT%d" % s])
                P.add("act", lambda e: e.copy(out=KTL[:, s * 128:(s + 1) * 128], in_=psb(b2)[:, 2, :]),
                      R=["psT%d" % b2], W=["KTL%d_%d" % (sl, s)])
                P.add("act", lambda e: e.copy(out=KTR[0:64, s * 128:(s + 1) * 128], in_=psb(b2)[0:64, 3, :]),
                      R=["psT%d" % b2], W=["KTR%d_%d" % (sl, s)])
            return pp, post, tr

        cqt_keys = ["CQT%d" % s for s in range(4)]

        def qhead(i, h):
            sl = i % 2
            hb = h % 2
            QL, QR, RCt, RSt = QLs[sl], QRs[sl], RC[sl], RS[sl]
            def f_qn(e):
                e.matmul(out=PS[4][:, :], lhsT=WUQ[:, 0, h * 192:h * 192 + 128], rhs=CQT[:, 0, :], start=True, stop=False)
                return e.matmul(out=PS[4][:, :], lhsT=WUQ[:, 1, h * 192:h * 192 + 128], rhs=CQT[:, 1, :], start=False, stop=True)
            P.add("pe", f_qn, R=cqt_keys + ["WUQ"], W=["ps4"])
            P.add("act", lambda e: e.copy(out=qn_sb[hb], in_=PS[4][:, :]), R=["ps4"], W=["qn%d" % hb])
            P.add("pe", lambda e: e.matmul(out=PS[5][:, :], lhsT=WUKT[:, h, :], rhs=qn_sb[hb], start=True, stop=True),
                  R=["qn%d" % hb, "WUKT"], W=["ps5"])
            P.add("act", lambda e: e.activation(out=QL[:, h, :], in_=PS[5][:, :], func=AF.Copy, scale=SCALE),
                  R=["ps5"], W=["QL%d_%d" % (sl, h)])
            def f_qr(e):
                e.matmul(out=PS[6][0:64, :], lhsT=WUQ[:, 0, h * 192 + 128:h * 192 + 192], rhs=CQT[:, 0, :], start=True, stop=False)
                e.matmul(out=PS[6][0:64, :], lhsT=WUQ[:, 1, h * 192 + 128:h * 192 + 192], rhs=CQT[:, 1, :], start=False, stop=True)
                e.matmul(out=PS[7][0:64, :], lhsT=WUQS[:, 0, h * 64:h * 64 + 64], rhs=CQT[:, 0, :], start=True, stop=False)
                return e.matmul(out=PS[7][0:64, :], lhsT=WUQS[:, 1, h * 64:h * 64 + 64], rhs=CQT[:, 1, :], start=False, stop=True)
            P.add("pe", f_qr, R=cqt_keys + ["WUQ", "WUQS"], W=["ps6", "ps7"])
            def f_rq(e):
                e.tensor_tensor(out=qt1[hb][0:64, :], in0=PS[6][0:64, :], in1=RCt[0:64, :], op=ALU.mult)
                e.tensor_tensor(out=qt2[hb][0:64, :], in0=PS[7][0:64, :], in1=RSt[0:64, :], op=ALU.mult)
                return e.tensor_copy(out=dummy[:, 0:1], in_=dummy[:, 1:2])
            P.add("dve", f_rq, R=["ps6", "ps7", "RCS%d" % sl, "dummy"], W=["qt%d" % hb])
            P.add("dve", lambda e: e.tensor_tensor(out=QR[0:64, h, :], in0=qt1[hb][0:64, :], in1=qt2[hb][0:64, :], op=ALU.add),
                  R=["qt%d" % hb], W=["QR%d_%d" % (sl, h)])

        def own_tok0(i):
            if USE_CC:
                return i * 512
            if i < 8:
                return i * 512
            if i >= 32:
                return 4096 + (i - 32) * 512
            return None

        def tile(i):
            sl = i % 2
            UB, VB, KTL, KTR, QL, QR = UBs[sl], VBs[sl], KTLs[sl], KTRs[sl], QLs[sl], QRs[sl]
            parts = [subtile_parts(i, s, own_tok0(i) is not None) for s in range(4)]
            if i + 1 < NTL:
                nsl = (i + 1) % 2
                nstats, nhn = norm_a_split(t, Xb[nsl], gain, hn4, stat[nsl], [t + "X%d" % nsl], junkb)
                nstats()
            else:
                nhn = lambda s: None
            parts[0][0]()
            parts[1][0]()
            parts[0][1]()
            nhn(0)
            parts[2][0]()
            parts[1][1]()
            nhn(1)
            parts[3][0]()
            parts[0][2]()
            parts[2][1]()
            nhn(2)
            parts[1][2]()
            parts[3][1]()
            nhn(3)
            parts[2][2]()
            parts[3][2]()
            tok0 = own_tok0(i)
            if tok0 is not None:
                for h in range(4):
                    qhead(i, h)
                qlv = ql_d.rearrange("(h p) n -> p h n", p=128)[:, :, tok0:tok0 + 512]
                qrv = qr_d.rearrange("(h p) n -> p h n", p=64)[:, :, tok0:tok0 + 512]
                P.add("sp", lambda e: [e.dma_start(out=qlv, in_=QL), e.dma_start(out=qrv, in_=QR[0:64, :, :])],
                      R=["QL%d_%d" % (sl, h) for h in range(4)] + ["QR%d_%d" % (sl, h) for h in range(4)],
                      dma=True, chan="Aq%d" % sl, n=2)
            if USE_CC and i < 8:
                r0 = i * 512
                vv = send[0:1024, :].rearrange("r (q l) -> (r q) l", q=4)
                ktv = send[1024:2560, :].rearrange("(f a) c -> f (a c)", a=8)
                uv = send[2560:6656, :]
            elif USE_CC:
                r0 = (i - 8) * 512
                uv, vv, ktv = u_samp, v_samp, kt_samp
            elif i < 32:
                r0 = i * 512
                uv, vv, ktv = u_p[0], v_p[0], kt_p[0]
            else:
                r0 = (i - 32) * 512
                uv, vv, ktv = u_samp, v_samp, kt_samp
            uo = uv[r0:r0 + 512, :].rearrange("(s p) c -> p s c", p=128)
            vo = vv[r0:r0 + 512, :].rearrange("(s p) c -> p s c", p=128)
            P.add("sp", lambda e: [
                e.dma_start(out=uo, in_=UB), e.dma_start(out=vo, in_=VB),
                e.dma_start(out=ktv[0:128, r0:r0 + 512], in_=KTL),
                e.dma_start(out=ktv[128:192, r0:r0 + 512], in_=KTR[0:64, :])],
                R=["UB%d_%d" % (sl, s) for s in range(4)] + ["VB%d_%d" % (sl, s) for s in range(4)]
                  + ["KTL%d_%d" % (sl, s) for s in range(4)] + ["KTR%d_%d" % (sl, s) for s in range(4)],
                W=["send_p"] if (USE_CC and i < 8) else [], dma=True, chan="Ast%d" % sl, n=4)
            if USE_CC and i == 7:
                P.add("pool", lambda e: [e.collective_compute(
                    "AllGather", ALU.bypass, replica_groups=[list(range(NCORE))],
                    ins=[send.opt()], outs=[gath.opt()])], R=["send_p"], W=["gath"], dma=True, chan="cc_all", cc=True)
            if i + 1 < NTL:
                norm_b(t, hn4, hT)
            if i + 2 < NTL:
                load_x(i + 2)

        norm_a(t, Xb[0], gain, hn4, stat[0], [t + "X0"])
        norm_b(t, hn4, hT)
        for i in range(NTL):
            tile(i)
        P.barrier()

    def fourier1():
        assert not USE_CC
        AR.reset()
        COS = AR.bf(16 * 2048).rearrange("p (a b) -> p a b", a=16)
        SIN = AR.bf(16 * 2048).rearrange("p (a b) -> p a b", a=16)
        U = [AR.bf(16 * 512).rearrange("p (a b) -> p a b", a=16) for _ in range(2)]
        CH = AR.bf(3 * 128).rearrange("p (a b) -> p a b", a=3)
        ATs = [AR.bf(512) for _ in range(2)]
        BTs = [AR.bf(512) for _ in range(2)]
        STG = [[AR.bf(4 * 512).rearrange("p (a b) -> p a b", a=4) for _ in range(2)] for _ in range(2)]
        load_w_kc(COS, dft_cos, 16, "Fcos", "COS")
        load_w_kc(SIN, dft_sin, 16, "Fsin", "SIN")
        P.add("pool", lambda e: [e.dma_start(out=CH, in_=ch_tabs.rearrange("a p c -> p a c"))], W=["CH"], dma=True, chan="Fch")
        NPJ = 8
        NJ = NPJ + 2
        cnt = {"q": 0, "s": 0}

        def load_u(job):
            ub = U[job % 2]
            if job < NPJ:
                v = u_p[0].rearrange("(n2 e) c -> e n2 c", e=8)[job].rearrange("(j p) c -> p j c", p=128)
            else:
                v = u_samp.rearrange("(j p two) c -> two p j c", p=128, two=2)[job - NPJ]
            P.add("sp", lambda e: [e.dma_start(out=ub, in_=v)], W=["U%d" % (job % 2)], dma=True, chan="Fu%d" % (job % 2))

        def blk(job, kb, g, stg, sk):
            ub = U[job % 2]
            q = cnt["q"] % 2
            cnt["q"] += 1
            pe_b, pf_b = 2 + 2 * q, 3 + 2 * q
            def f_ab(e):
                for j in range(16):
                    e.matmul(out=PS[0][:, :], lhsT=ub[:, j, g * 128:(g + 1) * 128],
                             rhs=COS[:, j, kb * 512:(kb + 1) * 512], start=(j == 0), stop=(j == 15))
                r = None
                for j in range(16):
                    r = e.matmul(out=PS[1][:, :], lhsT=ub[:, j, g * 128:(g + 1) * 128],
                                 rhs=SIN[:, j, kb * 512:(kb + 1) * 512], start=(j == 0), stop=(j == 15))
                return r
            P.add("pe", f_ab, R=["U%d" % (job % 2), "COS", "SIN"], W=["ps0", "ps1"])
            P.add("act", lambda e: e.copy(out=ATs[q], in_=PS[0][:, :]), R=["ps0"], W=["AT%d" % q])
            P.add("dve", lambda e: e.tensor_copy(out=BTs[q], in_=PS[1][:, :]), R=["ps1"], W=["BT%d" % q])
            def f_ef(e):
                r = None
                for sub in range(4):
                    cs = slice(sub * 128, (sub + 1) * 128)
                    e.matmul(out=PS[pe_b][:, cs], lhsT=ATs[q][:, cs], rhs=CH[:, 0, :], start=True, stop=False)
                    e.matmul(out=PS[pe_b][:, cs], lhsT=BTs[q][:, cs], rhs=CH[:, 2, :], start=False, stop=True)
                    e.matmul(out=PS[pf_b][:, cs], lhsT=BTs[q][:, cs], rhs=CH[:, 0, :], start=True, stop=False)
                    r = e.matmul(out=PS[pf_b][:, cs], lhsT=ATs[q][:, cs], rhs=CH[:, 1, :], start=False, stop=True)
                return r
            P.add("pe", f_ef, R=["AT%d" % q, "BT%d" % q, "CH"], W=["ps%d" % pe_b, "ps%d" % pf_b])
            P.add("act", lambda e: e.copy(out=stg[0][:, :, g * 128:(g + 1) * 128],
                                          in_=PS[pe_b][:, :].rearrange("p (s c) -> p s c", s=4)),
                  R=["ps%d" % pe_b], W=[sk + "e%d" % g])
            P.add("dve", lambda e: e.tensor_copy(out=stg[1][:, :, g * 128:(g + 1) * 128],
                                                 in_=PS[pf_b][:, :].rearrange("p (s c) -> p s c", s=4)),
                  R=["ps%d" % pf_b], W=[sk + "f%d" % g])

        def kblock(job, kb):
            sl = cnt["s"] % 2
            cnt["s"] += 1
            stg = STG[sl]
            sk = "STG%d" % sl
            for g in range(4):
                blk(job, kb, g, stg, sk)
            dst = ef_t if job < NPJ else efs_t
            n1 = job if job < NPJ else job - NPJ
            r0 = n1 * 4096 + kb * 512
            de = dst[r0:r0 + 512, :].rearrange("(s p) c -> p s c", p=128)
            df = dst[r0 + 2048:r0 + 2048 + 512, :].rearrange("(s p) c -> p s c", p=128)
            P.add("sp", lambda e: [e.dma_start(out=de, in_=stg[0]), e.dma_start(out=df, in_=stg[1])],
                  R=[sk + "e%d" % g for g in range(4)] + [sk + "f%d" % g for g in range(4)],
                  dma=True, chan="Fst%d" % sl, n=2)

        load_u(0)
        load_u(1)
        for job in range(NJ):
            for kb in range(4):
                kblock(job, kb)
            if job + 2 < NJ:
                load_u(job + 2)
        P.barrier()

    def fourier2():
        AR.reset()
        ET = [[AR.bf(4 * 512).rearrange("p (a b) -> p a b", a=4) for _ in range(16)] for _ in range(2)]
        TWs = [AR.f32(4 * 16).rearrange("p (a b) -> p a b", a=4) for _ in range(2)]
        DG = [AR.bf(4 * 16 * 128).rearrange("p (a b c) -> p a b c", a=4, b=16) for _ in range(2)]
        ob = [AR.bf(512) for _ in range(2)]
        cnt = {"d": 0, "o": 0, "l": 0}
        engs = ("act", "dve", "pool")

        def seg_cg(seg, cg, nj, src_t, esl):
            tok0 = seg * 2048 + cg * 512
            dsl = cnt["d"] % 2
            cnt["d"] += 1
            tws, dg = TWs[dsl], DG[dsl]
            tv = twt[seg, cg * 512:(cg + 1) * 512, :].rearrange("(ch p) j -> p ch j", p=128)
            P.add("sp", lambda e: [e.dma_start(out=tws, in_=tv)], W=["TW%d" % dsl], dma=True, chan="Gt%d" % dsl)
            def mk_diag(ch):
                idb = ident[:, :].unsqueeze(1).to_broadcast([128, nj, 128])
                twb = tws[:, ch, 0:nj].unsqueeze(2).to_broadcast([128, nj, 128])
                P.add("dve", lambda e: e.tensor_tensor(out=dg[:, ch, 0:nj, :], in0=idb, in1=twb, op=ALU.mult),
                      R=["TW%d" % dsl, "ident"], W=["DG%d_0" % dsl])
            for ch in range(4):
                mk_diag(ch)
            dkeys = ["DG%d_0" % dsl]
            def one_g(g):
                osl = cnt["o"] % 2
                cnt["o"] += 1
                bank = 2 + osl
                def f_mm(e):
                    r = None
                    for ch in range(4):
                        for j in range(nj):
                            r = e.matmul(out=PS[bank][:, ch * 128:(ch + 1) * 128],
                                         lhsT=ET[esl][j][:, ch, g * 128:(g + 1) * 128], rhs=dg[:, ch, j, :],
                                         start=(j == 0), stop=(j == nj - 1))
                    return r
                P.add("pe", f_mm, R=["ET%d" % esl] + dkeys, W=["ps%d" % bank])
                o = ob[osl]
                P.add("act", lambda e: e.copy(out=o, in_=PS[bank][:, :]), R=["ps%d" % bank], W=["ob%d" % osl])
                P.add("sp", lambda e: [e.dma_start(out=four_d[g * 128:(g + 1) * 128, tok0:tok0 + 512], in_=o)],
                      R=["ob%d" % osl], dma=True, chan="Gs%d" % osl)
            for g in range(4):
                one_g(g)

        def load_et(cg, nj, src_t):
            esl = cnt["l"] % 2
            cnt["l"] += 1
            def f(e):
                r = []
                for j in range(nj):
                    n1, ef = j // 2, j % 2
                    r0 = n1 * 4096 + ef * 2048 + cg * 512
                    r.append(e.dma_start(out=ET[esl][j], in_=src_t[r0:r0 + 512, :].rearrange("(ch p) c -> p ch c", p=128)))
                return r
            P.add("sp", f, W=["ET%d" % esl], dma=True, chan="Gl%d" % esl, n=nj)
            return esl

        work = [(cg, 16, ef_t, (0, 1)) for cg in range(4)] + [(cg, 4, efs_t, (2, 3)) for cg in range(4)]
        esl_next = load_et(*work[0][:3])
        for wi, (cg, nj, src_t, segs) in enumerate(work):
            esl = esl_next
            if wi + 1 < len(work):
                esl_next = load_et(*work[wi + 1][:3])
            for seg in segs:
                seg_cg(seg, cg, nj, src_t, esl)
        P.barrier()

    def attn_phase():
        AR.reset()
        KTL = AR.bf(16384)
        KTR = AR.bf(16384)
        V = AR.bf(128 * 128).rearrange("p (a b) -> p a b", a=128)
        QLs = [AR.bf(4 * 512).rearrange("p (a b) -> p a b", a=4) for _ in range(2)]
        QRs = [AR.bf(4 * 512).rearrange("p (a b) -> p a b", a=4) for _ in range(2)]
        pTp = [AR.bf(2 * 512).rearrange("p (a b) -> p a b", a=2) for _ in range(3)]
        pT = [pTp[i // 2][:, i % 2, :] for i in range(6)]
        sacc2w = [AR.f32(2 * 512).rearrange("p (a b) -> p a b", a=2) for _ in range(2)]
        sacc = [AR.f32(512) for _ in range(2)]
        saccb = [AR.bf(512) for _ in range(2)]
        slo = [AR.bf(512) for _ in range(2)]
        rinv = [AR.f32(512) for _ in range(2)]
        olat = [AR.bf(512) for _ in range(2)]
        mixTs = [AR.bf(8 * 512).rearrange("p (a b) -> p a b", a=8) for _ in range(2)]
        WO = AR.bf(8 * D).rearrange("p (a b) -> p a b", a=8)
        WUV = AR.bf(4 * 128).rearrange("p (a b) -> p a b", a=4)
        X1s = [AR.f32(4 * D).rearrange("p (a b) -> p a b", a=4) for _ in range(2)]
        load_w_kc(WO, w_o, 8, "Bwo", "WO")
        ukv = w_ukv.rearrange("l (h two n) -> l h two n", h=4, two=2)
        P.add("pool", lambda e: [e.dma_start(out=WUV, in_=ukv[:, :, 1, :])], W=["WUV"], dma=True, chan="Bwuv")
        st = {"hc": 0, "tc": 0}
        P.add("dve", lambda e: e.memset(KTR[64:128, :], 0.0), W=["KVz"])
        P.add("dve", lambda e: e.memset(QRs[0][64:128, :, :], 0.0), W=["Qz0"])
        P.add("dve", lambda e: e.memset(QRs[1][64:128, :, :], 0.0), W=["Qz1"])

        def load_kv(seq):
            if seq == "s":
                P.add("sp", lambda e: [e.dma_start(out=KTL[:, 0:4096], in_=kt_samp[0:128, :]),
                                       e.dma_start(out=KTR[0:64, 0:4096], in_=kt_samp[128:192, :]),
                                       e.dma_start(out=V[:, 0:32, :], in_=v_samp.rearrange("(j p) l -> p j l", p=128))],
                      W=["KV"], dma=True, chan="Bkv", n=3)
            else:
                b = seq
                def f_kv(e):
                    r = []
                    for rr in range(8):
                        if USE_CC:
                            gk = gath[rr * SR + 1024:rr * SR + 2560, :].rearrange("(f a) c -> f (a c)", a=8)
                            gvv = gath[rr * SR:rr * SR + 1024, :].rearrange("r (q l) -> (r q) l", q=4)
                            r.append(e.dma_start(out=KTL[:, rr * 2048:(rr + 1) * 2048], in_=gk[0:128, b * 2048:(b + 1) * 2048]))
                            r.append(e.dma_start(out=KTR[0:64, rr * 2048:(rr + 1) * 2048], in_=gk[128:192, b * 2048:(b + 1) * 2048]))
                            r.append(e.dma_start(out=V[:, rr * 16:(rr + 1) * 16, :],
                                                 in_=gvv[b * 2048:(b + 1) * 2048, :].rearrange("(j p) l -> p j l", p=128)))
                            continue
                        r.append(e.dma_start(out=KTL[:, rr * 2048:(rr + 1) * 2048], in_=kt_p[b][0:128, rr * 2048:(rr + 1) * 2048]))
                        r.append(e.dma_start(out=KTR[0:64, rr * 2048:(rr + 1) * 2048], in_=kt_p[b][128:192, rr * 2048:(rr + 1) * 2048]))
                        r.append(e.dma_start(out=V[:, rr * 16:(rr + 1) * 16, :],
                                             in_=v_p[b][rr * 2048:(rr + 1) * 2048, :].rearrange("(j p) l -> p j l", p=128)))
                    return r
                P.add("sp", f_kv, W=["KV"], dma=True, chan="Bkv", n=24)

        def load_blk(k, tok0):
            bs = k % 2
            QL, QR, X1, mixT = QLs[bs], QRs[bs], X1s[bs], mixTs[bs]
            qlv = ql_d.rearrange("(h p) n -> p h n", p=128)[:, :, tok0:tok0 + 512]
            qrv = qr_d.rearrange("(h p) n -> p h n", p=64)[:, :, tok0:tok0 + 512]
            P.add("sp", lambda e: [e.dma_start(out=QL, in_=qlv), e.dma_start(out=QR[0:64, :, :], in_=qrv)],
                  W=["Q%d" % bs], dma=True, chan="Bq%d" % bs, n=2)
            lt = tok0 if (USE_CC or tok0 < 4096) else 16384 + tok0 - 4096
            x1v = x1_d[lt:lt + 512, :].rearrange("(s p) d -> p s d", p=128)
            P.add("sp", lambda e: [e.dma_start(out=X1, in_=x1v)], W=["X1_%d" % bs], dma=True, chan="Bx%d" % bs)
            fv = four_d.rearrange("(g p) n -> p g n", p=128)[:, :, tok0:tok0 + 512]
            P.add("sp", lambda e: [e.dma_start(out=mixT[:, 4:8, :], in_=fv)], W=["mixF%d" % bs], dma=True, chan="Bf%d" % bs)

        pend = []

        def flush(upto):
            while pend and pend[0][0] <= upto:
                pend.pop(0)[1]()

        def head(k, h, nk):
            bs = k % 2
            QL, QR, mixT = QLs[bs], QRs[bs], mixTs[bs]
            hs = st["hc"] % 2
            st["hc"] += 1
            po_b = 3 + hs
            SB = (0, 1, 2)
            def qk(kt):
                sb = SB[kt % 3]
                def f(e):
                    e.matmul(out=PS[sb][:, :], lhsT=KTL[:, kt * 128:(kt + 1) * 128], rhs=QL[:, h, :], start=True, stop=False)
                    return e.matmul(out=PS[sb][:, :], lhsT=KTR[:, kt * 128:(kt + 1) * 128], rhs=QR[:, h, :],
                                    start=False, stop=True)
                P.add("pe", f, R=["KV", "Q%d" % bs, "KVz", "Qz%d" % bs], W=["ps%d" % sb])
            def rest(kt):
                sb = SB[kt % 3]
                ps_ = st["tc"] % 6
                st["tc"] += 1
                P.add("act", lambda e: e.activation(out=pT[ps_], in_=PS[sb][:, :], func=AF.Exp),
                      R=["ps%d" % sb], W=["pT%d" % ps_])
                if kt % 2 == 1:
                    pr = ps_ // 2
                    w = sacc2w[hs]
                    if kt == 1:
                        P.add("dve", lambda e: e.tensor_copy(out=w, in_=pTp[pr]),
                              R=["pT%d" % (ps_ - 1), "pT%d" % ps_], W=["sacc%d" % hs])
                    else:
                        P.add("dve", lambda e: e.tensor_tensor(out=w, in0=w, in1=pTp[pr], op=ALU.add),
                              R=["pT%d" % (ps_ - 1), "pT%d" % ps_, "sacc%d" % hs], W=["sacc%d" % hs])
                P.add("pe", lambda e: e.matmul(out=PS[po_b][:, :], lhsT=V[:, kt, :], rhs=pT[ps_],
                                              start=(kt == 0), stop=(kt == nk - 1)),
                      R=["pT%d" % ps_, "KV"], W=["ps%d" % po_b])
            qk(0)
            qk(1)
            for kt in range(nk):
                if kt + 2 < nk:
                    qk(kt + 2)
                rest(kt)
                flush(kt)
            flush(10 ** 9)
            P.add("dve", lambda e: e.tensor_tensor(out=sacc[hs], in0=sacc2w[hs][:, 0, :], in1=sacc2w[hs][:, 1, :], op=ALU.add),
                  R=["sacc%d" % hs], W=["sacc%d" % hs])
            P.add("dve", lambda e: e.tensor_copy(out=saccb[hs], in_=sacc[hs]), R=["sacc%d" % hs], W=["saccb%d" % hs])
            P.add("dve", lambda e: e.tensor_tensor(out=slo[hs], in0=sacc[hs], in1=saccb[hs], op=ALU.subtract),
                  R=["saccb%d" % hs, "sacc%d" % hs], W=["slo%d" % hs])
            def step_a():
                def f_sum(e):
                    e.matmul(out=PS[5][:, :], lhsT=ones[:, :], rhs=saccb[hs], start=True, stop=False)
                    return e.matmul(out=PS[5][:, :], lhsT=ones[:, :], rhs=slo[hs], start=False, stop=True)
                P.add("pe", f_sum, R=["saccb%d" % hs, "slo%d" % hs, "ones"], W=["ps5"])
                P.add("dve", lambda e: e.reciprocal(out=rinv[hs], in_=PS[5][:, :]), R=["ps5"], W=["rinv%d" % hs])
                P.add("dve", lambda e: e.tensor_tensor(out=olat[hs], in0=PS[po_b][:, :], in1=rinv[hs], op=ALU.mult),
                      R=["ps%d" % po_b, "rinv%d" % hs], W=["olat%d" % hs])
            def step_b():
                P.add("pe", lambda e: e.matmul(out=PS[5][:, :], lhsT=WUV[:, h, :], rhs=olat[hs], start=True, stop=True),
                      R=["olat%d" % hs, "WUV"], W=["ps5"])
                P.add("act", lambda e: e.copy(out=mixT[:, h, :], in_=PS[5][:, :]), R=["ps5"], W=["mix%d_%d" % (bs, h)])
            pend.append((3, step_a))
            pend.append((9, step_b))

        def wo(k, tok0):
            bs = k % 2
            X1, mixT = X1s[bs], mixTs[bs]
            def one(s, half):
                bank = 6 + half
                def f_wo(e):
                    r = None
                    for ch in range(8):
                        r = e.matmul(out=PS[bank][:, :], lhsT=mixT[:, ch, s * 128:(s + 1) * 128],
                                     rhs=WO[:, ch, half * 512:(half + 1) * 512], start=(ch == 0), stop=(ch == 7))
                    return r
                P.add("pe", f_wo, R=["mix%d_%d" % (bs, h) for h in range(4)] + ["mixF%d" % bs, "WO"], W=["ps%d" % bank])
                P.add("dve", lambda e: e.tensor_tensor(
                    out=X1[:, s, half * 512:(half + 1) * 512], in0=PS[bank][:, :],
                    in1=X1[:, s, half * 512:(half + 1) * 512], op=ALU.add),
                    R=["ps%d" % bank, "X1_%d" % bs], W=["X1_%d" % bs])
            kt0 = 12
            for s in range(4):
                for half in range(2):
                    pend.append((kt0, (lambda s=s, half=half: one(s, half))))
                    kt0 += 2
            x2v = x2_d[tok0:tok0 + 512, :].rearrange("(s p) d -> p s d", p=128)
            pend.append((kt0, lambda: P.add("sp", lambda e: [e.dma_start(out=x2v, in_=X1)], R=["X1_%d" % bs], dma=True,
                                            chan="Bxs%d" % bs)))

        blocks = []
        for seq in (("s", 0, 1) if USE_CC else ("s", 0)):
            if seq == "s":
                for j in range(8):
                    blocks.append((seq, 4096 + 512 * j, 32))
            elif USE_CC:
                for j in range(4):
                    blocks.append((seq, seq * 2048 + 512 * j, 128))
            else:
                for j in range(8):
                    blocks.append((seq, 512 * j, 128))
        cur = None
        load_blk(0, blocks[0][1])
        for k, (seq, tok0, nk) in enumerate(blocks):
            if seq != cur:
                load_kv(seq)
                cur = seq
            head(k, 0, nk)
            if k + 1 < len(blocks):
                load_blk(k + 1, blocks[k + 1][1])
            for h in range(1, 4):
                head(k, h, nk)
            wo(k, tok0)
        flush(10 ** 9)
        P.barrier()

    import os
    stop = int(os.environ.get("MK_STOP", "99"))
    if stop >= 1:
        ffn_phase("F1", xin, x1_d, w1_gate, w1_up, w1_down, g_ffn1, None, NTL)
    if stop >= 2:
        proj_phase()
    if stop >= 3:
        fourier1()
    if stop >= 4:
        fourier2()
    if stop >= 5:
        attn_phase()
    if stop >= 6:
        ffn_phase("F2", x2_d, yout, w2_gate, w2_up, w2_down, g_ffn2, g_final, NT)
    P.barrier()
    P.emit()
    es.close()
    return nc


_CACHE = {}


def _tables(c):
    f32 = np.float32
    inv = (1.0 / (np.float32(10000.0) ** (np.arange(0, 64, 2, dtype=f32) / f32(64)))).astype(f32)
    if USE_CC:
        ppos = 2048 * c + np.arange(2048)
    else:
        t = np.arange(16384)
        ppos = 4096 * (((t // 4096) + (c % 4)) % 4) + (t % 4096)
    pos = (np.concatenate([ppos, ppos, np.arange(4096)]) if USE_CC else np.concatenate([ppos, np.arange(4096)])).astype(f32)
    ang = (pos[:, None] * inv[None, :]).astype(f32)
    cs, sn = np.cos(ang).astype(f32), np.sin(ang).astype(f32)
    rope_tok = np.concatenate([cs, cs, -sn, sn], axis=1).astype(f32)
    rope_cos = (np.concatenate([cs, cs], axis=1).T * f32(SCALE)).astype(f32)
    rope_sin = (np.concatenate([-sn, sn], axis=1).T * f32(SCALE)).astype(f32)
    k2 = np.arange(2048)
    tw = np.zeros((20, 2, 2048), np.float64)
    normP = 1.0 / np.sqrt(16384.0 * 128.0)
    normS = 1.0 / np.sqrt(4096.0 * 128.0)
    for b in range(2):
        for n1 in range(8):
            k = (2048 * c + k2) if USE_CC else (4096 * (c % 4) + 2048 * b + k2)
            a = 2 * np.pi * (((n1 * k) % 16384) / 16384.0 + (0.0 if USE_CC else (((c % 4) * k) % 4) / 4.0))
            tw[b * 8 + n1, 0] = np.cos(a) * normP
            tw[b * 8 + n1, 1] = -np.sin(a) * normP
    for hh in range(2):
        for n1 in range(2):
            k = 2048 * hh + k2
            a = 2 * np.pi * ((n1 * k) % 4096) / 4096.0
            tw[16 + hh * 2 + n1, 0] = np.cos(a) * normS
            tw[16 + hh * 2 + n1, 1] = -np.sin(a) * normS
    twt = np.zeros((4, 2048, 16), np.float64)
    for seg in range(2):
        twt[seg] = tw[seg * 8:(seg + 1) * 8].transpose(2, 0, 1).reshape(2048, 16)
    for hh in range(2):
        twt[2 + hh, :, 0:4] = tw[16 + hh * 2:16 + hh * 2 + 2].transpose(2, 0, 1).reshape(2048, 4)
    return rope_tok, np.ascontiguousarray(rope_cos), np.ascontiguousarray(rope_sin), tw.astype(f32), twt.astype(f32)


def _const_tables():
    n = np.arange(2048)
    m = (n[:, None] * n[None, :]) % 2048
    a = 2 * np.pi * m / 2048.0
    dc, ds = np.cos(a).astype(np.float32), np.sin(a).astype(np.float32)
    c = np.arange(128)
    ac = 2 * np.pi * ((c[:, None] * c[None, :]) % 128) / 128.0
    ch = np.stack([np.cos(ac), np.sin(ac), -np.sin(ac)]).astype(np.float32)
    return dc, ds, ch, np.eye(128, dtype=np.float32)


def kernel(**inputs):
    if "nc" not in _CACHE:
        _CACHE["nc"] = build_program()
        _CACHE["const"] = _const_tables()
    nc = _CACHE["nc"]
    dc, ds, ch, ident = _CACHE["const"]
    f = lambda k: np.ascontiguousarray(np.asarray(inputs[k], dtype=np.float32))
    xp, xs = f("x_prompt"), f("x_sample")
    shared = {
        "g_ffn1": f("g_ffn1")[0], "g_mix": f("g_mix")[0], "g_ffn2": f("g_ffn2")[0], "g_final": f("g_final"),
        "g_q": f("g_q")[0], "g_kv": f("g_kv")[0],
        "w_uq": f("w_uq")[0], "w_ukv": f("w_ukv")[0], "ch_tabs": ch, "ident": ident,
    }
    for k in ("w1_gate", "w1_up", "w1_down", "w2_gate", "w2_up", "w2_down", "w_in", "w_o"):
        shared[k] = f(k)[0]
    shared["dft_cos"], shared["dft_sin"] = dc, ds
    in_maps = []
    for c in range(NCORE):
        rt, rc, rs, tw, twt = _tables(c)
        m = dict(shared)
        if USE_CC:
            p0, p1 = xp[0, 2048 * c:2048 * (c + 1)], xp[1, 2048 * c:2048 * (c + 1)]
        else:
            p0 = np.roll(xp[c // 4].reshape(4, 4096, D), -(c % 4), axis=0).reshape(16384, D)
            p1 = None
        m["xin"] = np.ascontiguousarray(np.concatenate([p0, xs[c]] if p1 is None else [p0, p1, xs[c]], axis=0))
        m["rope_tok"], m["rope_cos"], m["rope_sin"], m["tw"], m["twt"] = rt, rc, rs, tw, twt
        in_maps.append(m)
    res = run_bass_kernel_spmd(nc, in_maps, core_ids=list(range(NCORE)))
    yp = np.empty((2, 16384, D), np.float32)
    ys = np.empty((8, 4096, D), np.float32)
    for c in range(NCORE):
        y = np.asarray(res.results[c]["yout"])
        if USE_CC:
            yp[0, 2048 * c:2048 * (c + 1)] = y[0:2048]
            yp[1, 2048 * c:2048 * (c + 1)] = y[2048:4096]
        else:
            yp[c // 4, 4096 * (c % 4):4096 * (c % 4 + 1)] = y[0:4096]
        ys[c] = y[4096:8192]
    return (yp, ys)
```

```python
import numpy as np
from contextlib import ExitStack

import concourse.bass as bass
import concourse.mybir as mybir
from concourse.bass_utils import run_bass_kernel_spmd

F32 = mybir.dt.float32
BF16 = mybir.dt.bfloat16
ALU = mybir.AluOpType
AF = mybir.ActivationFunctionType

D = 1024
DFF = 2816
NFC = 22
NTOK = 8192
NT = 16
USE_CC = False
NLOC = 8192 if USE_CC else 20480
NTL = NLOC // 512
NCORE = 8
EPS = 1e-6
SCALE = 192 ** -0.5


class Op:
    __slots__ = ("eng", "fn", "deps", "sig", "dma", "chan", "n", "has_dep", "cc")


class Prog:
    ENGS = ("pe", "act", "dve", "pool", "sp")

    def __init__(self, nc):
        self.nc = nc
        self.ops = []
        self.lw = {}
        self.rd = {}
        self.pending_dma = []
        self.last_compute = {}
        self.init_hooks = {}

    def add(self, eng, fn, R=(), W=(), dma=False, chan=None, n=1, cc=False):
        op = Op()
        op.eng, op.fn, op.dma, op.chan, op.n = eng, fn, dma, chan, n
        op.cc = cc
        op.deps, op.sig, op.has_dep = set(), None, False
        for r in R:
            w = self.lw.get(r)
            if w is not None:
                self._dep(op, w, True)
        for x in W:
            w = self.lw.get(x)
            if w is not None:
                self._dep(op, w, False)
            for r in self.rd.get(x, {}).values():
                self._dep(op, r, False)
        for r in R:
            k = ("d", len(self.ops)) if dma else eng
            self.rd.setdefault(r, {})[k] = op
        for x in W:
            self.lw[x] = op
            self.rd[x] = {}
        self.ops.append(op)
        if dma:
            self.pending_dma.append(op)
        else:
            self.last_compute[eng] = op
        return op

    def _dep(self, op, d, raw):
        if d is op:
            return
        if (not op.dma) and (not d.dma) and op.eng == d.eng:
            if (not raw) or op.eng == "pe":
                return
        op.deps.add(d)
        d.has_dep = True

    def barrier(self):
        targets = list(self.last_compute.values()) + list(self.pending_dma)
        for e in self.ENGS:
            b = Op()
            b.eng, b.fn, b.dma, b.chan, b.n = e, None, False, None, 1
            b.cc = False
            b.deps, b.sig, b.has_dep = set(), None, False
            for t in targets:
                if (not t.dma) and t.eng == e:
                    continue
                b.deps.add(t)
                t.has_dep = True
            self.ops.append(b)
        self.lw.clear()
        self.rd.clear()
        self.pending_dma = []

    def emit(self):
        nc = self.nc
        cnt = {}
        for o in self.ops:
            if not o.has_dep:
                continue
            key = ("c", o.chan) if o.dma else ("e", o.eng)
            cnt[key] = cnt.get(key, 0) + ((1 if o.cc else 16 * o.n) if o.dma else 1)
            o.sig = (key, cnt[key])
        with ExitStack() as es:
            sems = {}
            for i, key in enumerate(cnt):
                sems[key] = es.enter_context(nc.semaphore("s%d" % i))
            block = es.enter_context(nc.Block())
            decos = {"pe": block.tensor, "act": block.scalar, "dve": block.vector,
                     "pool": block.gpsimd, "sp": block.sync}
            for e in self.ENGS:
                ops_e = [o for o in self.ops if o.eng == e]

                def body(eng, ops_e=ops_e, e=e):
                    waited = {}
                    if e in self.init_hooks:
                        self.init_hooks[e](eng)
                    for o in ops_e:
                        need = {}
                        for d in o.deps:
                            k, v = d.sig
                            if need.get(k, 0) < v:
                                need[k] = v
                        for k, v in need.items():
                            if waited.get(k, 0) < v:
                                eng.wait_ge(sems[k], v)
                                waited[k] = v
                        if o.fn is None:
                            continue
                        r = o.fn(eng)
                        if o.sig is not None:
                            if o.cc:
                                r[0].then_inc(sems[o.sig[0]])
                            elif o.dma:
                                assert len(r) == o.n
                                for ins in r:
                                    ins.then_inc(sems[o.sig[0]], 16)
                            else:
                                r.then_inc(sems[o.sig[0]], 1)

                decos[e](body)


class Arena:
    def __init__(self, ap, nbytes):
        self.ap = ap
        self.cap = nbytes // 2
        self.off = 0

    def reset(self):
        self.off = 0

    def bf(self, n):
        n2 = (n + 31) // 32 * 32
        assert self.off + n2 <= self.cap, ("arena overflow", self.off, n2, self.cap)
        v = self.ap[:, self.off:self.off + n]
        self.off += n2
        return v

    def f32(self, n):
        return self.bf(2 * n).bitcast(F32)


def build_program():
    nc = bass.Bass("TRN2", target_bir_lowering=False)

    def din(name, shape):
        return nc.dram_tensor(name, shape, F32, kind="ExternalInput").ap()

    xin = din("xin", [NLOC, D])
    g_ffn1 = din("g_ffn1", [D]); g_mix = din("g_mix", [D]); g_ffn2 = din("g_ffn2", [D])
    g_final = din("g_final", [D]); g_q = din("g_q", [256]); g_kv = din("g_kv", [128])
    w_uq = din("w_uq", [256, 768]); w_ukv = din("w_ukv", [128, 1024])
    BIGW = {"w1_gate": (D, DFF), "w1_up": (D, DFF), "w1_down": (DFF, D),
            "w2_gate": (D, DFF), "w2_up": (D, DFF), "w2_down": (DFF, D),
            "w_in": (D, 960), "w_o": (D, D), "dft_cos": (2048, 2048), "dft_sin": (2048, 2048)}
    wfull = {k: din(k, [r, c]) for k, (r, c) in BIGW.items()}
    rope_tok = din("rope_tok", [NLOC, 128])
    rope_cos = din("rope_cos", [64, NLOC]); rope_sin = din("rope_sin", [64, NLOC])
    ch_tabs = din("ch_tabs", [3, 128, 128])
    tw = din("tw", [20, 2, 2048])
    twt = din("twt", [4, 2048, 16])
    ident_in = din("ident", [128, 128])
    yout = nc.dram_tensor("yout", [NTOK, D], F32, kind="ExternalOutput").ap()

    def dscr(name, shape, dt):
        return nc.dram_tensor(name, shape, dt).ap()

    x1_d = dscr("x1_d", [NLOC, D], F32)
    x2_d = dscr("x2_d", [NTOK, D], F32)
    ql_d = dscr("ql_d", [4 * 128, NTOK], BF16)
    qr_d = dscr("qr_d", [4 * 64, NTOK], BF16)
    v_p = [dscr("v_p%d" % b, [16384, 128], BF16) for b in range(2)]
    kt_p = [dscr("kt_p%d" % b, [192, 16384], BF16) for b in range(2)]
    u_p = [dscr("u_p%d" % b, [16384, 512], BF16) for b in range(2)]
    SR = 6656
    send = dscr("send", [SR, 512], BF16)
    gath = dscr("gath", [8 * SR, 512], BF16)
    v_samp = dscr("v_samp", [4096, 128], BF16)
    kt_samp = dscr("kt_samp", [192, 4096], BF16)
    u_samp = dscr("u_samp", [4096, 512], BF16)
    ef_g = dscr("ef_g", [8 * 2048, 2048], BF16)
    ef_samp = dscr("ef_samp", [2048, 2048], BF16)
    ef_t = dscr("ef_t", [8 * 2 * 2048, 512], BF16)
    efs_t = dscr("efs_t", [2 * 2 * 2048, 512], BF16)
    four_d = dscr("four_d", [512, NTOK], BF16)

    P = Prog(nc)
    if USE_CC:
        P.init_hooks['pool'] = lambda eng: get_pid()
    es = ExitStack()
    ARENA_BYTES = 206 * 1024
    arena_t = es.enter_context(nc.sbuf_tensor("arena", [128, ARENA_BYTES // 2], BF16))
    AR = Arena(arena_t, ARENA_BYTES)
    ident = es.enter_context(nc.sbuf_tensor("identb", [128, 128], BF16))
    ones = es.enter_context(nc.sbuf_tensor("onesb", [128, 128], BF16))
    dummy = es.enter_context(nc.sbuf_tensor("dummyt", [128, 8], F32))
    epst = es.enter_context(nc.sbuf_tensor("epst", [128, 1], F32))
    PS = [es.enter_context(nc.psum_tensor("ps%d" % i, [128, 512], F32)) for i in range(8)]

    def psb(i):
        return PS[i][:, :].bitcast(BF16).rearrange("p (a b) -> p a b", a=8)

    pidc = {}

    def get_pid():
        if 'v' not in pidc:
            pidc['v'] = nc.partition_id([mybir.EngineType.Pool])
        return pidc['v']

    P.add("pool", lambda e: [e.dma_start(out=ident[:, :], in_=ident_in)], W=["ident"], dma=True, chan="c_ident")
    P.add("dve", lambda e: e.memset(ones[:, :], 1.0), W=["ones"])
    P.add("dve", lambda e: e.memset(epst[:, :], EPS), W=["eps"])
    P.add("dve", lambda e: e.memset(dummy[:, :], 0.0), W=["dummy"])

    wbf = {}
    def precast(k):
        r, c = BIGW[k]
        wb = dscr(k + "_bf", [r, c], BF16)
        wbf[k] = wb
        P.add("pool", lambda e: [e.dma_start(out=wb, in_=wfull[k])], W=[k + "_bf"], dma=True, chan="pc_" + k)
    PRECAST = ("w_in", "dft_cos", "dft_sin", "w_o", "w2_gate", "w2_up", "w2_down")
    w1_gate, w1_up, w1_down = wfull["w1_gate"], wfull["w1_up"], wfull["w1_down"]
    w2_gate = w2_up = w2_down = w_in = w_o = dft_cos = dft_sin = None

    def load_w_kc(dst3, src2, kchunks, chan, key, eng="pool", rkeys=()):
        v = src2.rearrange("(kc p) f -> p kc f", p=128)
        def fn(e):
            return [e.dma_start(out=dst3[:, kc, :], in_=v[:, kc, :]) for kc in range(kchunks)]
        if src2.dtype == BF16:
            eng = "sp"
        P.add(eng, fn, R=list(rkeys), W=[key], dma=True, chan=chan, n=kchunks)

    def norm_a(tag, X, gain, hn4, stat, src_keys):
        ss, rt, rs = stat
        def f_stats(e):
            for s in range(4):
                e.scalar_tensor_tensor(out=hn4[:, s, :], in0=X[:, s, :], scalar=1.0 / D, in1=X[:, s, :],
                                       op0=ALU.mult, op1=ALU.mult, accum_out=ss[:, s:s + 1])
            return e.tensor_copy(out=dummy[:, 0:1], in_=dummy[:, 1:2])
        P.add("dve", f_stats, R=src_keys + ["dummy"], W=[tag + "ss"] + [tag + "hn%d" % s for s in range(4)])
        P.add("act", lambda e: e.activation(out=rt, in_=ss, func=AF.Sqrt, bias=epst[:, 0:1], scale=1.0),
              R=[tag + "ss", "eps"], W=[tag + "rt"])
        P.add("dve", lambda e: e.reciprocal(out=rs, in_=rt), R=[tag + "rt"], W=[tag + "rs"])
        def one(s):
            P.add("dve", lambda e: e.scalar_tensor_tensor(
                out=hn4[:, s, :], in0=X[:, s, :], scalar=rs[:, s:s + 1], in1=gain,
                op0=ALU.mult, op1=ALU.mult), R=src_keys + [tag + "rs", tag + "gain"], W=[tag + "hn%d" % s])
        for s in range(4):
            one(s)

    def norm_a_split(tag, X, gain, hn4, stat, src_keys, junkb):
        ss, rt, rs = stat
        def stats():
            def f_stats(e):
                for s in range(4):
                    e.activation(out=junkb, in_=X[:, s, :], func=AF.Square, scale=1.0 / 32.0, accum_out=ss[:, s:s + 1])
                return e.activation(out=dummy[:, 2:3], in_=dummy[:, 3:4], func=AF.Copy)
            P.add("act", f_stats, R=src_keys + ["dummy"], W=[tag + "ss", tag + "junkb"])
            P.add("act", lambda e: e.activation(out=rt, in_=ss, func=AF.Sqrt, bias=epst[:, 0:1], scale=1.0),
                  R=[tag + "ss", "eps"], W=[tag + "rt"])
            P.add("dve", lambda e: e.reciprocal(out=rs, in_=rt), R=[tag + "rt"], W=[tag + "rs"])
        def hn(s):
            P.add("dve", lambda e: e.scalar_tensor_tensor(
                out=hn4[:, s, :], in0=X[:, s, :], scalar=rs[:, s:s + 1], in1=gain,
                op0=ALU.mult, op1=ALU.mult), R=src_keys + [tag + "rs", tag + "gain"], W=[tag + "hn%d" % s])
        return stats, hn

    def norm_b(tag, hn4, hT):
        def one(s):
            tb = s % 2
            def f_tr(e):
                r = None
                for kc in range(8):
                    r = e.transpose(out=psb(tb)[:, kc, :], in_=hn4[:, s, kc * 128:(kc + 1) * 128],
                                    identity=ident[:, :])
                return r
            P.add("pe", f_tr, R=[tag + "hn%d" % s, "ident"], W=["psT%d" % tb])
            P.add("act", lambda e: e.copy(out=hT[:, :, s * 128:(s + 1) * 128], in_=psb(tb)),
                  R=["psT%d" % tb], W=[tag + "hT%d" % s])
        for s in range(4):
            one(s)

    def ffn_phase(tag, src, dst, wg, wu, wd, g_in, g_fin, ntiles):
        AR.reset()
        t = tag
        WG = AR.bf(8 * DFF).rearrange("p (a b) -> p a b", a=8)
        WU = AR.bf(8 * DFF).rearrange("p (a b) -> p a b", a=8)
        WD = AR.bf(NFC * D).rearrange("p (a b) -> p a b", a=NFC)
        XP = [AR.f32(D) for _ in range(2)]
        XR = [AR.f32(D) for _ in range(2)]
        hn4 = AR.bf(4 * D).rearrange("p (a b) -> p a b", a=4)
        hT = AR.bf(8 * 512).rearrange("p (a b) -> p a b", a=8)
        actT = AR.bf(NFC * 512).rearrange("p (a b) -> p a b", a=NFC)
        sg = [AR.f32(512) for _ in range(2)]
        junk = AR.bf(D)
        gain = AR.f32(D)
        gfin = AR.f32(D) if g_fin is not None else None
        stp = [(AR.f32(1), AR.f32(1), AR.f32(1)) for _ in range(2)]
        stf = [(AR.f32(1), AR.f32(1), AR.f32(1)) for _ in range(2)]
        P.add("sp", lambda e: [e.dma_start(out=gain, in_=g_in.partition_broadcast(128))],
              W=[t + "gain"], dma=True, chan=t + "gain")
        if g_fin is not None:
            P.add("sp", lambda e: [e.dma_start(out=gfin, in_=g_fin.partition_broadcast(128))],
                  W=[t + "gfin"], dma=True, chan=t + "gfin")
        cnt = {"p": 0, "r": 0}

        def prep_sub(i, s):
            sl = cnt["p"] % 2
            cnt["p"] += 1
            xp = XP[sl]
            ss, rt, rs = stp[sl]
            r0 = i * 512 + s * 128
            P.add("sp", lambda e: [e.dma_start(out=xp, in_=src[r0:r0 + 128, :])], W=[t + "XP%d" % sl], dma=True,
                  chan=t + "xp%d" % sl)
            def f_stats(e):
                e.scalar_tensor_tensor(out=hn4[:, s, :], in0=xp, scalar=1.0 / D, in1=xp,
                                       op0=ALU.mult, op1=ALU.mult, accum_out=ss)
                return e.tensor_copy(out=dummy[:, 0:1], in_=dummy[:, 1:2])
            P.add("dve", f_stats, R=[t + "XP%d" % sl, "dummy"], W=[t + "ss%d" % sl, t + "hn%d" % s])
            P.add("act", lambda e: e.activation(out=rt, in_=ss, func=AF.Sqrt, bias=epst[:, 0:1], scale=1.0),
                  R=[t + "ss%d" % sl, "eps"], W=[t + "rt%d" % sl])
            P.add("dve", lambda e: e.reciprocal(out=rs, in_=rt), R=[t + "rt%d" % sl], W=[t + "rs%d" % sl])
            P.add("dve", lambda e: e.scalar_tensor_tensor(
                out=hn4[:, s, :], in0=xp, scalar=rs[:, 0:1], in1=gain, op0=ALU.mult, op1=ALU.mult),
                R=[t + "XP%d" % sl, t + "rs%d" % sl, t + "gain"], W=[t + "hn%d" % s])

        def prep_a(i):
            for s in range(4):
                prep_sub(i, s)

        load_w_kc(WG, wg, 8, t + "wg", t + "WG")
        load_w_kc(WU, wu, 8, t + "wu", t + "WU")
        load_w_kc(WD, wd, NFC, t + "wd", t + "WD")
        if tag == "F1":
            for k_ in PRECAST:
                precast(k_)
        hTkeys = [t + "hT%d" % s for s in range(4)]

        def gu(fc):
            gb, ub = 2 + (fc % 2), 4 + (fc % 2)
            def mm(Wm, bank, wkey):
                def f_mm(e):
                    r = None
                    for kc in range(8):
                        r = e.matmul(out=PS[bank][:, :], lhsT=Wm[:, kc, fc * 128:(fc + 1) * 128],
                                     rhs=hT[:, kc, :], start=(kc == 0), stop=(kc == 7))
                    return r
                P.add("pe", f_mm, R=hTkeys + [wkey], W=["ps%d" % bank])
            mm(WG, gb, t + "WG")
            mm(WU, ub, t + "WU")
            P.add("act", lambda e: e.activation(out=sg[fc % 2], in_=PS[gb][:, :], func=AF.Silu),
                  R=["ps%d" % gb], W=[t + "sg%d" % (fc % 2)])
            P.add("dve", lambda e: e.tensor_tensor(out=actT[:, fc, :], in0=sg[fc % 2], in1=PS[ub][:, :], op=ALU.mult),
                  R=[t + "sg%d" % (fc % 2), "ps%d" % ub], W=[t + "actT%d" % fc])

        def down_sub(i, s):
            sl = cnt["r"] % 2
            cnt["r"] += 1
            xr = XR[sl]
            xkey = t + "XR%d" % sl
            r0 = i * 512 + s * 128
            P.add("sp", lambda e: [e.dma_start(out=xr, in_=src[r0:r0 + 128, :])], W=[xkey], dma=True, chan=t + "xr%d" % sl)
            def half_(half):
                bank = 6 + half
                def f_dn(e):
                    r = None
                    for fc in range(NFC):
                        r = e.matmul(out=PS[bank][:, :], lhsT=actT[:, fc, s * 128:(s + 1) * 128],
                                     rhs=WD[:, fc, half * 512:(half + 1) * 512],
                                     start=(fc == 0), stop=(fc == NFC - 1))
                    return r
                P.add("pe", f_dn, R=[t + "actT%d" % fc for fc in range(NFC)] + [t + "WD"], W=["ps%d" % bank])
                P.add("dve", lambda e: e.scalar_tensor_tensor(
                    out=xr[:, half * 512:(half + 1) * 512], in0=PS[bank][:, :], scalar=0.5,
                    in1=xr[:, half * 512:(half + 1) * 512], op0=ALU.mult, op1=ALU.add),
                    R=["ps%d" % bank, xkey], W=[xkey])
            half_(0)
            half_(1)
            if g_fin is not None:
                ss, rt, rs = stf[sl]
                def f_st(e):
                    e.scalar_tensor_tensor(out=junk, in0=xr, scalar=1.0 / D, in1=xr,
                                           op0=ALU.mult, op1=ALU.mult, accum_out=ss)
                    return e.tensor_copy(out=dummy[:, 0:1], in_=dummy[:, 1:2])
                P.add("dve", f_st, R=[xkey, "dummy"], W=[t + "fss%d" % sl, t + "junk"])
                P.add("act", lambda e: e.activation(out=rt, in_=ss, func=AF.Sqrt, bias=epst[:, 0:1], scale=1.0),
                      R=[t + "fss%d" % sl, "eps"], W=[t + "frt%d" % sl])
                P.add("dve", lambda e: e.reciprocal(out=rs, in_=rt), R=[t + "frt%d" % sl], W=[t + "frs%d" % sl])
                P.add("dve", lambda e: e.scalar_tensor_tensor(
                    out=xr, in0=xr, scalar=rs[:, 0:1], in1=gfin, op0=ALU.mult, op1=ALU.mult),
                    R=[xkey, t + "frs%d" % sl, t + "gfin"], W=[xkey])
            P.add("sp", lambda e: [e.dma_start(out=dst[r0:r0 + 128, :], in_=xr)], R=[xkey], dma=True, chan=t + "xs%d" % sl)

        def tile(i):
            for fc in range(NFC):
                gu(fc)
                if fc == 9 and i + 1 < ntiles:
                    prep_a(i + 1)
            if i + 1 < ntiles:
                norm_b(t, hn4, hT)
            for s in range(4):
                down_sub(i, s)

        prep_a(0)
        norm_b(t, hn4, hT)
        for i in range(ntiles):
            tile(i)
        P.barrier()

    def proj_phase():
        AR.reset()
        t = "A"
        WIN = AR.bf(8 * 960).rearrange("p (a b) -> p a b", a=8)
        WUQ = AR.bf(2 * 768).rearrange("p (a b) -> p a b", a=2)
        WUQS = AR.bf(2 * 256).rearrange("p (a b) -> p a b", a=2)
        WUKn = AR.bf(4 * 128).rearrange("p (a b) -> p a b", a=4)
        WUKT = AR.bf(4 * 128).rearrange("p (a b) -> p a b", a=4)
        Xb = [AR.f32(4 * D).rearrange("p (a b) -> p a b", a=4) for _ in range(2)]
        hn4 = AR.bf(4 * D).rearrange("p (a b) -> p a b", a=4)
        hT = AR.bf(8 * 512).rearrange("p (a b) -> p a b", a=8)
        gain = AR.f32(D)
        gq = AR.f32(256)
        gkv = AR.f32(128)
        stat = [(AR.f32(4), AR.f32(4), AR.f32(4)) for _ in range(2)]
        RT = [AR.f32(4 * 128).rearrange("p (a b) -> p a b", a=4) for _ in range(2)]
        RC = [AR.f32(512) for _ in range(2)]
        RS = [AR.f32(512) for _ in range(2)]
        st2 = [(AR.f32(2), AR.f32(2), AR.f32(2)) for _ in range(2)]
        junk = [AR.bf(384) for _ in range(2)]
        PAS = [AR.f32(448) for _ in range(4)]
        junkb = AR.bf(D)
        cqn = [AR.bf(256) for _ in range(2)]
        krt = [AR.f32(64) for _ in range(2)]
        kra = [AR.f32(64) for _ in range(2)]
        krr = [AR.bf(64) for _ in range(2)]
        CQT = AR.bf(2 * 512).rearrange("p (a b) -> p a b", a=2)
        KTLs = [AR.bf(512) for _ in range(2)]
        KTRs = [AR.bf(512) for _ in range(2)]
        UBs = [AR.bf(4 * 512).rearrange("p (a b) -> p a b", a=4) for _ in range(2)]
        VBs = [AR.bf(4 * 128).rearrange("p (a b) -> p a b", a=4) for _ in range(2)]
        qn_sb = [AR.bf(512) for _ in range(2)]
        QLs = [AR.bf(4 * 512).rearrange("p (a b) -> p a b", a=4) for _ in range(2)]
        QRs = [AR.bf(4 * 512).rearrange("p (a b) -> p a b", a=4) for _ in range(2)]
        qt1 = [AR.f32(512) for _ in range(2)]
        qt2 = [AR.f32(512) for _ in range(2)]

        P.add("sp", lambda e: [e.dma_start(out=gain, in_=g_mix.partition_broadcast(128))], W=[t + "gain"], dma=True, chan="Again")
        P.add("sp", lambda e: [e.dma_start(out=gq, in_=g_q.partition_broadcast(128))], W=["gq"], dma=True, chan="Agq")
        P.add("sp", lambda e: [e.dma_start(out=gkv, in_=g_kv.partition_broadcast(128))], W=["gkv"], dma=True, chan="Agkv")

        def load_x(i):
            sl = i % 2
            v = x1_d[i * 512:(i + 1) * 512, :].rearrange("(s p) d -> p s d", p=128)
            P.add("sp", lambda e: [e.dma_start(out=Xb[sl], in_=v)], W=[t + "X%d" % sl], dma=True, chan="Axl%d" % sl)
            vr = rope_tok[i * 512:(i + 1) * 512, :].rearrange("(s p) d -> p s d", p=128)
            P.add("sp", lambda e: [e.dma_start(out=RT[sl], in_=vr)], W=["RT%d" % sl], dma=True, chan="Art%d" % sl)
            P.add("sp", lambda e: [e.dma_start(out=RC[sl][0:64, :], in_=rope_cos[:, i * 512:(i + 1) * 512]),
                                   e.dma_start(out=RS[sl][0:64, :], in_=rope_sin[:, i * 512:(i + 1) * 512])],
                  W=["RCS%d" % sl], dma=True, chan="Arcs%d" % sl, n=2)

        load_x(0)
        load_x(1)
        load_w_kc(WIN, wbf["w_in"], 8, "Awin", "WIN")
        load_w_kc(WUQ, w_uq, 2, "Awuq", "WUQ", eng="pool")
        uqv = w_uq.rearrange("(kc p) (h f) -> p kc h f", p=128, h=4)
        def f_wuqs(e):
            r = []
            for h in range(4):
                for kc in range(2):
                    r.append(e.dma_start(out=WUQS[:, kc, h * 64:h * 64 + 32], in_=uqv[:, kc, h, 160:192]))
                    r.append(e.dma_start(out=WUQS[:, kc, h * 64 + 32:h * 64 + 64], in_=uqv[:, kc, h, 128:160]))
            return r
        P.add("pool", f_wuqs, W=["WUQS"], dma=True, chan="Awuqs", n=16)
        ukv = w_ukv.rearrange("l (h two n) -> l h two n", h=4, two=2)
        P.add("pool", lambda e: [e.dma_start(out=WUKn, in_=ukv[:, :, 0, :])], W=["WUKn"], dma=True, chan="Awukn")
        def f_trw(e):
            r = None
            for h in range(4):
                r = e.transpose(out=psb(0)[:, h, :], in_=WUKn[:, h, :], identity=ident[:, :])
            return r
        P.add("pe", f_trw, R=["WUKn", "ident"], W=["psT0"])
        P.add("act", lambda e: e.copy(out=WUKT, in_=psb(0)[:, 0:4, :]), R=["psT0"], W=["WUKT"])

        def subtile_parts(i, s, own=True):
            c0 = 0 if own else 256
            sl = i % 2
            b2 = s % 2
            pa, pb = (2, 3) if s % 2 == 0 else (4, 5)
            ka, kb_ = "ps%d" % pa, "ps%d" % pb
            UB, VB, KTL, KTR, RTt = UBs[sl], VBs[sl], KTLs[sl], KTRs[sl], RT[sl]
            ss, rt, rs = st2[b2]
            def f_pp(e):
                r = None
                for kc in range(8):
                    e.matmul(out=PS[pa][:, c0:448], lhsT=hT[:, kc, s * 128:(s + 1) * 128], rhs=WIN[:, kc, c0:448],
                             start=(kc == 0), stop=(kc == 7))
                for kc in range(8):
                    r = e.matmul(out=PS[pb][:, :], lhsT=hT[:, kc, s * 128:(s + 1) * 128], rhs=WIN[:, kc, 448:960],
                                 start=(kc == 0), stop=(kc == 7))
                return r
            pas = PAS[s]
            pk = "PAS%d" % s
            def pp():
                P.add("pe", f_pp, R=[t + "hT%d" % s, "WIN"], W=[ka, kb_])
                P.add("act", lambda e: e.copy(out=pas[:, c0:448], in_=PS[pa][:, c0:448]), R=[ka], W=[pk])
                P.add("act", lambda e: e.copy(out=UB[:, s, :], in_=PS[pb][:, :]), R=[kb_], W=["UB%d_%d" % (sl, s)])
            def post():
                def f_sq(e):
                    if own:
                        e.activation(out=junk[b2][:, 0:256], in_=pas[:, 0:256], func=AF.Square, scale=1.0 / 16.0,
                                     accum_out=ss[:, 0:1])
                    e.activation(out=junk[b2][:, 256:384], in_=pas[:, 256:384], func=AF.Square, scale=128 ** -0.5,
                                 accum_out=ss[:, 1:2])
                    return e.activation(out=dummy[:, 2:3], in_=dummy[:, 3:4], func=AF.Copy)
                P.add("act", f_sq, R=[pk, "dummy"], W=["ss2_%d" % b2, "junk%d" % b2])
                P.add("act", lambda e: e.activation(out=rt, in_=ss, func=AF.Sqrt, bias=epst[:, 0:1], scale=1.0),
                      R=["ss2_%d" % b2, "eps"], W=["rt2_%d" % b2])
                P.add("dve", lambda e: e.reciprocal(out=rs, in_=rt), R=["rt2_%d" % b2], W=["rs2_%d" % b2])
                if own:
                    P.add("dve", lambda e: e.scalar_tensor_tensor(
                        out=cqn[b2], in0=pas[:, 0:256], scalar=rs[:, 0:1], in1=gq, op0=ALU.mult, op1=ALU.mult),
                        R=[pk, "rs2_%d" % b2, "gq"], W=["cqn%d" % b2])
                P.add("dve", lambda e: e.scalar_tensor_tensor(
                    out=VB[:, s, :], in0=pas[:, 256:384], scalar=rs[:, 1:2], in1=gkv, op0=ALU.mult, op1=ALU.mult),
                    R=[pk, "rs2_%d" % b2, "gkv"], W=["VB%d_%d" % (sl, s)])
                def f_kr(e):
                    e.tensor_tensor(out=kra[b2], in0=pas[:, 384:448], in1=RTt[:, s, 0:64], op=ALU.mult)
                    e.tensor_tensor(out=krt[b2][:, 0:32], in0=pas[:, 416:448], in1=RTt[:, s, 64:96], op=ALU.mult)
                    e.tensor_tensor(out=krt[b2][:, 32:64], in0=pas[:, 384:416], in1=RTt[:, s, 96:128], op=ALU.mult)
                    return e.tensor_copy(out=dummy[:, 0:1], in_=dummy[:, 1:2])
                P.add("dve", f_kr, R=[pk, "RT%d" % sl, "dummy"], W=["kra%d" % b2])
                P.add("dve", lambda e: e.tensor_tensor(out=krr[b2], in0=kra[b2], in1=krt[b2], op=ALU.add),
                      R=["kra%d" % b2], W=["krr%d" % b2])
            def tr():
                def f_tr(e):
                    if own:
                        e.transpose(out=psb(b2)[:, 0, :], in_=cqn[b2][:, 0:128], identity=ident[:, :])
                        e.transpose(out=psb(b2)[:, 1, :], in_=cqn[b2][:, 128:256], identity=ident[:, :])
                    e.transpose(out=psb(b2)[:, 2, :], in_=VB[:, s, :], identity=ident[:, :])
                    return e.transpose(out=psb(b2)[0:64, 3, :], in_=krr[b2], identity=ident[:, :])
                P.add("pe", f_tr, R=(["cqn%d" % b2] if own else []) + ["VB%d_%d" % (sl, s), "krr%d" % b2, "ident"], W=["psT%d" % b2])
                if own:
                    P.add("act", lambda e: e.copy(out=CQT[:, :, s * 128:(s + 1) * 128], in_=psb(b2)[:, 0:2, :]),
                          R=["psT%d" % b2], W=["CQT%d" % s])
                P.add("act", lambda e: e.copy(out=KTL[:, s * 128:(s + 1) * 128], in_=psb(b2)[:, 2, :]),
                      R=["psT%d" % b2], W=["KTL%d_%d" % (sl, s)])
                P.add("act", lambda e: e.copy(out=KTR[0:64, s * 128:(s + 1) * 128], in_=psb(b2)[0:64, 3, :]),
                      R=["psT%d" % b2], W=["KTR%d_%d" % (sl, s)])
            return pp, post, tr

        cqt_keys = ["CQT%d" % s for s in range(4)]

        def qhead(i, h):
            sl = i % 2
            hb = h % 2
            QL, QR, RCt, RSt = QLs[sl], QRs[sl], RC[sl], RS[sl]
            def f_qn(e):
                e.matmul(out=PS[4][:, :], lhsT=WUQ[:, 0, h * 192:h * 192 + 128], rhs=CQT[:, 0, :], start=True, stop=False)
                return e.matmul(out=PS[4][:, :], lhsT=WUQ[:, 1, h * 192:h * 192 + 128], rhs=CQT[:, 1, :], start=False, stop=True)
            P.add("pe", f_qn, R=cqt_keys + ["WUQ"], W=["ps4"])
            P.add("act", lambda e: e.copy(out=qn_sb[hb], in_=PS[4][:, :]), R=["ps4"], W=["qn%d" % hb])
            P.add("pe", lambda e: e.matmul(out=PS[5][:, :], lhsT=WUKT[:, h, :], rhs=qn_sb[hb], start=True, stop=True),
                  R=["qn%d" % hb, "WUKT"], W=["ps5"])
            P.add("act", lambda e: e.activation(out=QL[:, h, :], in_=PS[5][:, :], func=AF.Copy, scale=SCALE),
                  R=["ps5"], W=["QL%d_%d" % (sl, h)])
            def f_qr(e):
                e.matmul(out=PS[6][0:64, :], lhsT=WUQ[:, 0, h * 192 + 128:h * 192 + 192], rhs=CQT[:, 0, :], start=True, stop=False)
                e.matmul(out=PS[6][0:64, :], lhsT=WUQ[:, 1, h * 192 + 128:h * 192 + 192], rhs=CQT[:, 1, :], start=False, stop=True)
                e.matmul(out=PS[7][0:64, :], lhsT=WUQS[:, 0, h * 64:h * 64 + 64], rhs=CQT[:, 0, :], start=True, stop=False)
                return e.matmul(out=PS[7][0:64, :], lhsT=WUQS[:, 1, h * 64:h * 64 + 64], rhs=CQT[:, 1, :], start=False, stop=True)
            P.add("pe", f_qr, R=cqt_keys + ["WUQ", "WUQS"], W=["ps6", "ps7"])
            def f_rq(e):
                e.tensor_tensor(out=qt1[hb][0:64, :], in0=PS[6][0:64, :], in1=RCt[0:64, :], op=ALU.mult)
                e.tensor_tensor(out=qt2[hb][0:64, :], in0=PS[7][0:64, :], in1=RSt[0:64, :], op=ALU.mult)
                return e.tensor_copy(out=dummy[:, 0:1], in_=dummy[:, 1:2])
            P.add("dve", f_rq, R=["ps6", "ps7", "RCS%d" % sl, "dummy"], W=["qt%d" % hb])
            P.add("dve", lambda e: e.tensor_tensor(out=QR[0:64, h, :], in0=qt1[hb][0:64, :], in1=qt2[hb][0:64, :], op=ALU.add),
                  R=["qt%d" % hb], W=["QR%d_%d" % (sl, h)])

        def own_tok0(i):
            if USE_CC:
                return i * 512
            if i < 8:
                return i * 512
            if i >= 32:
                return 4096 + (i - 32) * 512
            return None

        def tile(i):
            sl = i % 2
            UB, VB, KTL, KTR, QL, QR = UBs[sl], VBs[sl], KTLs[sl], KTRs[sl], QLs[sl], QRs[sl]
            parts = [subtile_parts(i, s, own_tok0(i) is not None) for s in range(4)]
            if i + 1 < NTL:
                nsl = (i + 1) % 2
                nstats, nhn = norm_a_split(t, Xb[nsl], gain, hn4, stat[nsl], [t + "X%d" % nsl], junkb)
                nstats()
            else:
                nhn = lambda s: None
            parts[0][0]()
            parts[1][0]()
            parts[0][1]()
            nhn(0)
            parts[2][0]()
            parts[1][1]()
            nhn(1)
            parts[3][0]()
            parts[0][2]()
            parts[2][1]()
            nhn(2)
            parts[1][2]()
            parts[3][1]()
            nhn(3)
            parts[2][2]()
            parts[3][2]()
            tok0 = own_tok0(i)
            if tok0 is not None:
                for h in range(4):
                    qhead(i, h)
                qlv = ql_d.rearrange("(h p) n -> p h n", p=128)[:, :, tok0:tok0 + 512]
                qrv = qr_d.rearrange("(h p) n -> p h n", p=64)[:, :, tok0:tok0 + 512]
                P.add("sp", lambda e: [e.dma_start(out=qlv, in_=QL), e.dma_start(out=qrv, in_=QR[0:64, :, :])],
                      R=["QL%d_%d" % (sl, h) for h in range(4)] + ["QR%d_%d" % (sl, h) for h in range(4)],
                      dma=True, chan="Aq%d" % sl, n=2)
            if USE_CC and i < 8:
                r0 = i * 512
                vv = send[0:1024, :].rearrange("r (q l) -> (r q) l", q=4)
                ktv = send[1024:2560, :].rearrange("(f a) c -> f (a c)", a=8)
                uv = send[2560:6656, :]
            elif USE_CC:
                r0 = (i - 8) * 512
                uv, vv, ktv = u_samp, v_samp, kt_samp
            elif i < 32:
                r0 = i * 512
                uv, vv, ktv = u_p[0], v_p[0], kt_p[0]
            else:
                r0 = (i - 32) * 512
                uv, vv, ktv = u_samp, v_samp, kt_samp
            uo = uv[r0:r0 + 512, :].rearrange("(s p) c -> p s c", p=128)
            vo = vv[r0:r0 + 512, :].rearrange("(s p) c -> p s c", p=128)
            P.add("sp", lambda e: [
                e.dma_start(out=uo, in_=UB), e.dma_start(out=vo, in_=VB),
                e.dma_start(out=ktv[0:128, r0:r0 + 512], in_=KTL),
                e.dma_start(out=ktv[128:192, r0:r0 + 512], in_=KTR[0:64, :])],
                R=["UB%d_%d" % (sl, s) for s in range(4)] + ["VB%d_%d" % (sl, s) for s in range(4)]
                  + ["KTL%d_%d" % (sl, s) for s in range(4)] + ["KTR%d_%d" % (sl, s) for s in range(4)],
                W=["send_p"] if (USE_CC and i < 8) else [], dma=True, chan="Ast%d" % sl, n=4)
            if USE_CC and i == 7:
                P.add("pool", lambda e: [e.collective_compute(
                    "AllGather", ALU.bypass, replica_groups=[list(range(NCORE))],
                    ins=[send.opt()], outs=[gath.opt()])], R=["send_p"], W=["gath"], dma=True, chan="cc_all", cc=True)
            if i + 1 < NTL:
                norm_b(t, hn4, hT)
            if i + 2 < NTL:
                load_x(i + 2)

        norm_a(t, Xb[0], gain, hn4, stat[0], [t + "X0"])
        norm_b(t, hn4, hT)
        for i in range(NTL):
            tile(i)
        P.barrier()

    def fourier1():
        assert not USE_CC
        AR.reset()
        COS = AR.bf(16 * 2048).rearrange("p (a b) -> p a b", a=16)
        SIN = AR.bf(16 * 2048).rearrange("p (a b) -> p a b", a=16)
        U = [AR.bf(16 * 512).rearrange("p (a b) -> p a b", a=16) for _ in range(2)]
        CH = AR.bf(3 * 128).rearrange("p (a b) -> p a b", a=3)
        ATs = [AR.bf(512) for _ in range(2)]
        BTs = [AR.bf(512) for _ in range(2)]
        STG = [[AR.bf(4 * 512).rearrange("p (a b) -> p a b", a=4) for _ in range(2)] for _ in range(2)]
        load_w_kc(COS, wbf["dft_cos"], 16, "Fcos", "COS")
        load_w_kc(SIN, wbf["dft_sin"], 16, "Fsin", "SIN")
        P.add("pool", lambda e: [e.dma_start(out=CH, in_=ch_tabs.rearrange("a p c -> p a c"))], W=["CH"], dma=True, chan="Fch")
        NPJ = 8
        NJ = NPJ + 2
        cnt = {"q": 0, "s": 0}

        def load_u(job):
            ub = U[job % 2]
            if job < NPJ:
                v = u_p[0].rearrange("(n2 e) c -> e n2 c", e=8)[job].rearrange("(j p) c -> p j c", p=128)
            else:
                v = u_samp.rearrange("(j p two) c -> two p j c", p=128, two=2)[job - NPJ]
            P.add("sp", lambda e: [e.dma_start(out=ub, in_=v)], W=["U%d" % (job % 2)], dma=True, chan="Fu%d" % (job % 2))

        def blk(job, kb, g, stg, sk):
            ub = U[job % 2]
            q = cnt["q"] % 2
            cnt["q"] += 1
            pe_b, pf_b = 2 + 2 * q, 3 + 2 * q
            def f_ab(e):
                for j in range(16):
                    e.matmul(out=PS[0][:, :], lhsT=ub[:, j, g * 128:(g + 1) * 128],
                             rhs=COS[:, j, kb * 512:(kb + 1) * 512], start=(j == 0), stop=(j == 15))
                r = None
                for j in range(16):
                    r = e.matmul(out=PS[1][:, :], lhsT=ub[:, j, g * 128:(g + 1) * 128],
                                 rhs=SIN[:, j, kb * 512:(kb + 1) * 512], start=(j == 0), stop=(j == 15))
                return r
            P.add("pe", f_ab, R=["U%d" % (job % 2), "COS", "SIN"], W=["ps0", "ps1"])
            P.add("act", lambda e: e.copy(out=ATs[q], in_=PS[0][:, :]), R=["ps0"], W=["AT%d" % q])
            P.add("dve", lambda e: e.tensor_copy(out=BTs[q], in_=PS[1][:, :]), R=["ps1"], W=["BT%d" % q])
            def f_ef(e):
                r = None
                for sub in range(4):
                    cs = slice(sub * 128, (sub + 1) * 128)
                    e.matmul(out=PS[pe_b][:, cs], lhsT=ATs[q][:, cs], rhs=CH[:, 0, :], start=True, stop=False)
                    e.matmul(out=PS[pe_b][:, cs], lhsT=BTs[q][:, cs], rhs=CH[:, 2, :], start=False, stop=True)
                    e.matmul(out=PS[pf_b][:, cs], lhsT=BTs[q][:, cs], rhs=CH[:, 0, :], start=True, stop=False)
                    r = e.matmul(out=PS[pf_b][:, cs], lhsT=ATs[q][:, cs], rhs=CH[:, 1, :], start=False, stop=True)
                return r
            P.add("pe", f_ef, R=["AT%d" % q, "BT%d" % q, "CH"], W=["ps%d" % pe_b, "ps%d" % pf_b])
            P.add("act", lambda e: e.copy(out=stg[0][:, :, g * 128:(g + 1) * 128],
                                          in_=PS[pe_b][:, :].rearrange("p (s c) -> p s c", s=4)),
                  R=["ps%d" % pe_b], W=[sk + "e%d" % g])
            P.add("dve", lambda e: e.tensor_copy(out=stg[1][:, :, g * 128:(g + 1) * 128],
                                                 in_=PS[pf_b][:, :].rearrange("p (s c) -> p s c", s=4)),
                  R=["ps%d" % pf_b], W=[sk + "f%d" % g])

        def kblock(job, kb):
            sl = cnt["s"] % 2
            cnt["s"] += 1
            stg = STG[sl]
            sk = "STG%d" % sl
            for g in range(4):
                blk(job, kb, g, stg, sk)
            dst = ef_t if job < NPJ else efs_t
            n1 = job if job < NPJ else job - NPJ
            r0 = n1 * 4096 + kb * 512
            de = dst[r0:r0 + 512, :].rearrange("(s p) c -> p s c", p=128)
            df = dst[r0 + 2048:r0 + 2048 + 512, :].rearrange("(s p) c -> p s c", p=128)
            P.add("sp", lambda e: [e.dma_start(out=de, in_=stg[0]), e.dma_start(out=df, in_=stg[1])],
                  R=[sk + "e%d" % g for g in range(4)] + [sk + "f%d" % g for g in range(4)],
                  dma=True, chan="Fst%d" % sl, n=2)

        load_u(0)
        load_u(1)
        for job in range(NJ):
            for kb in range(4):
                kblock(job, kb)
            if job + 2 < NJ:
                load_u(job + 2)
        P.barrier()

    def fourier2():
        AR.reset()
        ET = [[AR.bf(4 * 512).rearrange("p (a b) -> p a b", a=4) for _ in range(16)] for _ in range(2)]
        TWs = [AR.f32(4 * 16).rearrange("p (a b) -> p a b", a=4) for _ in range(2)]
        DG = [AR.bf(4 * 16 * 128).rearrange("p (a b c) -> p a b c", a=4, b=16) for _ in range(2)]
        ob = [AR.bf(512) for _ in range(2)]
        cnt = {"d": 0, "o": 0, "l": 0}
        engs = ("act", "dve", "pool")

        def seg_cg(seg, cg, nj, src_t, esl):
            tok0 = seg * 2048 + cg * 512
            dsl = cnt["d"] % 2
            cnt["d"] += 1
            tws, dg = TWs[dsl], DG[dsl]
            tv = twt[seg, cg * 512:(cg + 1) * 512, :].rearrange("(ch p) j -> p ch j", p=128)
            P.add("sp", lambda e: [e.dma_start(out=tws, in_=tv)], W=["TW%d" % dsl], dma=True, chan="Gt%d" % dsl)
            def mk_diag(ch):
                idb = ident[:, :].unsqueeze(1).to_broadcast([128, nj, 128])
                twb = tws[:, ch, 0:nj].unsqueeze(2).to_broadcast([128, nj, 128])
                P.add("dve", lambda e: e.tensor_tensor(out=dg[:, ch, 0:nj, :], in0=idb, in1=twb, op=ALU.mult),
                      R=["TW%d" % dsl, "ident"], W=["DG%d_0" % dsl])
            for ch in range(4):
                mk_diag(ch)
            dkeys = ["DG%d_0" % dsl]
            def one_g(g):
                osl = cnt["o"] % 2
                cnt["o"] += 1
                bank = 2 + osl
                def f_mm(e):
                    r = None
                    for ch in range(4):
                        for j in range(nj):
                            r = e.matmul(out=PS[bank][:, ch * 128:(ch + 1) * 128],
                                         lhsT=ET[esl][j][:, ch, g * 128:(g + 1) * 128], rhs=dg[:, ch, j, :],
                                         start=(j == 0), stop=(j == nj - 1))
                    return r
                P.add("pe", f_mm, R=["ET%d" % esl] + dkeys, W=["ps%d" % bank])
                o = ob[osl]
                P.add("act", lambda e: e.copy(out=o, in_=PS[bank][:, :]), R=["ps%d" % bank], W=["ob%d" % osl])
                P.add("sp", lambda e: [e.dma_start(out=four_d[g * 128:(g + 1) * 128, tok0:tok0 + 512], in_=o)],
                      R=["ob%d" % osl], dma=True, chan="Gs%d" % osl)
            for g in range(4):
                one_g(g)

        def load_et(cg, nj, src_t):
            esl = cnt["l"] % 2
            cnt["l"] += 1
            def f(e):
                r = []
                for j in range(nj):
                    n1, ef = j // 2, j % 2
                    r0 = n1 * 4096 + ef * 2048 + cg * 512
                    r.append(e.dma_start(out=ET[esl][j], in_=src_t[r0:r0 + 512, :].rearrange("(ch p) c -> p ch c", p=128)))
                return r
            P.add("sp", f, W=["ET%d" % esl], dma=True, chan="Gl%d" % esl, n=nj)
            return esl

        work = [(cg, 16, ef_t, (0, 1)) for cg in range(4)] + [(cg, 4, efs_t, (2, 3)) for cg in range(4)]
        esl_next = load_et(*work[0][:3])
        for wi, (cg, nj, src_t, segs) in enumerate(work):
            esl = esl_next
            if wi + 1 < len(work):
                esl_next = load_et(*work[wi + 1][:3])
            for seg in segs:
                seg_cg(seg, cg, nj, src_t, esl)
        P.barrier()

    def attn_phase():
        AR.reset()
        KTL = AR.bf(16384)
        KTR = AR.bf(16384)
        V = AR.bf(128 * 128).rearrange("p (a b) -> p a b", a=128)
        QLs = [AR.bf(4 * 512).rearrange("p (a b) -> p a b", a=4) for _ in range(2)]
        QRs = [AR.bf(4 * 512).rearrange("p (a b) -> p a b", a=4) for _ in range(2)]
        pTp = [AR.bf(2 * 512).rearrange("p (a b) -> p a b", a=2) for _ in range(3)]
        pT = [pTp[i // 2][:, i % 2, :] for i in range(6)]
        sacc2w = [AR.f32(2 * 512).rearrange("p (a b) -> p a b", a=2) for _ in range(2)]
        sacc = [AR.f32(512) for _ in range(2)]
        saccb = [AR.bf(512) for _ in range(2)]
        slo = [AR.bf(512) for _ in range(2)]
        rinv = [AR.f32(512) for _ in range(2)]
        olat = [AR.bf(512) for _ in range(2)]
        mixTs = [AR.bf(8 * 512).rearrange("p (a b) -> p a b", a=8) for _ in range(2)]
        WO = AR.bf(8 * D).rearrange("p (a b) -> p a b", a=8)
        WUV = AR.bf(4 * 128).rearrange("p (a b) -> p a b", a=4)
        X1s = [AR.f32(4 * D).rearrange("p (a b) -> p a b", a=4) for _ in range(2)]
        load_w_kc(WO, wbf["w_o"], 8, "Bwo", "WO")
        ukv = w_ukv.rearrange("l (h two n) -> l h two n", h=4, two=2)
        P.add("pool", lambda e: [e.dma_start(out=WUV, in_=ukv[:, :, 1, :])], W=["WUV"], dma=True, chan="Bwuv")
        st = {"hc": 0, "tc": 0}
        P.add("dve", lambda e: e.memset(KTR[64:128, :], 0.0), W=["KVz"])
        P.add("dve", lambda e: e.memset(QRs[0][64:128, :, :], 0.0), W=["Qz0"])
        P.add("dve", lambda e: e.memset(QRs[1][64:128, :, :], 0.0), W=["Qz1"])

        def load_kv(seq):
            if seq == "s":
                P.add("sp", lambda e: [e.dma_start(out=KTL[:, 0:4096], in_=kt_samp[0:128, :]),
                                       e.dma_start(out=KTR[0:64, 0:4096], in_=kt_samp[128:192, :]),
                                       e.dma_start(out=V[:, 0:32, :], in_=v_samp.rearrange("(j p) l -> p j l", p=128))],
                      W=["KV"], dma=True, chan="Bkv", n=3)
            else:
                b = seq
                def f_kv(e):
                    r = []
                    for rr in range(8):
                        if USE_CC:
                            gk = gath[rr * SR + 1024:rr * SR + 2560, :].rearrange("(f a) c -> f (a c)", a=8)
                            gvv = gath[rr * SR:rr * SR + 1024, :].rearrange("r (q l) -> (r q) l", q=4)
                            r.append(e.dma_start(out=KTL[:, rr * 2048:(rr + 1) * 2048], in_=gk[0:128, b * 2048:(b + 1) * 2048]))
                            r.append(e.dma_start(out=KTR[0:64, rr * 2048:(rr + 1) * 2048], in_=gk[128:192, b * 2048:(b + 1) * 2048]))
                            r.append(e.dma_start(out=V[:, rr * 16:(rr + 1) * 16, :],
                                                 in_=gvv[b * 2048:(b + 1) * 2048, :].rearrange("(j p) l -> p j l", p=128)))
                            continue
                        r.append(e.dma_start(out=KTL[:, rr * 2048:(rr + 1) * 2048], in_=kt_p[b][0:128, rr * 2048:(rr + 1) * 2048]))
                        r.append(e.dma_start(out=KTR[0:64, rr * 2048:(rr + 1) * 2048], in_=kt_p[b][128:192, rr * 2048:(rr + 1) * 2048]))
                        r.append(e.dma_start(out=V[:, rr * 16:(rr + 1) * 16, :],
                                             in_=v_p[b][rr * 2048:(rr + 1) * 2048, :].rearrange("(j p) l -> p j l", p=128)))
                    return r
                P.add("sp", f_kv, W=["KV"], dma=True, chan="Bkv", n=24)

        def load_blk(k, tok0):
            bs = k % 2
            QL, QR, X1, mixT = QLs[bs], QRs[bs], X1s[bs], mixTs[bs]
            qlv = ql_d.rearrange("(h p) n -> p h n", p=128)[:, :, tok0:tok0 + 512]
            qrv = qr_d.rearrange("(h p) n -> p h n", p=64)[:, :, tok0:tok0 + 512]
            P.add("sp", lambda e: [e.dma_start(out=QL, in_=qlv), e.dma_start(out=QR[0:64, :, :], in_=qrv)],
                  W=["Q%d" % bs], dma=True, chan="Bq%d" % bs, n=2)
            lt = tok0 if (USE_CC or tok0 < 4096) else 16384 + tok0 - 4096
            x1v = x1_d[lt:lt + 512, :].rearrange("(s p) d -> p s d", p=128)
            P.add("sp", lambda e: [e.dma_start(out=X1, in_=x1v)], W=["X1_%d" % bs], dma=True, chan="Bx%d" % bs)
            fv = four_d.rearrange("(g p) n -> p g n", p=128)[:, :, tok0:tok0 + 512]
            P.add("sp", lambda e: [e.dma_start(out=mixT[:, 4:8, :], in_=fv)], W=["mixF%d" % bs], dma=True, chan="Bf%d" % bs)

        pend = []

        def flush(upto):
            while pend and pend[0][0] <= upto:
                pend.pop(0)[1]()

        def head(k, h, nk):
            bs = k % 2
            QL, QR, mixT = QLs[bs], QRs[bs], mixTs[bs]
            hs = st["hc"] % 2
            st["hc"] += 1
            po_b = 3 + hs
            SB = (0, 1, 2)
            def qk(kt):
                sb = SB[kt % 3]
                def f(e):
                    e.matmul(out=PS[sb][:, :], lhsT=KTL[:, kt * 128:(kt + 1) * 128], rhs=QL[:, h, :], start=True, stop=False)
                    return e.matmul(out=PS[sb][:, :], lhsT=KTR[:, kt * 128:(kt + 1) * 128], rhs=QR[:, h, :],
                                    start=False, stop=True)
                P.add("pe", f, R=["KV", "Q%d" % bs, "KVz", "Qz%d" % bs], W=["ps%d" % sb])
            def rest(kt):
                sb = SB[kt % 3]
                ps_ = st["tc"] % 6
                st["tc"] += 1
                P.add("act", lambda e: e.activation(out=pT[ps_], in_=PS[sb][:, :], func=AF.Exp),
                      R=["ps%d" % sb], W=["pT%d" % ps_])
                if kt % 2 == 1:
                    pr = ps_ // 2
                    w = sacc2w[hs]
                    if kt == 1:
                        P.add("dve", lambda e: e.tensor_copy(out=w, in_=pTp[pr]),
                              R=["pT%d" % (ps_ - 1), "pT%d" % ps_], W=["sacc%d" % hs])
                    else:
                        P.add("dve", lambda e: e.tensor_tensor(out=w, in0=w, in1=pTp[pr], op=ALU.add),
                              R=["pT%d" % (ps_ - 1), "pT%d" % ps_, "sacc%d" % hs], W=["sacc%d" % hs])
                P.add("pe", lambda e: e.matmul(out=PS[po_b][:, :], lhsT=V[:, kt, :], rhs=pT[ps_],
                                              start=(kt == 0), stop=(kt == nk - 1)),
                      R=["pT%d" % ps_, "KV"], W=["ps%d" % po_b])
            qk(0)
            qk(1)
            for kt in range(nk):
                if kt + 2 < nk:
                    qk(kt + 2)
                rest(kt)
                flush(kt)
            flush(10 ** 9)
            P.add("dve", lambda e: e.tensor_tensor(out=sacc[hs], in0=sacc2w[hs][:, 0, :], in1=sacc2w[hs][:, 1, :], op=ALU.add),
                  R=["sacc%d" % hs], W=["sacc%d" % hs])
            P.add("dve", lambda e: e.tensor_copy(out=saccb[hs], in_=sacc[hs]), R=["sacc%d" % hs], W=["saccb%d" % hs])
            P.add("dve", lambda e: e.tensor_tensor(out=slo[hs], in0=sacc[hs], in1=saccb[hs], op=ALU.subtract),
                  R=["saccb%d" % hs, "sacc%d" % hs], W=["slo%d" % hs])
            def step_a():
                def f_sum(e):
                    e.matmul(out=PS[5][:, :], lhsT=ones[:, :], rhs=saccb[hs], start=True, stop=False)
                    return e.matmul(out=PS[5][:, :], lhsT=ones[:, :], rhs=slo[hs], start=False, stop=True)
                P.add("pe", f_sum, R=["saccb%d" % hs, "slo%d" % hs, "ones"], W=["ps5"])
                P.add("dve", lambda e: e.reciprocal(out=rinv[hs], in_=PS[5][:, :]), R=["ps5"], W=["rinv%d" % hs])
                P.add("dve", lambda e: e.tensor_tensor(out=olat[hs], in0=PS[po_b][:, :], in1=rinv[hs], op=ALU.mult),
                      R=["ps%d" % po_b, "rinv%d" % hs], W=["olat%d" % hs])
            def step_b():
                P.add("pe", lambda e: e.matmul(out=PS[5][:, :], lhsT=WUV[:, h, :], rhs=olat[hs], start=True, stop=True),
                      R=["olat%d" % hs, "WUV"], W=["ps5"])
                P.add("act", lambda e: e.copy(out=mixT[:, h, :], in_=PS[5][:, :]), R=["ps5"], W=["mix%d_%d" % (bs, h)])
            pend.append((3, step_a))
            pend.append((9, step_b))

        def wo(k, tok0):
            bs = k % 2
            X1, mixT = X1s[bs], mixTs[bs]
            def one(s, half):
                bank = 6 + half
                def f_wo(e):
                    r = None
                    for ch in range(8):
                        r = e.matmul(out=PS[bank][:, :], lhsT=mixT[:, ch, s * 128:(s + 1) * 128],
                                     rhs=WO[:, ch, half * 512:(half + 1) * 512], start=(ch == 0), stop=(ch == 7))
                    return r
                P.add("pe", f_wo, R=["mix%d_%d" % (bs, h) for h in range(4)] + ["mixF%d" % bs, "WO"], W=["ps%d" % bank])
                P.add("dve", lambda e: e.tensor_tensor(
                    out=X1[:, s, half * 512:(half + 1) * 512], in0=PS[bank][:, :],
                    in1=X1[:, s, half * 512:(half + 1) * 512], op=ALU.add),
                    R=["ps%d" % bank, "X1_%d" % bs], W=["X1_%d" % bs])
            kt0 = 12
            for s in range(4):
                for half in range(2):
                    pend.append((kt0, (lambda s=s, half=half: one(s, half))))
                    kt0 += 2
            x2v = x2_d[tok0:tok0 + 512, :].rearrange("(s p) d -> p s d", p=128)
            pend.append((kt0, lambda: P.add("sp", lambda e: [e.dma_start(out=x2v, in_=X1)], R=["X1_%d" % bs], dma=True,
                                            chan="Bxs%d" % bs)))

        blocks = []
        for seq in (("s", 0, 1) if USE_CC else ("s", 0)):
            if seq == "s":
                for j in range(8):
                    blocks.append((seq, 4096 + 512 * j, 32))
            elif USE_CC:
                for j in range(4):
                    blocks.append((seq, seq * 2048 + 512 * j, 128))
            else:
                for j in range(8):
                    blocks.append((seq, 512 * j, 128))
        cur = None
        load_blk(0, blocks[0][1])
        for k, (seq, tok0, nk) in enumerate(blocks):
            if seq != cur:
                load_kv(seq)
                cur = seq
            head(k, 0, nk)
            if k + 1 < len(blocks):
                load_blk(k + 1, blocks[k + 1][1])
            for h in range(1, 4):
                head(k, h, nk)
            wo(k, tok0)
        flush(10 ** 9)
        P.barrier()

    import os
    stop = int(os.environ.get("MK_STOP", "99"))
    if stop >= 1:
        ffn_phase("F1", xin, x1_d, w1_gate, w1_up, w1_down, g_ffn1, None, NTL)
    if stop >= 2:
        proj_phase()
    if stop >= 3:
        fourier1()
    if stop >= 4:
        fourier2()
    if stop >= 5:
        attn_phase()
    if stop >= 6:
        ffn_phase("F2", x2_d, yout, wbf["w2_gate"], wbf["w2_up"], wbf["w2_down"], g_ffn2, g_final, NT)
    P.barrier()
    P.emit()
    es.close()
    return nc


_CACHE = {}


def _tables(c):
    f32 = np.float32
    inv = (1.0 / (np.float32(10000.0) ** (np.arange(0, 64, 2, dtype=f32) / f32(64)))).astype(f32)
    if USE_CC:
        ppos = 2048 * c + np.arange(2048)
    else:
        t = np.arange(16384)
        ppos = 4096 * (((t // 4096) + (c % 4)) % 4) + (t % 4096)
    pos = (np.concatenate([ppos, ppos, np.arange(4096)]) if USE_CC else np.concatenate([ppos, np.arange(4096)])).astype(f32)
    ang = (pos[:, None] * inv[None, :]).astype(f32)
    cs, sn = np.cos(ang).astype(f32), np.sin(ang).astype(f32)
    rope_tok = np.concatenate([cs, cs, -sn, sn], axis=1).astype(f32)
    rope_cos = (np.concatenate([cs, cs], axis=1).T * f32(SCALE)).astype(f32)
    rope_sin = (np.concatenate([-sn, sn], axis=1).T * f32(SCALE)).astype(f32)
    k2 = np.arange(2048)
    tw = np.zeros((20, 2, 2048), np.float64)
    normP = 1.0 / np.sqrt(16384.0 * 128.0)
    normS = 1.0 / np.sqrt(4096.0 * 128.0)
    for b in range(2):
        for n1 in range(8):
            k = (2048 * c + k2) if USE_CC else (4096 * (c % 4) + 2048 * b + k2)
            a = 2 * np.pi * (((n1 * k) % 16384) / 16384.0 + (0.0 if USE_CC else (((c % 4) * k) % 4) / 4.0))
            tw[b * 8 + n1, 0] = np.cos(a) * normP
            tw[b * 8 + n1, 1] = -np.sin(a) * normP
    for hh in range(2):
        for n1 in range(2):
            k = 2048 * hh + k2
            a = 2 * np.pi * ((n1 * k) % 4096) / 4096.0
            tw[16 + hh * 2 + n1, 0] = np.cos(a) * normS
            tw[16 + hh * 2 + n1, 1] = -np.sin(a) * normS
    twt = np.zeros((4, 2048, 16), np.float64)
    for seg in range(2):
        twt[seg] = tw[seg * 8:(seg + 1) * 8].transpose(2, 0, 1).reshape(2048, 16)
    for hh in range(2):
        twt[2 + hh, :, 0:4] = tw[16 + hh * 2:16 + hh * 2 + 2].transpose(2, 0, 1).reshape(2048, 4)
    return rope_tok, np.ascontiguousarray(rope_cos), np.ascontiguousarray(rope_sin), tw.astype(f32), twt.astype(f32)


def _const_tables():
    n = np.arange(2048)
    m = (n[:, None] * n[None, :]) % 2048
    a = 2 * np.pi * m / 2048.0
    dc, ds = np.cos(a).astype(np.float32), np.sin(a).astype(np.float32)
    c = np.arange(128)
    ac = 2 * np.pi * ((c[:, None] * c[None, :]) % 128) / 128.0
    ch = np.stack([np.cos(ac), np.sin(ac), -np.sin(ac)]).astype(np.float32)
    return dc, ds, ch, np.eye(128, dtype=np.float32)


def kernel(**inputs):
    if "nc" not in _CACHE:
        _CACHE["nc"] = build_program()
        _CACHE["const"] = _const_tables()
    nc = _CACHE["nc"]
    dc, ds, ch, ident = _CACHE["const"]
    f = lambda k: np.ascontiguousarray(np.asarray(inputs[k], dtype=np.float32))
    xp, xs = f("x_prompt"), f("x_sample")
    shared = {
        "g_ffn1": f("g_ffn1")[0], "g_mix": f("g_mix")[0], "g_ffn2": f("g_ffn2")[0], "g_final": f("g_final"),
        "g_q": f("g_q")[0], "g_kv": f("g_kv")[0],
        "w_uq": f("w_uq")[0], "w_ukv": f("w_ukv")[0], "ch_tabs": ch, "ident": ident,
    }
    for k in ("w1_gate", "w1_up", "w1_down", "w2_gate", "w2_up", "w2_down", "w_in", "w_o"):
        shared[k] = f(k)[0]
    shared["dft_cos"], shared["dft_sin"] = dc, ds
    in_maps = []
    for c in range(NCORE):
        rt, rc, rs, tw, twt = _tables(c)
        m = dict(shared)
        if USE_CC:
            p0, p1 = xp[0, 2048 * c:2048 * (c + 1)], xp[1, 2048 * c:2048 * (c + 1)]
        else:
            p0 = np.roll(xp[c // 4].reshape(4, 4096, D), -(c % 4), axis=0).reshape(16384, D)
            p1 = None
        m["xin"] = np.ascontiguousarray(np.concatenate([p0, xs[c]] if p1 is None else [p0, p1, xs[c]], axis=0))
        m["rope_tok"], m["rope_cos"], m["rope_sin"], m["tw"], m["twt"] = rt, rc, rs, tw, twt
        in_maps.append(m)
    res = run_bass_kernel_spmd(nc, in_maps, core_ids=list(range(NCORE)))
    yp = np.empty((2, 16384, D), np.float32)
    ys = np.empty((8, 4096, D), np.float32)
    for c in range(NCORE):
        y = np.asarray(res.results[c]["yout"])
        if USE_CC:
            yp[0, 2048 * c:2048 * (c + 1)] = y[0:2048]
            yp[1, 2048 * c:2048 * (c + 1)] = y[2048:4096]
        else:
            yp[c // 4, 4096 * (c % 4):4096 * (c % 4 + 1)] = y[0:4096]
        ys[c] = y[4096:8192]
    return (yp, ys)
```
